# Optimizing a Trainium2 kernel written in Bass

```python
import jax, jax.numpy as jnp
from jax import lax
import numpy as np

D_MODEL = 1024
BATCH = 32
SEQ = 256
DEPTH = 2
DEC_BATCH = 8
DEC_SEQ = 1024
PAST_LEN = 256

GRID_W = 64
Q_BLOCK = 128
CHUNK = 128
HEAD_DIM = 64
ATTN_W = D_MODEL // 2
N_HEADS = ATTN_W // HEAD_DIM
N_KV_HEADS = N_HEADS // 4
KV_W = N_KV_HEADS * HEAD_DIM
LRU_W = D_MODEL // 4
LRU_BLOCKS = 4
LRU_BW = LRU_W // LRU_BLOCKS
CONV_W = 4
RG_C = 8.0
MLP_W = D_MODEL // 4
MLP_GROUPS = 4
MLP_GW = MLP_W // MLP_GROUPS
MIX_W = ATTN_W + LRU_W + MLP_W
IN_W = ATTN_W + 2 * KV_W + 2 * LRU_W + 2 * MLP_W
SPLITS = (ATTN_W, ATTN_W + KV_W, ATTN_W + 2 * KV_W,
          ATTN_W + 2 * KV_W + LRU_W, ATTN_W + 2 * KV_W + 2 * LRU_W)
D_FF = 4 * D_MODEL
ROPE_THETA = 10000.0
ALPHA = (2 * DEPTH) ** 0.25
BETA = (8 * DEPTH) ** -0.25
EPS = 1e-6

kernel_name = "hybrid_diffusion_parallel_groups_step"


def _layernorm(x, g, b):
    xf = x.astype(jnp.float32)
    mu = jnp.mean(xf, -1, keepdims=True)
    var = jnp.mean(jnp.square(xf - mu), -1, keepdims=True)
    return ((xf - mu) * lax.rsqrt(var + EPS) * g + b).astype(x.dtype)


def _rmsnorm(x, g):
    xf = x.astype(jnp.float32)
    return (xf * lax.rsqrt(jnp.mean(xf * xf, -1, keepdims=True) + EPS) * g).astype(x.dtype)


def _axial_rope(x):
    L = x.shape[1]
    rows = L // GRID_W
    pos_row = jnp.repeat(jnp.arange(rows), GRID_W).astype(jnp.float32)
    pos_col = jnp.tile(jnp.arange(GRID_W), rows).astype(jnp.float32)
    n_f = HEAD_DIM // 4
    inv = ROPE_THETA ** (-jnp.arange(n_f, dtype=jnp.float32) / n_f)
    ang = jnp.concatenate([pos_row[:, None] * inv, pos_col[:, None] * inv], -1)
    cos = jnp.cos(ang)[None, :, None, :]
    sin = jnp.sin(ang)[None, :, None, :]
    x1 = x[..., 0::2].astype(jnp.float32)
    x2 = x[..., 1::2].astype(jnp.float32)
    o = jnp.stack([x1 * cos - x2 * sin, x1 * sin + x2 * cos], -1)
    return o.reshape(x.shape).astype(x.dtype)


def _block_attention(q, k, v):
    B, Lq = q.shape[:2]
    G = N_HEADS // N_KV_HEADS
    nb = Lq // Q_BLOCK
    qb = q.reshape(B, nb, Q_BLOCK, N_KV_HEADS, G, HEAD_DIM).transpose(1, 0, 2, 3, 4, 5)
    scale = HEAD_DIM ** -0.5

    def one_block(qblk):
        s = jnp.einsum('bqkgd,btkd->bkgqt', qblk, k).astype(jnp.float32) * scale
        p = jax.nn.softmax(s, axis=-1).astype(v.dtype)
        return jnp.einsum('bkgqt,btkd->bqkgd', p, v)

    o = lax.map(one_block, qb)
    return o.transpose(1, 0, 2, 3, 4, 5).reshape(B, Lq, ATTN_W)


def _dwconv(x, w, b):
    L = x.shape[1]
    left = (CONV_W - 1) // 2
    right = CONV_W - 1 - left
    xp = jnp.pad(x, ((0, 0), (left, right), (0, 0)))
    return sum(xp[:, j:j + L] * w[j] for j in range(CONV_W)) + b


def _rglru_dir(x, wa, ba, wx, bx, lam, h0):
    B, L, _ = x.shape
    xb = x.reshape(B, L, LRU_BLOCKS, LRU_BW)
    r = jax.nn.sigmoid((jnp.einsum('blnc,ncd->blnd', xb, wa).reshape(B, L, LRU_W) + ba).astype(jnp.float32))
    i = jax.nn.sigmoid((jnp.einsum('blnc,ncd->blnd', xb, wx).reshape(B, L, LRU_W) + bx).astype(jnp.float32))
    log_a = -RG_C * jax.nn.softplus(-lam.astype(jnp.float32)) * r
    a = jnp.exp(log_a)
    u = jnp.sqrt(-jnp.expm1(2.0 * log_a)) * i * x.astype(jnp.float32)

    def combine(e1, e2):
        a1, b1 = e1
        a2, b2 = e2
        return a1 * a2, a2 * b1 + b2

    A, Bc = lax.associative_scan(combine, (a, u), axis=1)
    h = A * h0.astype(jnp.float32)[:, None, :] + Bc
    return h, h[:, -1]


def _rglru_bidir(x, lp, h0):
    hf, sf = _rglru_dir(x, lp['wa'][0], lp['ba'][0], lp['wx'][0], lp['bx'][0], lp['lam'][0], h0[:, 0])
    hb, sb = _rglru_dir(jnp.flip(x, 1), lp['wa'][1], lp['ba'][1], lp['wx'][1], lp['bx'][1],
                        lp['lam'][1], h0[:, 1])
    y = (hf + jnp.flip(hb, 1)).astype(x.dtype)
    return y, jnp.stack([sf, sb], 1).astype(x.dtype)


def _chunk_gmlp(zm, lp):
    z = jax.nn.gelu(zm)
    u, v = z[..., :MLP_W], z[..., MLP_W:]
    v = _layernorm(v, lp['mlp_g'], lp['mlp_b'])
    B, L, _ = v.shape
    vb = v.reshape(B, L // CHUNK, CHUNK, MLP_GROUPS, MLP_GW)
    s = jnp.einsum('gpq,bnqgc->bnpgc', lp['ws'], vb) + lp['bs'].T[None, None, :, :, None]
    return u * s.reshape(B, L, MLP_W)


def _mixer(h, lp, ctx):
    B, L, _ = h.shape
    z = h @ lp['w_in']
    q, k, v, xr, gr, zm = jnp.split(z, SPLITS, axis=-1)
    q = _rmsnorm(q.reshape(B, L, N_HEADS, HEAD_DIM), lp['q_g'])
    k = _rmsnorm(k.reshape(B, L, N_KV_HEADS, HEAD_DIM), lp['k_g'])
    v = v.reshape(B, L, N_KV_HEADS, HEAD_DIM)
    if ctx is None:
        attn = _block_attention(q, k, v)
        h0 = jnp.zeros((B, 2, LRU_W), h.dtype)
    else:
        ck, cv, cs = ctx
        keys = jnp.concatenate([ck, _axial_rope(k)], axis=1)
        vals = jnp.concatenate([cv, v], axis=1)
        attn = _block_attention(_axial_rope(q), keys, vals)
        h0 = cs
    xc = _dwconv(xr, lp['conv_w'], lp['conv_b'])
    y_lru, s_fin = _rglru_bidir(xc, lp, h0)
    y_lru = y_lru * jax.nn.gelu(gr)
    y_mlp = _chunk_gmlp(zm, lp)
    out = jnp.concatenate([attn, y_lru, y_mlp], axis=-1) @ lp['w_out']
    return out, k, v, s_fin


def _layer(x, mod, lp, ctx):
    sh1, sc1, g1, sh2, sc2, g2 = jnp.split(mod, 6, axis=-1)
    out, k, v, s = _mixer(x * (1 + sc1) + sh1, lp, ctx)
    x = _layernorm(ALPHA * x + g1 * out, lp['ln1_g'], lp['ln1_b'])
    hff = x * (1 + sc2) + sh2
    f = jnp.square(jax.nn.relu(hff @ lp['w_ff1'] + lp['b_ff1'])) @ lp['w_ff2'] + lp['b_ff2']
    x = _layernorm(ALPHA * x + g2 * f, lp['ln2_g'], lp['ln2_b'])
    return x, k, v, s


def setup_inputs(seed: int = 0) -> dict:
    key = jax.random.key(seed)
    ks = jax.random.split(key, 40)
    nrm = lambda i, shape, s: jax.random.normal(ks[i], shape, jnp.float32) * s
    a8 = jax.random.uniform(ks[20], (DEPTH, 2, LRU_W), jnp.float32, 0.9, 0.999)
    sig = a8 ** (1.0 / RG_C)
    lam = jnp.log(sig) - jnp.log1p(-sig)
    return {
        "x_prompt": nrm(0, (BATCH, SEQ, D_MODEL), 1.0),
        "x_sample": nrm(1, (DEC_BATCH, DEC_SEQ, D_MODEL), 1.0),
        "c": nrm(2, (DEC_BATCH, D_MODEL), 1.0),
        "cache_k": nrm(3, (DEC_BATCH, DEPTH, PAST_LEN, N_KV_HEADS, HEAD_DIM), 1.0),
        "cache_v": nrm(4, (DEC_BATCH, DEPTH, PAST_LEN, N_KV_HEADS, HEAD_DIM), 1.0),
        "state_lru": nrm(5, (DEC_BATCH, DEPTH, 2, LRU_W), 0.5),
        "c_ctx": nrm(6, (D_MODEL,), 1.0),
        "w_ada": nrm(7, (DEPTH, D_MODEL, 6 * D_MODEL), 0.5 * D_MODEL ** -0.5),
        "b_ada": nrm(8, (DEPTH, 6 * D_MODEL), 0.02),
        "w_in": nrm(9, (DEPTH, D_MODEL, IN_W), D_MODEL ** -0.5),
        "q_norm_g": 1.0 + nrm(10, (DEPTH, HEAD_DIM), 0.02),
        "k_norm_g": 1.0 + nrm(11, (DEPTH, HEAD_DIM), 0.02),
        "conv_w": nrm(12, (DEPTH, CONV_W, LRU_W), CONV_W ** -0.5),
        "conv_b": nrm(13, (DEPTH, LRU_W), 0.02),
        "lru_wa": nrm(14, (DEPTH, 2, LRU_BLOCKS, LRU_BW, LRU_BW), LRU_BW ** -0.5),
        "lru_ba": nrm(15, (DEPTH, 2, LRU_W), 0.02),
        "lru_wx": nrm(16, (DEPTH, 2, LRU_BLOCKS, LRU_BW, LRU_BW), LRU_BW ** -0.5),
        "lru_bx": nrm(17, (DEPTH, 2, LRU_W), 0.02),
        "lru_lam": lam,
        "mlp_norm_g": 1.0 + nrm(18, (DEPTH, MLP_W), 0.02),
        "mlp_norm_b": nrm(19, (DEPTH, MLP_W), 0.02),
        "mlp_ws": nrm(21, (DEPTH, MLP_GROUPS, CHUNK, CHUNK), 0.5 * CHUNK ** -0.5),
        "mlp_bs": 1.0 + nrm(22, (DEPTH, MLP_GROUPS, CHUNK), 0.02),
        "w_out": nrm(23, (DEPTH, MIX_W, D_MODEL), BETA * MIX_W ** -0.5),
        "ln1_g": 1.0 + nrm(24, (DEPTH, D_MODEL), 0.02),
        "ln1_b": nrm(25, (DEPTH, D_MODEL), 0.02),
        "w_ff1": nrm(26, (DEPTH, D_MODEL, D_FF), D_MODEL ** -0.5),
        "b_ff1": nrm(27, (DEPTH, D_FF), 0.02),
        "w_ff2": nrm(28, (DEPTH, D_FF, D_MODEL), BETA * D_FF ** -0.5),
        "b_ff2": nrm(29, (DEPTH, D_MODEL), 0.02),
        "ln2_g": 1.0 + nrm(30, (DEPTH, D_MODEL), 0.02),
        "ln2_b": nrm(31, (DEPTH, D_MODEL), 0.02),
    }


def reference(x_prompt, x_sample, c, cache_k, cache_v, state_lru, c_ctx, w_ada, b_ada, w_in,
              q_norm_g, k_norm_g, conv_w, conv_b, lru_wa, lru_ba, lru_wx, lru_bx, lru_lam,
              mlp_norm_g, mlp_norm_b, mlp_ws, mlp_bs, w_out, ln1_g, ln1_b, w_ff1, b_ff1,
              w_ff2, b_ff2, ln2_g, ln2_b):
    y_prompt = x_prompt
    y_sample = x_sample
    new_k, new_v, new_s = [], [], []
    for l in range(DEPTH):
        lp = dict(w_in=w_in[l], q_g=q_norm_g[l], k_g=k_norm_g[l], conv_w=conv_w[l], conv_b=conv_b[l],
                  wa=lru_wa[l], ba=lru_ba[l], wx=lru_wx[l], bx=lru_bx[l], lam=lru_lam[l],
                  mlp_g=mlp_norm_g[l], mlp_b=mlp_norm_b[l], ws=mlp_ws[l], bs=mlp_bs[l],
                  w_out=w_out[l], ln1_g=ln1_g[l], ln1_b=ln1_b[l], w_ff1=w_ff1[l], b_ff1=b_ff1[l],
                  w_ff2=w_ff2[l], b_ff2=b_ff2[l], ln2_g=ln2_g[l], ln2_b=ln2_b[l])
        mod_ctx = (jax.nn.silu(c_ctx) @ w_ada[l] + b_ada[l])[None, None, :]
        mod_lat = (jax.nn.silu(c) @ w_ada[l] + b_ada[l])[:, None, :]
        y_prompt, k_l, v_l, s_l = _layer(y_prompt, mod_ctx, lp, None)
        new_k.append(k_l)
        new_v.append(v_l)
        new_s.append(s_l)
        y_sample, _, _, _ = _layer(y_sample, mod_lat, lp,
                                   (cache_k[:, l], cache_v[:, l], state_lru[:, l]))
    new_cache_k = jnp.stack(new_k, axis=1)
    new_cache_v = jnp.stack(new_v, axis=1)
    new_state_lru = jnp.stack(new_s, axis=1)
    return (y_prompt, y_sample, new_cache_k, new_cache_v, new_state_lru)
```

```python
import contextlib
import numpy as np
import concourse.bass as bass
import concourse.mybir as mybir
from concourse.bass_utils import run_bass_kernel_spmd
from concourse.ap import AP

F32 = mybir.dt.float32
BF16 = mybir.dt.bfloat16
AF = mybir.ActivationFunctionType
ALU = mybir.AluOpType
AX = mybir.AxisListType

D = 1024
ALPHA = 4.0 ** 0.25
EPS = 1e-6
NWB = 6


class Op:
    __slots__ = ("eng", "fn", "deps", "odeps", "ddeps", "dwaits", "sig", "sigidx", "dma", "dma_cnt",
                 "pos", "cost", "users", "nrem", "ready", "fin", "lidx", "tbl", "phase", "issue", "rdma")

    def __init__(self, eng, fn, dma, cost):
        self.eng, self.fn, self.dma, self.cost = eng, fn, dma, cost
        self.deps = []
        self.odeps = []
        self.ddeps = []
        self.dwaits = []
        self.sig = False
        self.sigidx = 0
        self.dma_cnt = 0
        self.pos = 0
        self.users = []
        self.nrem = 0
        self.ready = 0.0
        self.fin = 0.0
        self.lidx = 0
        self.tbl = None
        self.issue = None
        self.rdma = False


class Sched:
    ENGS = ("pe", "act", "dve", "pool", "sp")
    import os as _os4
    OOO = tuple(_os4.environ.get("K_OOO", "pe,dve").split(","))
    import os as _os3
    WINDOW = int(_os3.environ.get("K_WIN", "48"))
    MAXFILL = int(_os3.environ.get("K_MAXFILL", "0"))
    FILLFRAC = float(_os3.environ.get("K_FILLFRAC", "0.6"))
    FILLCAP = int(_os3.environ.get("K_FILLCAP", "40"))

    def __init__(self):
        self.ops = {e: [] for e in self.ENGS}
        self.lastw = {}
        self.readers = {}
        self.dma_counts = {}
        self.dma_ops = {}
        self.filler = None
        self.nfill = 0
        self.nops = 0
        self.phase = "pro"

    def add(self, eng, fn, r=(), w=(), dma=None, cost=300.0, tbl=None):
        op = Op(eng, fn, dma, cost)
        op.tbl = tbl
        op.phase = self.phase
        op.lidx = self.nops
        self.nops += 1
        r = list(r)
        w = list(w)
        deps = {}

        def consider(d):
            if d is None or d is op:
                return
            deps[id(d)] = d

        for k in r:
            consider(self.lastw.get(k))
        for k in w:
            consider(self.lastw.get(k))
            for d in self.readers.get(k, ()):
                consider(d)
        for d in deps.values():
            if d.dma is not None:
                op.dwaits.append((d.dma, self.dma_counts[d.dma] * 16))
                op.ddeps.append(d)
                lastd = self.dma_ops[d.dma][-1]
                if lastd is not d and lastd is not op:
                    op.ddeps.append(lastd)
            elif d.eng == eng and eng == "pe" and op.dma is None:
                op.odeps.append(d)
            else:
                d.sig = True
                op.deps.append(d)
        for k in r:
            self.readers.setdefault(k, []).append(op)
        for k in w:
            self.lastw[k] = op
            self.readers[k] = []
        if dma is not None:
            self.dma_counts[dma] = self.dma_counts.get(dma, 0) + 1
            op.dma_cnt = self.dma_counts[dma]
            self.dma_ops.setdefault(dma, []).append(op)
        self.ops[eng].append(op)
        return op

    def schedule(self):
        allops = [op for e in self.ENGS for op in self.ops[e]]
        for op in allops:
            op.users = []
        for op in allops:
            ds = op.deps + op.odeps + op.ddeps
            op.nrem = len(ds)
            op.ready = 0.0
            for d in ds:
                d.users.append(op)
        pend = {e: list(self.ops[e]) for e in self.ENGS}
        new = {e: [] for e in self.ENGS}
        free = {e: 0.0 for e in self.ENGS}
        dma_free = [0.0]
        cur_tbl = [None]
        total = len(allops)
        done = 0
        while done < total:
            best = None
            for e in self.ENGS:
                q = pend[e]
                if not q:
                    continue
                cand = None
                if e in self.OOO:
                    lim = min(len(q), self.WINDOW)
                    if e == "act" and q[0].phase == "pro":
                        lim = 1
                    ft = free[e]
                    for i in range(lim):
                        op = q[i]
                        if op.nrem:
                            continue
                        st = op.ready if op.ready > ft else ft
                        if e == "act" and op.tbl is not None and op.tbl != cur_tbl[0]:
                            st += 1300.0
                        if cand is None or st < cand[0]:
                            cand = (st, i, op)
                            if st <= ft:
                                break
                else:
                    op = q[0]
                    if op.nrem == 0:
                        cand = (max(op.ready, free[e]), 0, op)
                if cand is not None and (best is None or cand[0] < best[1][0]):
                    best = (e, cand)
            assert best is not None, "scheduler deadlock (dependency cycle)"
            e, (st, i, op) = best
            if e == "pe" and self.filler is not None and self.nfill < self.MAXFILL:
                gap = st - free["pe"]
                if gap > 400.0 and free["pe"] > 0.0 and not op.rdma:
                    nf = min(int(gap * self.FILLFRAC / 215.0), self.FILLCAP)
                    for _ in range(nf):
                        fo = Op("pe", self.filler[0], None, 215.0)
                        fo.deps = [self.filler[1]]
                        fo.phase = "fill"
                        new["pe"].append(fo)
                        self.nfill += 1
            pend[e].pop(i)
            new[e].append(op)
            if op.dma is not None:
                free[e] = st + (op.issue if op.issue else (1100.0 if e == "pool" else 400.0))
                b = max(st + 1800.0, dma_free[0])
                op.fin = b + op.cost
                dma_free[0] = op.fin
            else:
                op.fin = st + op.cost
                free[e] = op.fin
                if e == "act" and op.tbl is not None:
                    cur_tbl[0] = op.tbl
            for u in op.users:
                u.nrem -= 1
                t = op.fin + (0.0 if u.eng == e and op.dma is None else 120.0)
                if t > u.ready:
                    u.ready = t
                    u.rdma = op.dma is not None
            done += 1
        self.ops = new
        self.sim_time = max(free.values())
        import os as _os
        if _os.environ.get("KDEBUG"):
            ph = {}
            for e in self.ENGS:
                for op in new[e]:
                    d = ph.setdefault(op.phase, {})
                    a = d.setdefault(e, [1e18, 0.0, 0.0])
                    a[0] = min(a[0], op.fin - op.cost)
                    a[1] = max(a[1], op.fin)
                    a[2] += op.cost if op.dma is None else 0.0
            for p, d in ph.items():
                print(f"{p:10s}", "  ".join(f"{e}:{a[0] / 1e3:7.0f}-{a[1] / 1e3:7.0f} busy{a[2] / 1e3:6.0f}" for e, a in d.items()))

    def finalize(self):
        for e in self.ENGS:
            c = 0
            for op in self.ops[e]:
                if op.dma is None and op.sig:
                    c += 1
                    op.sigidx = c

    def check(self):
        ptr = {e: 0 for e in self.ENGS}
        cnt = {("e", e): 0 for e in self.ENGS}
        seen = {e: {} for e in self.ENGS}
        total = sum(len(v) for v in self.ops.values())
        done = 0
        while done < total:
            prog = False
            for e in self.ENGS:
                while ptr[e] < len(self.ops[e]):
                    op = self.ops[e][ptr[e]]
                    ok = True
                    for d in op.deps:
                        if cnt[("e", d.eng)] < d.sigidx:
                            ok = False
                    for (sname, v) in op.dwaits:
                        if cnt.get(("d", sname), 0) < v:
                            ok = False
                    if not ok:
                        break
                    if op.dma is not None:
                        cnt[("d", op.dma)] = cnt.get(("d", op.dma), 0) + 16
                    elif op.sig:
                        cnt[("e", e)] += 1
                        assert cnt[("e", e)] == op.sigidx
                    ptr[e] += 1
                    done += 1
                    prog = True
            if not prog:
                for e in self.ENGS:
                    if ptr[e] < len(self.ops[e]):
                        op = self.ops[e][ptr[e]]
                        print("BLOCKED", e, ptr[e], op.phase, [(d.eng, d.sigidx, cnt[("e", d.eng)]) for d in op.deps],
                              [(s_, v, cnt.get(("d", s_), 0)) for s_, v in op.dwaits])
                raise AssertionError("abstract deadlock")
        return True

    def emit(self, eng, handle, esem, dsem):
        seen = {}
        for op in self.ops[eng]:
            waits = {}
            for d in op.deps:
                k = ("e", d.eng)
                waits[k] = max(waits.get(k, 0), d.sigidx)
            for (s, v) in op.dwaits:
                k = ("d", s)
                waits[k] = max(waits.get(k, 0), v)
            for k, v in waits.items():
                if seen.get(k, 0) >= v:
                    continue
                seen[k] = v
                sem = esem[k[1]] if k[0] == "e" else dsem[k[1]]
                handle.wait_ge(sem, v)
            ins = op.fn(handle)
            if op.dma is not None:
                ins.then_inc(dsem[op.dma], 16)
            elif op.sig:
                ins.then_inc(esem[eng], 1)


def ap_of(base, off, dims):
    return AP(base.tensor, base.offset + off, [list(base.ap[0])] + [list(d) for d in dims])


def build_nc():
    nc = bass.Bass("TRN2", target_bir_lowering=False)
    S = Sched()

    def din(name, shape):
        return nc.dram_tensor(name, list(shape), F32, kind="ExternalInput").ap()

    def dout(name, shape):
        return nc.dram_tensor(name, list(shape), F32, kind="ExternalOutput").ap()

    xin = {"P": din("xp", [1024, D]), "S": din("xs", [1024, D])}
    cvec = din("cvec", [2, D])
    ck_d = din("ck", [2, 256, 128])
    cv_d = din("cv", [2, 256, 128])
    st_d = din("st", [2, 2, 256])
    w_ada = din("w_ada", [2, D, 6 * D])
    b_ada = din("b_ada", [2, 6 * D])
    w_in = din("w_in", [2, D, 1792])
    q_g = din("q_norm_g", [2, 64])
    k_g = din("k_norm_g", [2, 64])
    conv_w = din("conv_w", [2, 4, 256])
    conv_b = din("conv_b", [2, 256])
    lru_wa = din("lru_wa", [2, 2, 4, 64, 64])
    lru_ba = din("lru_ba", [2, 2, 256])
    lru_wx = din("lru_wx", [2, 2, 4, 64, 64])
    lru_bx = din("lru_bx", [2, 2, 256])
    lru_lam = din("lru_lam", [2, 2, 256])
    mlp_g = din("mlp_norm_g", [2, 256])
    mlp_b = din("mlp_norm_b", [2, 256])
    mlp_ws = din("mlp_ws", [2, 4, 128, 128])
    mlp_bs = din("mlp_bs", [2, 4, 128])
    w_out = din("w_out", [2, D, D])
    ln_gb = {(1, 0): din("ln1_g", [2, D]), (1, 1): din("ln1_b", [2, D]),
             (2, 0): din("ln2_g", [2, D]), (2, 1): din("ln2_b", [2, D])}
    w_ff1 = din("w_ff1", [2, D, 4 * D])
    b_ff1 = din("b_ff1", [2, 4 * D])
    w_ff2 = din("w_ff2", [2, 4 * D, D])
    b_ff2 = din("b_ff2", [2, D])
    ident_d = din("ident", [128, 128])
    rope_d = din("rope", [1024, 64])

    yout = {"P": dout("yp", [1024, D]), "S": dout("ys", [1024, D])}
    nk_d = dout("nk", [4, 2, 256, 128])
    nv_d = dout("nv", [4, 2, 256, 128])
    ns_d = dout("ns", [4, 2, 2, 256])
    modD = nc.dram_tensor("modD", [2, 2, 6 * D], F32).ap()

    es = contextlib.ExitStack()
    with es:
        def sb(name, shape, dt):
            return es.enter_context(nc.sbuf_tensor(name, list(shape), dt))

        X = sb("X", [128, 8, D], F32)
        HT = sb("HT", [128, 8, 1024], BF16)
        MH = sb("MH", [128, 8, 1024], BF16)
        WB = [sb(f"WB{i}", [128, 4096], BF16) for i in range(NWB)]
        G1 = sb("G1", [128, D], F32)
        G2 = sb("G2", [128, D], F32)
        LNA = sb("LNA", [128, D], F32)
        LNB = sb("LNB", [128, D], F32)
        QT = sb("QT", [128, 4, 1024], BF16)
        KT = sb("KT", [128, 1280], BF16)
        V2 = sb("V2", [128, 10, 256], BF16)
        PT2 = [sb(f"PT{i}", [128, 1024], BF16) for i in range(2)]
        XR = sb("XR", [128, 2, 1024], F32)
        GG = sb("GG", [128, 2, 1024], BF16)
        UT = sb("UT", [128, 2, 1024], BF16)
        VN = sb("VN", [128, 8, 256], BF16)
        FS = [sb(f"FS{i}", [128, 512], F32) for i in range(3)]
        TG = sb("TG", [128, 1024], F32)
        QBS = [sb(f"QB{i}", [128, 512], BF16) for i in range(2)]
        KF = [sb(f"KF{i}", [128, 128], F32) for i in range(2)]
        VF = [sb(f"VF{i}", [128, 128], F32) for i in range(2)]
        KB = sb("KB", [128, 128], BF16)
        XHB = sb("XHB", [128, 1024], BF16)
        STTA = sb("STTA", [128, 8, 32], F32)
        MSB = sb("MSB", [2, 1024], F32)
        IDB = sb("IDB", [128, 128], BF16)
        ONES = sb("ONES", [128, 128], BF16)
        ROPE = sb("ROPE", [128, 8, 64], F32)
        CVF = sb("CVF", [128, 8, 2], F32)
        SCV = sb("SCV", [128, 8, 2], BF16)
        GQ = sb("GQ", [128, 2, 64], F32)
        GK8 = sb("GK8", [128, 2, 64], F32)
        CW = sb("CW", [128, 2, 4, 2], F32)
        CB = sb("CB", [128, 2, 2], F32)
        BD = sb("BD", [128, 2, 2, 2, 2, 128], BF16)
        LBA = sb("LBA", [128, 2, 2, 2, 2], F32)
        CL = sb("CL", [128, 2, 2, 2], F32)
        MG = sb("MG", [128, 2, 256], F32)
        MBt = sb("MBt", [128, 2, 256], F32)
        WST = sb("WST", [128, 2, 4, 128], BF16)
        BSB = sb("BSB", [128, 2, 2, 128], F32)
        B1 = sb("B1", [128, 2, 32], F32)
        STI = sb("STI", [128, 2, 2, 2], F32)
        LNF = sb("LNF", [128, 2, 4, 8], F32)
        MSC = sb("MSC", [128, 2, 4, 8], F32)
        GCB = sb("GCB", [128, 2, 2, 8], F32)
        MS = MSB[:, 0:512]
        BA = MSB[:, 512:1024]
        stt_n = [0]

        def stt_next():
            i = stt_n[0] % 8
            stt_n[0] += 1
            return STTA[:, i, 0:16], [f"STT{i}"]
        NSS = sb("NSS", [128, 2, 2, 4], F32)
        EPSB = sb("EPSB", [128, 2], F32)
        FILL = sb("FILL", [128, 512], BF16)

        PS2 = [es.enter_context(nc.psum_tensor(f"PS{i}", [128, 1024], F32)) for i in range(4)]
        PSB = [PS2[b // 2][:, (b % 2) * 512:(b % 2 + 1) * 512] for b in range(8)]

        ps_free = list(range(8))

        def ps_get():
            assert ps_free, "out of PSUM banks"
            return ps_free.pop(0)

        def ps_put(b):
            ps_free.append(b)

        def pk(b):
            return [f"ps{b}"]

        wb_state = {"n": 0}

        def wb_load(parts):
            i = wb_state["n"] % NWB
            wb_state["n"] += 1
            for (off, dims, src) in parts:
                dst = ap_of(WB[i][:, 0:1], off, dims)
                nel = 128
                for d_ in dims:
                    nel *= d_[1]
                S.add("pool", (lambda e, dst=dst, src=src: e.dma_start(out=dst, in_=src)),
                      w=[f"WB{i}"], dma=f"wb{i}", cost=nel * 4 / 180.0)
            return i

        def wsrc(wt, l, r0, c0, nk, ncol):
            return wt[l, r0:r0 + nk * 128, c0:c0 + ncol].rearrange("(k p) c -> p k c", p=128)

        def dve(fn, r, w, n=512):
            return S.add("dve", fn, r, w, cost=70.0 + 1.3 * n)

        def act(fn, r, w, n=512, tbl=None):
            return S.add("act", fn, r, w, cost=240.0 + 0.7 * n, tbl=tbl)

        def pe(fn, r, w, n=512):
            return S.add("pe", fn, r, w, cost=12.0 + 0.40 * n)

        def spdma(out, in_, r, w, sem, nbytes=65536):
            return S.add("sp", (lambda e: e.dma_start(out=out, in_=in_, allow_slow_non_contiguous=True)),
                         r, w, dma=sem, cost=nbytes / 180.0)

        def pooldma(out, in_, r, w, sem, nbytes=16384, issue=None):
            op = S.add("pool", (lambda e: e.dma_start(out=out, in_=in_, allow_slow_non_contiguous=True)),
                       r, w, dma=sem, cost=nbytes / 180.0)
            op.issue = issue
            return op

        def htk(ks, ts):
            return [f"HT{k}_{t}" for k in ks for t in ts]

        def mhk(ks, ts):
            return [f"MH{k}_{t}" for k in ks for t in ts]

        def qtk(js, ts):
            return [f"QT{j}_{t}" for j in js for t in ts]

        ALL8 = list(range(8))

        def lbuf(m):
            return HT[:, 2 * m:2 * m + 2, :].rearrange("p a b -> p (a b)").bitcast(F32)

        LA_, LI_, LT_, LH_ = lbuf(0), lbuf(1), lbuf(2), lbuf(3)
        LKEY = [htk([2 * m, 2 * m + 1], ALL8) for m in range(4)]

        def xc_ap(c):
            return QT[:, 2 * c:2 * c + 2, :].rearrange("p a b -> p (a b)").bitcast(F32)

        def xc_key(c):
            return qtk([2 * c, 2 * c + 1], ALL8)

        V2flat = V2[:, :, :].rearrange("p a b -> p (a b)")

        def xcb_ap(c):
            return V2flat[:, c * 1024:(c + 1) * 1024]

        def xcb_key(c):
            return [f"V2_{tt}" for tt in range(4 * c, 4 * c + 4)]

        for t in range(8):
            spdma(X[:, t, :], xin["P"][t * 128:(t + 1) * 128, :], [], [f"X{t}"], "xl", nbytes=524288)
        pooldma(IDB[:, :], ident_d[:, :], [], ["IDB"], "idb")
        for c_ in range(2):
            spdma(CVF[:, :, c_], cvec[c_].rearrange("(k p) -> p k", p=128), [], ["CVF"], "cvf", nbytes=4096)
        act(lambda e: e.activation(out=SCV[:, :, :], in_=CVF[:, :, :], func=AF.Silu), ["CVF"], ["SCV"], n=16)
        def mod_block(l, blk):
            i = wb_load([(0, [[512, 8], [1, 512]], wsrc(w_ada, l, 0, blk * 512, 8, 512))])
            spdma(BA, AP(b_ada.tensor, b_ada[l, blk * 512:blk * 512 + 1].offset, [[0, 2], [1, 512]]),
                  [], ["BA"], "ba", nbytes=4096)
            b = ps_get()
            for k in range(8):
                pe(lambda e, k=k, i=i, b=b: e.matmul(PSB[b][0:2, :], lhsT=SCV[:, k, :],
                                                     rhs=WB[i][:, k * 512:(k + 1) * 512],
                                                     start=(k == 0), stop=(k == 7)),
                   ["SCV", f"WB{i}"], pk(b))
            dve(lambda e, b=b: e.tensor_tensor(out=MS, in0=PSB[b][0:2, :], in1=BA, op=ALU.add),
                pk(b) + ["BA"], ["MS"])
            ps_put(b)
            spdma(modD[l, :, blk * 512:(blk + 1) * 512], MS, ["MS"], [f"modD{l}_{blk // 2}"], "ms", nbytes=4096)

        bg = [(0, blk) for blk in range(4, 12)] + [(1, blk) for blk in range(12)]

        def bg_step(n=1):
            for _ in range(n):
                if bg:
                    mod_block(*bg.pop(0))


        for blk in range(4):
            mod_block(0, blk)
        dve(lambda e: e.memset(ONES[:, :], 1.0), [], ["ONES"])
        fill_ms = dve(lambda e: e.memset(FILL[:, :], 0.5), [], ["FILL"])
        fill_ms.sig = True
        S.filler = (lambda e: e.matmul(PSB[7][:, :], lhsT=ONES[:, :], rhs=FILL[:, :], start=True, stop=True), fill_ms)
        dve(lambda e: e.memset(EPSB[:, 0:1], EPS), [], ["EPSB"])
        dve(lambda e: e.memset(EPSB[:, 1:2], 64 * EPS), [], ["EPSB"])
        for l in range(2):
            spdma(GQ[:, l, :], AP(q_g.tensor, q_g[l, 0:1].offset, [[0, 128], [1, 64]]), [], ["GQ"], "sm", nbytes=32768)
            spdma(GK8[:, l, :], AP(k_g.tensor, k_g[l, 0:1].offset, [[0, 128], [1, 64]]), [], ["GK8"], "sm", nbytes=32768)
            spdma(MG[:, l, :], AP(mlp_g.tensor, mlp_g[l, 0:1].offset, [[0, 128], [1, 256]]), [], ["MG"], "sm")
            spdma(MBt[:, l, :], AP(mlp_b.tensor, mlp_b[l, 0:1].offset, [[0, 128], [1, 256]]), [], ["MBt"], "sm")
        dve(lambda e: e.tensor_scalar(out=GK8[:, :, :], in0=GK8[:, :, :], scalar1=8.0, scalar2=None, op0=ALU.mult),
            ["GK8"], ["GK8"], n=128)

        def load_bd(l, d, gi, wt):
            for n in range(4):
                c, h = n // 2, n % 2
                pooldma(BD[h * 64:(h + 1) * 64, l, d, gi, c, h * 64:(h + 1) * 64],
                        wt[l, d, n, :, :], ["BD"], [f"BDx{l}"], f"bd{l}", issue=4000.0)

        def deferred_bd1():
            for d in range(2):
                for gi, wt in enumerate((lru_wa, lru_wx)):
                    load_bd(1, d, gi, wt)

        def deferred_cl():
            act(lambda e: e.activation(out=CL[:, :, :, :], in_=CL[:, :, :, :], func=AF.Exp, scale=-1.0), ["CL"], ["CL"],
                n=8, tbl="explog")
            act(lambda e: e.activation(out=CL[:, :, :, :], in_=CL[:, :, :, :], func=AF.Ln, bias=1.0), ["CL"], ["CL"],
                n=8, tbl="explog")
            dve(lambda e: e.tensor_scalar(out=CL[:, :, :, :], in0=CL[:, :, :, :], scalar1=-8.0, scalar2=None,
                                          op0=ALU.mult), ["CL"], ["CL"], n=8)

        def deferred_params():
            dve(lambda e: e.memset(BD[:, :, :, :, :, :].rearrange("p a b c d f -> p (a b c d f)"), 0.0), [], ["BD"],
                n=2048)
            spdma(ROPE[:, :, :], rope_d.rearrange("(t p) c -> p t c", p=128), [], ["ROPE"], "sm3", nbytes=8192)
            for l in range(2):
                for j_ in range(4):
                    spdma(CW[:, l, j_, :], conv_w[l, j_].rearrange("(c p) -> p c", p=128), [], ["CW"], "sm3", nbytes=8192)
                spdma(CB[:, l, :], conv_b[l].rearrange("(c p) -> p c", p=128), [], ["CB"], "sm3", nbytes=8192)
                for d in range(2):
                    for gi, (wt, bt) in enumerate(((lru_wa, lru_ba), (lru_wx, lru_bx))):
                        spdma(LBA[:, l, d, gi, :], bt[l, d].rearrange("(c p) -> p c", p=128), [], ["LBA"], "sm3", nbytes=8192)
                        if l == 0:
                            load_bd(l, d, gi, wt)
                    spdma(CL[:, l, d, :], lru_lam[l, d].rearrange("(c p) -> p c", p=128), [], ["CL"], "cl", nbytes=8192)
                    spdma(STI[:, l, d, :], st_d[l, d].rearrange("(c p) -> p c", p=128), [], ["STI"], "sm3", nbytes=8192)
                for g in range(4):
                    gi, c2 = g % 2, g // 2
                    spdma(BSB[gi * 64:(gi + 1) * 64, l, c2, :],
                          AP(mlp_bs.tensor, mlp_bs[l, g, 0:1].offset, [[0, 64], [1, 128]]), [], ["BSB"], "sm3", nbytes=8192)
                spdma(B1[:, l, :], b_ff1[l].rearrange("(c p) -> p c", p=128), [], ["B1"], "sm3", nbytes=8192)
                for j, key in enumerate(((1, 0), (1, 1), (2, 0), (2, 1))):
                    spdma(LNF[:, l, j, :], ln_gb[key][l].rearrange("(c p) -> p c", p=128), [], ["LNF"], "sm3", nbytes=8192)
            for l in range(2):
                i = wb_load([(0, [[128, 4], [1, 128]], mlp_ws[l].rearrange("g p q -> p g q"))])
                b = ps_get()
                pst = PSB[b][:, :].bitcast(BF16)
                for g in range(4):
                    pe(lambda e, g=g, i=i, pst=pst: e.transpose(out=pst[:, g * 128:(g + 1) * 128],
                                                                in_=WB[i][:, g * 128:(g + 1) * 128], identity=IDB[:, :]),
                       [f"WB{i}", "IDB"], pk(b), n=128)
                dve(lambda e, l=l, pst=pst: e.tensor_copy(out=WST[:, l, :, :].rearrange("p g q -> p (g q)"),
                                                           in_=pst[:, 0:512]), pk(b), ["WST"])
                ps_put(b)

        def load_msc(l, cond, half):
            lp = l % 2
            key = [f"MSC{lp}{half}"]
            for j, idx in (((0, 0), (1, 1)) if half == 0 else ((2, 3), (3, 4))):
                spdma(MSC[:, lp, j, :], modD[l, cond, idx * D:(idx + 1) * D].rearrange("(c p) -> p c", p=128),
                      [f"modD{l}_{idx}"], key, f"msc{lp}{half}", nbytes=4096)
            j = 1 if half == 0 else 3
            dve(lambda e: e.tensor_scalar(out=MSC[:, lp, j, :], in0=MSC[:, lp, j, :], scalar1=1.0,
                                          scalar2=None, op0=ALU.add), key, key, n=8)

        def bcast_row(dst, src_t, off_ap, key_r, key_w, sem):
            spdma(dst[:, :], AP(src_t.tensor, off_ap.offset, [[0, 128], [1, D]]), key_r, key_w, sem)

        def nmt_a1(t):
            xk = [f"X{t}"]
            xt = X[:, t, :]
            st, sk = stt_next()
            dve(lambda e: e.bn_stats(out=st[:, 0:6], in_=xt[:, 0:512]), xk, sk, n=450)
            dve(lambda e: e.bn_stats(out=st[:, 6:12], in_=xt[:, 512:1024]), xk, sk, n=450)
            dve(lambda e: e.bn_aggr(out=st[:, 12:14], in_=st[:, 0:12]), sk, sk, n=100)
            return st, sk

        def nmt_a2(t, stsk, final_store=None):
            st, sk = stsk
            xk = [f"X{t}"]
            xt = X[:, t, :]
            act(lambda e: e.activation(out=st[:, 14:15], in_=st[:, 13:14], func=AF.Sqrt, bias=EPSB[:, 0:1]),
                sk + ["EPSB"], sk, n=60, tbl="sqrt")
            dve(lambda e: e.reciprocal(out=st[:, 14:15], in_=st[:, 14:15]), sk, sk, n=80)
            if final_store is None:
                dve(lambda e: e.tensor_scalar(out=st[:, 15:16], in0=st[:, 12:13], scalar1=st[:, 14:15], scalar2=-1.0,
                                              op0=ALU.mult, op1=ALU.mult), sk, sk, n=8)
                act(lambda e: e.activation(out=XHB[:, :], in_=xt, func=AF.Identity, scale=st[:, 14:15],
                                           bias=st[:, 15:16]), xk + sk, ["XHB"], n=1024)
            dve(lambda e: e.scalar_tensor_tensor(out=xt, in0=xt, scalar=st[:, 12:13], in1=LNA[:, :],
                                                 op0=ALU.subtract, op1=ALU.mult), xk + sk + ["LNA"], xk, n=900)
            dve(lambda e: e.scalar_tensor_tensor(out=xt, in0=xt, scalar=st[:, 14:15], in1=LNB[:, :],
                                                 op0=ALU.mult, op1=ALU.add), xk + sk + ["LNB"], xk, n=900)
            if final_store is not None:
                spdma(final_store, xt, xk, [], "yst", nbytes=524288)

        def nmt_b(t, which):
            b = ps_get()
            pst = PSB[b][:, :].bitcast(BF16)
            for k in range(8):
                pe(lambda e, k=k, pst=pst: e.transpose(out=pst[:, k * 128:(k + 1) * 128],
                                                       in_=XHB[:, k * 128:(k + 1) * 128], identity=IDB[:, :]),
                   ["XHB", "IDB"], pk(b), n=128)
            for k in range(8):
                act(lambda e, k=k, pst=pst: e.activation(out=HT[:, k, t * 128:(t + 1) * 128],
                                                         in_=pst[:, k * 128:(k + 1) * 128], func=AF.Identity,
                                                         scale=GCB[:, which, 0, k:k + 1],
                                                         bias=GCB[:, which, 1, k:k + 1]),
                    pk(b) + [f"GCB{which}"], htk([k], [t]), n=128)
            ps_put(b)

        def nmt(t, l, which, norm, la_lb, final_store=None):
            xk = [f"X{t}"]
            xt = X[:, t, :]
            if norm:
                nmt_a2(t, nmt_a1(t), final_store)
            else:
                act(lambda e: e.copy(out=XHB[:, :], in_=xt), xk, ["XHB"], n=1024)
                dve(lambda e: e.tensor_scalar(out=xt, in0=xt, scalar1=ALPHA, scalar2=None, op0=ALU.mult), xk, xk, n=600)
            if final_store is None:
                nmt_b(t, which)

        def qk_norm1(src_ps, nh, bkeys, sq, sqk):
            n = nh * 64
            st, sk = stt_next()
            act(lambda e: e.activation(out=sq[:, 0:n], in_=src_ps, func=AF.Square), bkeys, sqk, n=n)
            dve(lambda e: e.tensor_reduce(out=st[:, 0:nh], in_=sq[:, 0:n].rearrange("p (h d) -> p h d", h=nh),
                                          axis=AX.X, op=ALU.add), sqk, sk, n=n)
            return st, sk

        def qk_norm2(stsk, src_ps, nh, dst_f, gtile, bkeys, dkeys, gkey, out_ap=None, okeys=None):
            st, sk = stsk
            n = nh * 64
            act(lambda e: e.activation(out=st[:, 0:nh], in_=st[:, 0:nh], func=AF.Sqrt, bias=EPSB[:, 1:2]),
                sk + ["EPSB"], sk, n=60, tbl="sqrt")
            dve(lambda e: e.reciprocal(out=st[:, 0:nh], in_=st[:, 0:nh]), sk, sk, n=80)
            dve(lambda e: e.tensor_tensor(out=dst_f.rearrange("p (h d) -> p h d", h=nh),
                                          in0=src_ps.rearrange("p (h d) -> p h d", h=nh),
                                          in1=ap_of(st[:, 0:1], 0, [[1, nh], [0, 64]]), op=ALU.mult),
                bkeys + sk, dkeys, n=n)
            if out_ap is None:
                d3 = dst_f.rearrange("p (h d) -> p h d", h=nh)
                g3 = ap_of(gtile[:, 0:1], 0, [[0, nh], [1, 64]])
                dve(lambda e: e.tensor_tensor(out=d3, in0=d3, in1=g3, op=ALU.mult), dkeys + [gkey], dkeys, n=n)
            else:
                o_, i0_, i1_ = out_ap
                dve(lambda e: e.tensor_tensor(out=o_, in0=i0_, in1=i1_, op=ALU.mult), dkeys + [gkey], okeys, n=n)

        def rope(src_f, nh, t, dst_b, dst_dims_even, skeys, dkeys):
            T = [TG[:, i * 256:i * 256 + nh * 32].rearrange("p (h i) -> p h i", h=nh) for i in range(4)]
            x1 = ap_of(src_f[:, 0:1], 0, [[64, nh], [2, 32]])
            x2 = ap_of(src_f[:, 0:1], 1, [[64, nh], [2, 32]])
            cs = ap_of(ROPE[:, t, 0:1], 0, [[0, nh], [1, 32]])
            sn = ap_of(ROPE[:, t, 0:1], 32, [[0, nh], [1, 32]])
            dve(lambda e: e.tensor_tensor(out=T[0], in0=x1, in1=cs, op=ALU.mult), skeys + ["ROPE"], ["TG"])
            dve(lambda e: e.tensor_tensor(out=T[1], in0=x2, in1=sn, op=ALU.mult), skeys + ["ROPE"], ["TG"])
            dve(lambda e: e.tensor_tensor(out=T[2], in0=x1, in1=sn, op=ALU.mult), skeys + ["ROPE"], ["TG"])
            dve(lambda e: e.tensor_tensor(out=T[3], in0=x2, in1=cs, op=ALU.mult), skeys + ["ROPE"], ["TG"])
            if nh == 8:
                Tv = [ap_of(TG[:, 0:1], i * 256, [[128, 2], [32, 4], [1, 32]]) for i in range(4)]
            else:
                Tv = T
            de = ap_of(dst_b, 0, dst_dims_even)
            do = ap_of(dst_b, 1, dst_dims_even)
            dve(lambda e: e.tensor_tensor(out=de, in0=Tv[0], in1=Tv[1], op=ALU.subtract), ["TG"], dkeys)
            dve(lambda e: e.tensor_tensor(out=do, in0=Tv[2], in1=Tv[3], op=ALU.add), ["TG"], dkeys)

        class _Stop(Exception):
            pass

        import os as _os2
        _stop = _os2.environ.get("K_STOP")

        def stop_at(name):
            if _stop == name:
                raise _Stop()

        def group_layer(grp, l, first, last):
            is_s = grp == "S"
            cond = 1 if is_s else 0
            lp = l % 2
            nseq = 1 if is_s else 4
            L = 1024 // nseq
            ktoff = 256 if is_s else 0
            vtoff = 2 if is_s else 0

            dve(lambda e: e.memset(ap_of(V2[:, 0, 0:1], 64, [[128, 20], [1, 64]]), 1.0), [],
                [f"V2_{tt}" for tt in range(10)], n=1280)
            if first:
                load_msc(l, cond, 0)
                if is_s:
                    for t in range(8):
                        spdma(X[:, t, :], xin[grp][t * 128:(t + 1) * 128, :], [], [f"X{t}"], f"xs{t}", nbytes=524288)
                dve(lambda e: e.tensor_copy(out=GCB[:, 0, 0, :], in_=MSC[:, lp, 1, :]), [f"MSC{lp}0"], ["GCB0"], n=8)
                dve(lambda e: e.tensor_copy(out=GCB[:, 0, 1, :], in_=MSC[:, lp, 0, :]), [f"MSC{lp}0"], ["GCB0"], n=8)
            if is_s:
                for tt in range(2):
                    spdma(KF[tt][:, :], ck_d[l, tt * 128:(tt + 1) * 128, :], [], [f"KF{tt}"], f"kst{tt}")
                    spdma(VF[tt][:, :], cv_d[l, tt * 128:(tt + 1) * 128, :], [], [f"VF{tt}"], f"vst{tt}")

            def late_setup():
                mk = [f"MSC{lp}1"]
                load_msc(l, cond, 1)
                bcast_row(G1, modD, modD[l, cond, 2 * D:2 * D + 1], [f"modD{l}_2"], ["G1"], "g1")
                dve(lambda e: e.tensor_tensor(out=GCB[:, 1, 0, :], in0=LNF[:, l, 0, :], in1=MSC[:, lp, 3, :],
                                              op=ALU.mult), ["LNF"] + mk, ["GCB1"], n=8)
                dve(lambda e: e.tensor_tensor(out=GCB[:, 1, 1, :], in0=LNF[:, l, 1, :], in1=MSC[:, lp, 3, :],
                                              op=ALU.mult), ["LNF"] + mk, ["GCB1"], n=8)
                dve(lambda e: e.tensor_tensor(out=GCB[:, 1, 1, :], in0=GCB[:, 1, 1, :], in1=MSC[:, lp, 2, :],
                                              op=ALU.add), ["GCB1"] + mk, ["GCB1"], n=8)
                bcast_row(G2, modD, modD[l, cond, 5 * D:5 * D + 1], [f"modD{l}_5"], ["G2"], "g2")
                bcast_row(LNA, ln_gb[(1, 0)], ln_gb[(1, 0)][l, 0:1], [], ["LNA"], "lna")
                bcast_row(LNB, ln_gb[(1, 1)], ln_gb[(1, 1)][l, 0:1], [], ["LNB"], "lnb")
                bcast_row(TG, b_ff2, b_ff2[l, 0:1], [], ["TG", "TGv0", "TGv1"], "lnc")
                dve(lambda e: e.tensor_scalar(out=LNA[:, :], in0=LNA[:, :], scalar1=ALPHA, scalar2=None, op0=ALU.mult),
                    ["LNA"], ["LNA"], n=1024)
                dve(lambda e: e.tensor_tensor(out=TG[:, :], in0=TG[:, :], in1=G2[:, :], op=ALU.mult), ["TG", "G2"],
                    ["TG"], n=1024)
                dve(lambda e: e.scalar_tensor_tensor(out=LNB[:, :], in0=LNB[:, :], scalar=ALPHA, in1=TG[:, :],
                                                     op0=ALU.mult, op1=ALU.add), ["LNB", "TG"], ["LNB"], n=1024)

            stop_at(f"{grp}{l}:start")
            S.phase = f"{grp}{l}:win"
            if first:
                for t in range(2):
                    nmt(t, l, 0, norm=False, la_lb=False)

            stop_at(f"{grp}{l}:h1")
            iA = wb_load([(0, [[512, 8], [1, 512]], wsrc(w_in, l, 0, 0, 8, 512))])
            iB = wb_load([(0, [[512, 8], [1, 256]], wsrc(w_in, l, 0, 512, 8, 256)),
                          (256, [[512, 8], [1, 256]], wsrc(w_in, l, 0, 1536, 8, 256))])
            iC = wb_load([(0, [[512, 8], [1, 512]], wsrc(w_in, l, 0, 768, 8, 512))])
            iD = wb_load([(0, [[512, 8], [1, 256]], wsrc(w_in, l, 0, 1280, 8, 256))])

            if is_s:
                for tt in range(2):
                    act(lambda e, tt=tt: e.copy(out=KB[:, :], in_=KF[tt][:, :]), [f"KF{tt}"], ["KB"])
                    b = ps_get()
                    pst = PSB[b][:, :].bitcast(BF16)
                    pe(lambda e, pst=pst: e.transpose(out=pst[:, 0:128], in_=KB[:, :], identity=IDB[:, :]),
                       ["KB", "IDB"], pk(b), n=128)
                    dve(lambda e, tt=tt, pst=pst: e.tensor_copy(out=KT[:, tt * 128:(tt + 1) * 128], in_=pst[:, 0:128]),
                        pk(b), [f"KT{tt}"])
                    ps_put(b)
                    dve(lambda e, tt=tt: e.tensor_copy(
                        out=ap_of(V2[:, tt, 0:1], 0, [[128, 2], [1, 64]]),
                        in_=VF[tt][:, :].rearrange("p (k d) -> p k d", k=2)), [f"VF{tt}"], [f"V2_{tt}"], n=128)

            pm_dims = [[64, 2], [128, 4], [1, 64]]
            nat_dims = [[256, 2], [64, 4], [1, 64]]
            qst = {}

            def q_a1(t):
                if first and t + 2 < 8:
                    nmt(t + 2, l, 0, norm=False, la_lb=False)
                b = ps_get()
                for k in range(8):
                    pe(lambda e, k=k, b=b, t=t: e.matmul(PSB[b][:, :], lhsT=HT[:, k, t * 128:(t + 1) * 128],
                                                         rhs=WB[iA][:, k * 512:(k + 1) * 512],
                                                         start=(k == 0), stop=(k == 7)),
                       htk([k], [t]) + [f"WB{iA}"], pk(b))
                qst[t] = (b, qk_norm1(PSB[b][:, :], 8, pk(b), FS[2], ["FS2"]))

            def q_a2(t):
                b, stsk = qst[t]
                qf = FS[t % 2]
                qfk = [f"FS{t % 2}"]
                QB = QBS[t % 2]
                qbk = [f"QB{t % 2}"]
                if is_s:
                    qk_norm2(stsk, PSB[b][:, :], 8, qf[:, :], GQ[:, l, :], pk(b), qfk, "GQ")
                    ps_put(b)
                    rope(qf[:, :], 8, t, QB[:, 0:1], [[64, 2], [128, 4], [2, 32]], qfk, qbk)
                else:
                    qk_norm2(stsk, PSB[b][:, :], 8, qf[:, :], GQ[:, l, :], pk(b), qfk, "GQ",
                             out_ap=(ap_of(QB[:, 0:1], 0, pm_dims), ap_of(qf[:, 0:1], 0, nat_dims),
                                     ap_of(GQ[:, l, 0:1], 0, [[0, 2], [0, 4], [1, 64]])), okeys=qbk)
                    ps_put(b)

            def q_b(t):
                QB = QBS[t % 2]
                qbk = [f"QB{t % 2}"]
                b2 = ps_get()
                pst = PSB[b2][:, :].bitcast(BF16)
                for j in range(4):
                    pe(lambda e, j=j, pst=pst, QB=QB: e.transpose(out=pst[:, j * 128:(j + 1) * 128],
                                                                  in_=QB[:, j * 128:(j + 1) * 128], identity=IDB[:, :]),
                       qbk + ["IDB"], pk(b2), n=128)
                act(lambda e, t=t, pst=pst: e.copy(out=QT[:, :, t * 128:(t + 1) * 128],
                                                   in_=pst[:, 0:512].rearrange("p (j q) -> p j q", j=4)),
                    pk(b2), qtk(range(4), [t]))
                ps_put(b2)

            kst = {}

            def kv_a1(t):
                b = ps_get()
                for k in range(8):
                    pe(lambda e, k=k, b=b, t=t: e.matmul(PSB[b][:, :], lhsT=HT[:, k, t * 128:(t + 1) * 128],
                                                         rhs=WB[iB][:, k * 512:(k + 1) * 512],
                                                         start=(k == 0), stop=(k == 7)),
                       htk([k], [t]) + [f"WB{iB}"], pk(b))
                sq = FS[2][:, (t % 2) * 128:(t % 2 + 1) * 128]
                stsk = qk_norm1(PSB[b][:, 0:128], 2, pk(b), sq, [f"FS2k{t % 2}", "FS2"])
                tv = vtoff + t
                if not is_s:
                    vf = VF[t % 2]
                    act(lambda e, vf=vf, b=b: e.copy(out=vf[:, :], in_=PSB[b][:, 128:256]), pk(b), [f"VF{t % 2}"], n=128)
                    s_, tt = t // 2, t % 2
                    spdma(nv_d[s_, l, tt * 128:(tt + 1) * 128, :], vf[:, :], [f"VF{t % 2}"], [], f"vst{t % 2}")
                dve(lambda e, tv=tv, b=b: e.tensor_copy(
                    out=ap_of(V2[:, tv, 0:1], 0, [[128, 2], [1, 64]]),
                    in_=PSB[b][:, 128:256].rearrange("p (k d) -> p k d", k=2)), pk(b), [f"V2_{tv}"], n=128)
                vg = TG[:, (t % 2) * 256:(t % 2 + 1) * 256] if not is_s else FS[t % 2][:, 0:256]
                vgk = [f"TGv{t % 2}"] if not is_s else [f"FS{t % 2}"]
                act(lambda e, b=b, vg=vg: e.activation(out=vg, in_=PSB[b][:, 256:512], func=AF.Gelu_apprx_tanh),
                    pk(b), vgk, n=256, tbl="gelu")
                st2, sk2 = stt_next()
                dve(lambda e, st2=st2, vg=vg: e.bn_stats(out=st2[:, 0:6], in_=vg), vgk, sk2, n=256)
                dve(lambda e, st2=st2: e.bn_aggr(out=st2[:, 12:14], in_=st2[:, 0:6]), sk2, sk2, n=60)
                kst[t] = (b, stsk, vg, vgk, st2, sk2)

            def kv_a2(t):
                b, stsk, vg, vgk, st2, sk2 = kst[t]
                kf = KF[t % 2]
                kfk = [f"KF{t % 2}"]
                qk_norm2(stsk, PSB[b][:, 0:128], 2, kf[:, :], GK8[:, l, :], pk(b), kfk, "GK8")
                ps_put(b)
                act(lambda e, st2=st2: e.activation(out=st2[:, 14:15], in_=st2[:, 13:14], func=AF.Sqrt,
                                                    bias=EPSB[:, 0:1]), sk2 + ["EPSB"], sk2, n=60, tbl="sqrt")
                dve(lambda e, st2=st2: e.reciprocal(out=st2[:, 14:15], in_=st2[:, 14:15]), sk2, sk2, n=80)
                dve(lambda e, st2=st2, vg=vg: e.tensor_scalar(out=vg, in0=vg, scalar1=st2[:, 12:13],
                                                              scalar2=st2[:, 14:15], op0=ALU.subtract, op1=ALU.mult),
                    vgk + sk2, vgk, n=256)
                dve(lambda e, vg=vg: e.tensor_tensor(out=vg, in0=vg, in1=MG[:, l, :], op=ALU.mult), vgk + ["MG"], vgk,
                    n=256)
                dve(lambda e, t=t, vg=vg: e.tensor_tensor(out=VN[:, t, :], in0=vg, in1=MBt[:, l, :], op=ALU.add),
                    vgk + ["MBt"], [f"VN{t}"], n=256)
                if is_s:
                    rope(kf[:, :], 2, t, KB[:, 0:1], [[64, 2], [2, 32]], kfk, ["KB"])

            def kv_b(t):
                kf = KF[t % 2]
                kfk = [f"KF{t % 2}"]
                if not is_s:
                    s_, tt = t // 2, t % 2
                    spdma(nk_d[s_, l, tt * 128:(tt + 1) * 128, :], kf[:, :], kfk, [], f"kst{t % 2}")
                    act(lambda e, kf=kf: e.copy(out=KB[:, :], in_=kf[:, :]), kfk, ["KB"], n=128)
                b2 = ps_get()
                pst = PSB[b2][:, :].bitcast(BF16)
                pe(lambda e, pst=pst: e.transpose(out=pst[:, 0:128], in_=KB[:, :], identity=IDB[:, :]),
                   ["KB", "IDB"], pk(b2), n=128)
                kc = ktoff + t * 128
                dve(lambda e, kc=kc, pst=pst: e.tensor_copy(out=KT[:, kc:kc + 128], in_=pst[:, 0:128]),
                    pk(b2), [f"KT{kc // 128}"], n=128)
                ps_put(b2)

            def skewed(a1, a2, bb):
                for step in range(8 + 2):
                    if step < 8:
                        a1(step)
                    if 0 <= step - 2 < 8:
                        bb(step - 2)
                    if 0 <= step - 1 < 8:
                        a2(step - 1)

            skewed(q_a1, q_a2, q_b)
            stop_at(f"{grp}{l}:q")
            skewed(kv_a1, kv_a2, kv_b)
            stop_at(f"{grp}{l}:kv")

            for (slot, ncol, j) in [(iC, 512, 0), (iC, 512, 1), (iC, 512, 2), (iC, 512, 3), (iD, 256, 0), (iD, 256, 1)]:
                for tg in range(2):
                    b = ps_get()
                    for k in range(8):
                        pe(lambda e, k=k, b=b, tg=tg, slot=slot, ncol=ncol, j=j: e.matmul(
                            PSB[b][:, :], lhsT=WB[slot][:, k * 512 + j * 128:k * 512 + (j + 1) * 128],
                            rhs=HT[:, k, tg * 512:(tg + 1) * 512], start=(k == 0), stop=(k == 7)),
                           htk([k], range(tg * 4, tg * 4 + 4)) + [f"WB{slot}"], pk(b))
                    cols = slice(tg * 512, (tg + 1) * 512)
                    if slot == iC and j < 2:
                        act(lambda e, b=b, j=j, cols=cols: e.copy(out=XR[:, j, cols], in_=PSB[b][:, :]),
                            pk(b), [f"XR{j}"])
                    elif slot == iC:
                        act(lambda e, b=b, j=j, cols=cols: e.activation(out=GG[:, j - 2, cols], in_=PSB[b][:, :],
                                                                        func=AF.Gelu_apprx_tanh),
                            pk(b), [f"GG{j - 2}"], tbl="gelu")
                    else:
                        act(lambda e, b=b, j=j, cols=cols: e.activation(out=UT[:, j, cols], in_=PSB[b][:, :],
                                                                        func=AF.Gelu_apprx_tanh),
                            pk(b), [f"UT{j}"], tbl="gelu")
                    ps_put(b)


            if grp == "P" and l == 0:
                S.phase = "P0:defer"
                deferred_params()
            stop_at(f"{grp}{l}:att")
            S.phase = f"{grp}{l}:att"
            npt = 0
            nunit = 0
            npair = [0]
            if is_s:
                for b_ in range(6):
                    ps_free.remove(b_)
                for u in range(2):
                    q0, N = u * 512, 512
                    qts = list(range(q0 // 128, (q0 + N) // 128))
                    for j in range(4):
                        bos = [ps_get(), ps_get()]
                        for pi in range(5):
                            dbs = []
                            for a in range(2):
                                dbs.append(npair[0] % 3)
                                npair[0] += 1
                            for ci in range(2):
                                tt = 2 * pi + ci
                                kc = tt * 128
                                for a in range(2):
                                    rows = slice(a * 64, a * 64 + 64)
                                    hb_ = 2 * dbs[a] + ci
                                    pe(lambda e, hb_=hb_, kc=kc, rows=rows, j=j, q0=q0, N=N: e.matmul(
                                        PSB[hb_][:, 0:N], lhsT=KT[rows, kc:kc + 128], rhs=QT[rows, j, q0:q0 + N],
                                        start=True, stop=True),
                                       [f"KT{kc // 128}"] + qtk([j], qts), pk(hb_), n=N)
                            for a in range(2):
                                pt = PT2[a]
                                ptk = [f"PT{a}_{k_}" for k_ in range(4)]
                                db = dbs[a]
                                act(lambda e, db=db, pt=pt: e.activation(out=pt[:, :], in_=PS2[db][:, :], func=AF.Exp),
                                    pk(2 * db) + pk(2 * db + 1), ptk, n=850, tbl="explog")
                                for ci in range(2):
                                    tt = 2 * pi + ci
                                    bo = bos[a]
                                    pe(lambda e, bo=bo, tt=tt, a=a, pt=pt, ci=ci: e.matmul(
                                        PSB[bo][:, 0:512], lhsT=V2[:, tt, a * 128:(a + 1) * 128],
                                        rhs=pt[:, ci * 512:(ci + 1) * 512],
                                        start=(tt == 0), stop=(tt == 9)), [f"V2_{tt}"] + ptk, pk(bo), n=512)
                        for a in range(2):
                            h = j + 4 * a
                            bo = bos[a]
                            pr = slice((h % 2) * 64, (h % 2) * 64 + 64)
                            rz = FS[a]
                            rzk = [f"FS{a}"]
                            dve(lambda e, bo=bo, rz=rz, N=N: e.reciprocal(out=rz[0:64, 0:N], in_=PSB[bo][64:128, 0:N]),
                                pk(bo), rzk, n=int(5.0 * N))
                            dve(lambda e, bo=bo, rz=rz, pr=pr, N=N, h=h, q0=q0: e.tensor_tensor(
                                out=MH[pr, h // 2, q0:q0 + N], in0=PSB[bo][0:64, 0:N], in1=rz[0:64, 0:N], op=ALU.mult),
                                pk(bo) + rzk, mhk([h // 2], qts), n=N)
                            ps_put(bo)
                ps_free.extend(range(6))
            for u in range(nseq if not is_s else 0):
                if is_s:
                    q0, N = u * 512, 512
                    tts = list(range(10))
                else:
                    q0, N = u * 256, 256
                    tts = [2 * u, 2 * u + 1]
                qts = list(range(q0 // 128, (q0 + N) // 128))
                for j in range(4):
                    for a in range(2):
                        h = j + 4 * a
                        rows = slice(a * 64, a * 64 + 64)
                        bo = ps_get()
                        if is_s:
                            for pi in range(5):
                                kp = npair[0] % 2
                                npair[0] += 1
                                b0_, b1_ = 2 * kp, 2 * kp + 1
                                for hb_, tt in ((b0_, 2 * pi), (b1_, 2 * pi + 1)):
                                    kc = tt * 128
                                    pe(lambda e, hb_=hb_, kc=kc, rows=rows, j=j, q0=q0, N=N: e.matmul(
                                        PSB[hb_][:, 0:N], lhsT=KT[rows, kc:kc + 128], rhs=QT[rows, j, q0:q0 + N],
                                        start=True, stop=True),
                                       [f"KT{kc // 128}"] + qtk([j], qts), pk(hb_), n=N)
                                pt = PT2[npt % 2]
                                ptk = [f"PT{npt % 2}_{k_}" for k_ in range(4)]
                                npt += 1
                                act(lambda e, kp=kp, pt=pt: e.activation(out=pt[:, :], in_=PS2[kp][:, :], func=AF.Exp),
                                    pk(b0_) + pk(b1_), ptk, n=850, tbl="explog")
                                for hi_, tt in ((0, 2 * pi), (1, 2 * pi + 1)):
                                    pe(lambda e, bo=bo, tt=tt, a=a, pt=pt, hi_=hi_: e.matmul(
                                        PSB[bo][:, 0:512], lhsT=V2[:, tt, a * 128:(a + 1) * 128],
                                        rhs=pt[:, hi_ * 512:(hi_ + 1) * 512],
                                        start=(tt == 0), stop=(tt == 9)), [f"V2_{tt}"] + ptk, pk(bo), n=512)
                        else:
                            for ti, tt in enumerate(tts):
                                bs_ = ps_get()
                                kc = tt * 128
                                pe(lambda e, bs_=bs_, kc=kc, rows=rows, j=j, q0=q0, N=N: e.matmul(
                                    PSB[bs_][:, 0:N], lhsT=KT[rows, kc:kc + 128], rhs=QT[rows, j, q0:q0 + N],
                                    start=True, stop=True),
                                   [f"KT{kc // 128}"] + qtk([j], qts), pk(bs_), n=N)
                                pt = PT2[(npt // 4) % 2][:, (npt % 4) * 256:(npt % 4 + 1) * 256]
                                ptk = [f"PT{(npt // 4) % 2}_{npt % 4}"]
                                npt += 1
                                act(lambda e, bs_=bs_, pt=pt, N=N: e.activation(out=pt[:, 0:N], in_=PSB[bs_][:, 0:N],
                                                                                func=AF.Exp), pk(bs_), ptk, n=N,
                                    tbl="explog")
                                ps_put(bs_)
                                pe(lambda e, bo=bo, tt=tt, a=a, pt=pt, N=N, ti=ti, nt=len(tts): e.matmul(
                                    PSB[bo][:, 0:N], lhsT=V2[:, tt, a * 128:(a + 1) * 128], rhs=pt[:, 0:N],
                                    start=(ti == 0), stop=(ti == nt - 1)), [f"V2_{tt}"] + ptk, pk(bo), n=N)
                        pr = slice((h % 2) * 64, (h % 2) * 64 + 64)
                        rz = FS[h % 2]
                        rzk = [f"FS{h % 2}"]
                        if is_s or nunit % 3 == 2:
                            dve(lambda e, bo=bo, rz=rz, N=N: e.reciprocal(out=rz[0:64, 0:N], in_=PSB[bo][64:128, 0:N]),
                                pk(bo), rzk, n=int(5.0 * N))
                        else:
                            act(lambda e, bo=bo, rz=rz, N=N: e.activation(out=rz[0:64, 0:N], in_=PSB[bo][64:128, 0:N],
                                                                          func=AF.Ln), pk(bo), rzk, n=N, tbl="explog")
                            act(lambda e, rz=rz, N=N: e.activation(out=rz[0:64, 0:N], in_=rz[0:64, 0:N], func=AF.Exp,
                                                                   scale=-1.0), rzk, rzk, n=N, tbl="explog")
                        dve(lambda e, bo=bo, rz=rz, pr=pr, N=N, h=h, q0=q0: e.tensor_tensor(
                            out=MH[pr, h // 2, q0:q0 + N], in0=PSB[bo][0:64, 0:N], in1=rz[0:64, 0:N], op=ALU.mult),
                            pk(bo) + rzk, mhk([h // 2], qts), n=N)
                        ps_put(bo)
                        nunit += 1

            stop_at(f"{grp}{l}:lru")
            S.phase = f"{grp}{l}:lru"
            if grp == "P" and l == 0:
                deferred_cl()
            def lru_chunk(c):
                xr = XR[:, c, :]
                xc = xc_ap(c)
                xck = xc_key(c)
                xrk = [f"XR{c}"]
                w_ = lambda j: CW[:, l, j, c:c + 1]
                xr3 = xr.rearrange("p (s t) -> p s t", s=nseq)
                xc3 = xc.rearrange("p (s t) -> p s t", s=nseq)
                dve(lambda e: e.tensor_scalar(out=xc, in0=xr, scalar1=w_(1), scalar2=CB[:, l, c:c + 1],
                                              op0=ALU.mult, op1=ALU.add), xrk + ["CW", "CB"], xck)
                dve(lambda e: e.scalar_tensor_tensor(out=xc3[:, :, 1:L], in0=xr3[:, :, 0:L - 1], scalar=w_(0),
                                                     in1=xc3[:, :, 1:L], op0=ALU.mult, op1=ALU.add),
                    xrk + xck + ["CW"], xck)
                dve(lambda e: e.scalar_tensor_tensor(out=xc3[:, :, 0:L - 1], in0=xr3[:, :, 1:L], scalar=w_(2),
                                                     in1=xc3[:, :, 0:L - 1], op0=ALU.mult, op1=ALU.add),
                    xrk + xck + ["CW"], xck)
                dve(lambda e: e.scalar_tensor_tensor(out=xc3[:, :, 0:L - 2], in0=xr3[:, :, 2:L], scalar=w_(3),
                                                     in1=xc3[:, :, 0:L - 2], op0=ALU.mult, op1=ALU.add),
                    xrk + xck + ["CW"], xck)
                xcb = xcb_ap(c)
                xcbk = xcb_key(c)
                act(lambda e: e.copy(out=xcb, in_=xc), xck, xcbk)
                def lru_dir(d):
                    for gi, (dst, dkey) in enumerate(((LA_, LKEY[0]), (LI_, LKEY[1]))):
                        for tg in range(2):
                            b = ps_get()
                            pe(lambda e, b=b, gi=gi, tg=tg: e.matmul(
                                PSB[b][:, :], lhsT=BD[:, l, d, gi, c, :], rhs=xcb[:, tg * 512:(tg + 1) * 512],
                                start=True, stop=True), [f"BDx{l}"] + xcbk, pk(b))
                            act(lambda e, b=b, gi=gi, tg=tg, dst=dst: e.activation(
                                out=dst[:, tg * 512:(tg + 1) * 512], in_=PSB[b][:, :], func=AF.Sigmoid,
                                bias=LBA[:, l, d, gi, c:c + 1]), pk(b) + ["LBA"], dkey, tbl="sig")
                            ps_put(b)
                    act(lambda e: e.activation(out=LA_, in_=LA_, func=AF.Exp, scale=CL[:, l, d, c:c + 1]),
                        LKEY[0] + ["CL"], LKEY[0], n=1024, tbl="explog")
                    dve(lambda e: e.tensor_tensor(out=LT_, in0=LA_, in1=LA_, op=ALU.mult), LKEY[0], LKEY[2], n=1024)
                    act(lambda e: e.activation(out=LT_, in_=LT_, func=AF.Ln, scale=-1.0, bias=1.0), LKEY[2], LKEY[2],
                        n=1024, tbl="explog")
                    act(lambda e: e.activation(out=LT_, in_=LT_, func=AF.Exp, scale=0.5), LKEY[2], LKEY[2],
                        n=1024, tbl="explog")
                    dve(lambda e: e.tensor_tensor(out=LI_, in0=LI_, in1=LT_, op=ALU.mult), LKEY[1] + LKEY[2], LKEY[1],
                        n=1024)
                    dve(lambda e: e.tensor_tensor(out=LI_, in0=LI_, in1=xc, op=ALU.mult), LKEY[1] + xck, LKEY[1], n=1024)
                    hdst = xr if d == 0 else LH_
                    hkey = xrk if d == 0 else LKEY[3]
                    for s_ in range(nseq):
                        if d == 0:
                            sl = lambda tns: tns[:, s_ * L:(s_ + 1) * L]
                        else:
                            def sl(tns, s_=s_):
                                base = tns[:, (s_ + 1) * L - 1:(s_ + 1) * L]
                                return AP(base.tensor, base.offset, [list(base.ap[0]), [-1, L]])
                        init = STI[:, l, d, c:c + 1] if is_s else 0.0
                        o_, a_, u_ = sl(hdst), sl(LA_), sl(LI_)
                        dve(lambda e, o_=o_, a_=a_, u_=u_, init=init: e.tensor_tensor_scan(
                            out=o_, data0=a_, data1=u_, initial=init, op0=ALU.mult, op1=ALU.add),
                            LKEY[0] + LKEY[1] + ["STI"], hkey)
                    if not is_s:
                        fin = (L - 1) if d == 0 else 0
                        dve(lambda e, hdst=hdst, fin=fin, d=d: e.tensor_copy(
                            out=NSS[:, d, c, :], in_=ap_of(hdst[:, 0:1], fin, [[L, 4]])), hkey, ["NSS"])
                for d in range(2):
                    bg_step(2)
                    lru_dir(d)
                dve(lambda e: e.tensor_tensor(out=xr, in0=xr, in1=LH_, op=ALU.add), xrk + LKEY[3], xrk)
                dve(lambda e: e.tensor_tensor(out=MH[:, 4 + c, :], in0=xr, in1=GG[:, c, :], op=ALU.mult),
                    xrk + [f"GG{c}"], mhk([4 + c], ALL8))
            for c in range(2):
                lru_chunk(c)
            while bg and bg[0][0] <= l:
                bg_step()
            late_setup()
            iE = wb_load([(0, [[512, 8], [1, 512]], wsrc(w_out, l, 0, 0, 8, 512))])
            iF = wb_load([(0, [[512, 8], [1, 512]], wsrc(w_out, l, 0, 512, 8, 512))])
            if not is_s:
                for s_ in range(4):
                    for d in range(2):
                        spdma(ns_d[s_, l, d, :].rearrange("(c p) -> p c", p=128), NSS[:, d, :, s_], ["NSS"], [], "nst")

            stop_at(f"{grp}{l}:mlp")
            S.phase = f"{grp}{l}:mlp"
            def gmlp_mix(c2, gi):
                if True:
                    g = 2 * c2 + gi
                    b0, b1 = ps_get(), ps_get()
                    for t in range(8):
                        bb = b0 if t < 4 else b1
                        pe(lambda e, bb=bb, t=t, g=g: e.matmul(
                            PSB[bb][:, (t % 4) * 128:(t % 4 + 1) * 128], lhsT=VN[:, t, c2 * 128:(c2 + 1) * 128],
                            rhs=WST[:, l, g, :], start=True, stop=True), [f"VN{t}", "WST"], pk(bb))
                    pr = slice(gi * 64, gi * 64 + 64)
                    for hh, bb in enumerate((b0, b1)):
                        cols = slice(hh * 512, (hh + 1) * 512)
                        dve(lambda e, bb=bb, pr=pr, cols=cols: e.tensor_tensor(
                            out=TG[pr, cols].rearrange("p (t q) -> p t q", t=4),
                            in0=PSB[bb][pr, :].rearrange("p (t q) -> p t q", t=4),
                            in1=ap_of(BSB[pr, l, c2, 0:1], 0, [[0, 4], [1, 128]]), op=ALU.add),
                            pk(bb) + ["BSB"], ["TG"])
                        dve(lambda e, pr=pr, cols=cols: e.tensor_tensor(
                            out=MH[pr, 6 + c2, cols], in0=TG[pr, cols], in1=UT[pr, c2, cols], op=ALU.mult),
                            ["TG", f"UT{c2}"], mhk([6 + c2], range(hh * 4, hh * 4 + 4)))
                    ps_put(b0)
                    ps_put(b1)

            for c2 in range(2):
                for gi in range(2):
                    bg_step(1)
                    gmlp_mix(c2, gi)

            stop_at(f"{grp}{l}:wout")
            S.phase = f"{grp}{l}:wout"
            wst_ = {}

            def wo_a1(t):
                for hf, slot in enumerate((iE, iF)):
                    b = ps_get()
                    for k in range(8):
                        pe(lambda e, k=k, b=b, t=t, slot=slot: e.matmul(
                            PSB[b][:, :], lhsT=MH[:, k, t * 128:(t + 1) * 128], rhs=WB[slot][:, k * 512:(k + 1) * 512],
                            start=(k == 0), stop=(k == 7)), mhk([k], [t]) + [f"WB{slot}"], pk(b))
                    cols = slice(hf * 512, (hf + 1) * 512)
                    tmp = FS[(2 * t + hf) % 3]
                    tk = [f"FS{(2 * t + hf) % 3}"]
                    dve(lambda e, b=b, cols=cols, tmp=tmp: e.tensor_tensor(out=tmp[:, :], in0=PSB[b][:, :],
                                                                           in1=G1[:, cols], op=ALU.mult),
                        pk(b) + ["G1"], tk)
                    ps_put(b)
                    dve(lambda e, t=t, cols=cols, tmp=tmp: e.tensor_tensor(out=X[:, t, cols], in0=X[:, t, cols],
                                                                           in1=tmp[:, :], op=ALU.add),
                        tk + [f"X{t}"], [f"X{t}"])
                wst_[t] = nmt_a1(t)

            skewed(wo_a1, lambda t: nmt_a2(t, wst_[t]), lambda t: nmt_b(t, 1))

            if not last:
                load_msc(l + 1, cond, 0)
                nlp = (l + 1) % 2
                dve(lambda e: e.tensor_tensor(out=GCB[:, 0, 0, :], in0=LNF[:, l, 2, :], in1=MSC[:, nlp, 1, :],
                                              op=ALU.mult), ["LNF", f"MSC{nlp}0"], ["GCB0"], n=8)
                dve(lambda e: e.tensor_tensor(out=GCB[:, 0, 1, :], in0=LNF[:, l, 3, :], in1=MSC[:, nlp, 1, :],
                                              op=ALU.mult), ["LNF", f"MSC{nlp}0"], ["GCB0"], n=8)
                dve(lambda e: e.tensor_tensor(out=GCB[:, 0, 1, :], in0=GCB[:, 0, 1, :], in1=MSC[:, nlp, 0, :],
                                              op=ALU.add), ["GCB0", f"MSC{nlp}0"], ["GCB0"], n=8)
            bcast_row(LNA, ln_gb[(2, 0)], ln_gb[(2, 0)][l, 0:1], [], ["LNA"], "lna")
            bcast_row(LNB, ln_gb[(2, 1)], ln_gb[(2, 1)][l, 0:1], [], ["LNB"], "lnb")
            if not last:
                dve(lambda e: e.tensor_scalar(out=LNA[:, :], in0=LNA[:, :], scalar1=ALPHA, scalar2=None,
                                              op0=ALU.mult), ["LNA"], ["LNA"])
                dve(lambda e: e.tensor_scalar(out=LNB[:, :], in0=LNB[:, :], scalar1=ALPHA, scalar2=None,
                                              op0=ALU.mult), ["LNB"], ["LNB"])

            stop_at(f"{grp}{l}:ffn")
            S.phase = f"{grp}{l}:ffn"
            if grp == "P" and l == 0:
                deferred_bd1()
            nrl = 0
            for qd in range(4):
                i1 = [wb_load([(0, [[512, 8], [1, 512]], wsrc(w_ff1, l, 0, qd * 1024 + hb * 512, 8, 512))])
                      for hb in range(2)]
                i2 = [wb_load([(0, [[1024, 4], [1, 1024]], wsrc(w_ff2, l, qd * 1024 + hb * 512, 0, 4, 1024))])
                      for hb in range(2)]
                for tg in range(2):
                    for hc in range(8):
                        slot = i1[hc // 4]
                        cc = (hc % 4) * 128
                        b = ps_get()
                        for k in range(8):
                            pe(lambda e, k=k, b=b, tg=tg, slot=slot, cc=cc: e.matmul(
                                PSB[b][:, :], lhsT=WB[slot][:, k * 512 + cc:k * 512 + cc + 128],
                                rhs=HT[:, k, tg * 512:(tg + 1) * 512], start=(k == 0), stop=(k == 7)),
                               htk([k], range(tg * 4, tg * 4 + 4)) + [f"WB{slot}"], pk(b))
                        rl = FS[nrl % 3]
                        rk = [f"FS{nrl % 3}"]
                        nrl += 1
                        chunk = qd * 8 + hc
                        act(lambda e, b=b, rl=rl, chunk=chunk: e.activation(out=rl[:, :], in_=PSB[b][:, :], func=AF.Relu,
                                                                            bias=B1[:, l, chunk:chunk + 1]),
                            pk(b) + ["B1"], rk)
                        ps_put(b)
                        act(lambda e, rl=rl, hc=hc, tg=tg: e.activation(out=MH[:, hc, tg * 512:(tg + 1) * 512],
                                                                        in_=rl[:, :], func=AF.Square),
                            rk, mhk([hc], range(tg * 4, tg * 4 + 4)))
                for tg in range(2):
                    for t in range(tg * 4, tg * 4 + 4):
                        for hf in range(2):
                            b = ps_get()
                            for hc in range(8):
                                slot = i2[hc // 4]
                                pe(lambda e, hc=hc, b=b, t=t, slot=slot, hf=hf: e.matmul(
                                    PSB[b][:, :], lhsT=MH[:, hc, t * 128:(t + 1) * 128],
                                    rhs=WB[slot][:, (hc % 4) * 1024 + hf * 512:(hc % 4) * 1024 + (hf + 1) * 512],
                                    start=(hc == 0), stop=(hc == 7)), mhk([hc], [t]) + [f"WB{slot}"], pk(b))
                            cols = slice(hf * 512, (hf + 1) * 512)
                            tmp = FS[nrl % 3]
                            tk = [f"FS{nrl % 3}"]
                            nrl += 1
                            dve(lambda e, b=b, cols=cols, tmp=tmp: e.tensor_tensor(out=tmp[:, :], in0=PSB[b][:, :],
                                                                                   in1=G2[:, cols], op=ALU.mult),
                                pk(b) + ["G2"], tk)
                            ps_put(b)
                            dve(lambda e, t=t, cols=cols, tmp=tmp: e.tensor_tensor(out=X[:, t, cols], in0=X[:, t, cols],
                                                                                   in1=tmp[:, :], op=ALU.add),
                                tk + [f"X{t}"], [f"X{t}"])
                        if qd == 3:
                            S.phase = f"{grp}{l}:ln2"
                            if last:
                                nmt(t, l, 0, norm=True, la_lb=True, final_store=yout[grp][t * 128:(t + 1) * 128, :])
                            else:
                                nmt(t, l, 0, norm=True, la_lb=True)
                            S.phase = f"{grp}{l}:ffn"

        try:
            for grp in ("P", "S"):
                for l in range(2):
                    group_layer(grp, l, first=(l == 0), last=(l == 1))
        except _Stop:
            pass

        final_sems = ["yst", "nst", "kst0", "kst1", "vst0", "vst1"]
        import os as _os
        S.schedule()
        if _os.environ.get("KDEBUG"):
            print("ops", {e: len(v) for e, v in S.ops.items()}, "sim_us", S.sim_time / 1e3,
                  "sbuf_left", nc.sbuf_bytes_remaining)
        S.finalize()
        S.check()
        if _os.environ.get("KDUMP"):
            for i, op in enumerate(S.ops[_os.environ["KDUMP"]]):
                print(i, "lidx", op.lidx, "tbl", op.tbl, "cost", int(op.cost), "sig", op.sigidx if op.sig else None,
                      "deps", [(d.eng, d.sigidx, d.lidx) for d in op.deps], "dw", op.dwaits)

        dma_names = sorted(S.dma_counts.keys())
        sems = {}
        for n in list(dma_names) + ["e_pe", "e_act", "e_dve", "e_pool", "e_sp"]:
            sems[n] = es.enter_context(nc.semaphore(n))
        esem = {e: sems["e_" + e] for e in Sched.ENGS}
        dsem = {n: sems[n] for n in dma_names}

        with nc.Block() as block:
            @block.tensor
            def _(e):
                S.emit("pe", e, esem, dsem)

            @block.scalar
            def _(e):
                S.emit("act", e, esem, dsem)

            @block.vector
            def _(e):
                S.emit("dve", e, esem, dsem)

            @block.gpsimd
            def _(e):
                S.emit("pool", e, esem, dsem)

            @block.sync
            def _(e):
                S.emit("sp", e, esem, dsem)
                for n in final_sems:
                    if n in dsem:
                        e.wait_ge(dsem[n], S.dma_counts[n] * 16)
    return nc


_NC_CACHE = {}


def _rope_table():
    pos = np.arange(1024)
    pr = (pos // 64).astype(np.float32)
    pc = (pos % 64).astype(np.float32)
    inv = (10000.0 ** (-np.arange(16, dtype=np.float32) / 16)).astype(np.float32)
    ang = np.concatenate([pr[:, None] * inv, pc[:, None] * inv], -1).astype(np.float32)
    return np.concatenate([np.cos(ang), np.sin(ang)], -1).astype(np.float32)


def kernel(x_prompt, x_sample, c, cache_k, cache_v, state_lru, c_ctx, w_ada, b_ada, w_in,
           q_norm_g, k_norm_g, conv_w, conv_b, lru_wa, lru_ba, lru_wx, lru_bx, lru_lam,
           mlp_norm_g, mlp_norm_b, mlp_ws, mlp_bs, w_out, ln1_g, ln1_b, w_ff1, b_ff1,
           w_ff2, b_ff2, ln2_g, ln2_b):
    f = lambda a: np.ascontiguousarray(np.asarray(a, dtype=np.float32))
    if "nc" not in _NC_CACHE:
        _NC_CACHE["nc"] = build_nc()
    nc = _NC_CACHE["nc"]
    shared = dict(w_ada=f(w_ada), b_ada=f(b_ada), w_in=f(w_in), q_norm_g=f(q_norm_g), k_norm_g=f(k_norm_g),
                  conv_w=f(conv_w), conv_b=f(conv_b), lru_wa=f(lru_wa), lru_ba=f(lru_ba), lru_wx=f(lru_wx),
                  lru_bx=f(lru_bx), lru_lam=f(lru_lam), mlp_norm_g=f(mlp_norm_g), mlp_norm_b=f(mlp_norm_b),
                  mlp_ws=f(mlp_ws), mlp_bs=f(mlp_bs), w_out=f(w_out), ln1_g=f(ln1_g), ln1_b=f(ln1_b),
                  w_ff1=f(w_ff1), b_ff1=f(b_ff1), w_ff2=f(w_ff2), b_ff2=f(b_ff2), ln2_g=f(ln2_g), ln2_b=f(ln2_b),
                  ident=np.eye(128, dtype=np.float32), rope=_rope_table())
    x_prompt, x_sample, c, c_ctx = f(x_prompt), f(x_sample), f(c), f(c_ctx)
    cache_k, cache_v, state_lru = f(cache_k), f(cache_v), f(state_lru)
    in_maps = []
    for i in range(8):
        m = dict(shared)
        m["xp"] = np.ascontiguousarray(x_prompt[4 * i:4 * i + 4].reshape(1024, D))
        m["xs"] = np.ascontiguousarray(x_sample[i])
        m["cvec"] = np.ascontiguousarray(np.stack([c_ctx, c[i]], 0))
        m["ck"] = np.ascontiguousarray(cache_k[i].reshape(2, 256, 128))
        m["cv"] = np.ascontiguousarray(cache_v[i].reshape(2, 256, 128))
        m["st"] = np.ascontiguousarray(state_lru[i])
        in_maps.append(m)
    res = run_bass_kernel_spmd(nc, in_maps, core_ids=list(range(8)))
    R = res.results
    y_prompt = np.concatenate([r["yp"].reshape(4, 256, D) for r in R], 0)
    y_sample = np.stack([r["ys"] for r in R], 0)
    nk = np.concatenate([r["nk"].reshape(4, 2, 256, 2, 64) for r in R], 0)
    nv = np.concatenate([r["nv"].reshape(4, 2, 256, 2, 64) for r in R], 0)
    ns = np.concatenate([r["ns"] for r in R], 0)
    return (y_prompt.astype(np.float32), y_sample.astype(np.float32), nk.astype(np.float32),
            nv.astype(np.float32), ns.astype(np.float32))
```

```python
import contextlib
import numpy as np
import concourse.bass as bass
import concourse.mybir as mybir
from concourse.bass_utils import run_bass_kernel_spmd
from concourse.ap import AP

F32 = mybir.dt.float32
BF16 = mybir.dt.bfloat16
AF = mybir.ActivationFunctionType
ALU = mybir.AluOpType
AX = mybir.AxisListType

D = 1024
ALPHA = 4.0 ** 0.25
EPS = 1e-6
NWB = 6


class Op:
    __slots__ = ("eng", "fn", "deps", "odeps", "ddeps", "dwaits", "sig", "sigidx", "dma", "dma_cnt",
                 "pos", "cost", "users", "nrem", "ready", "fin", "lidx", "tbl", "phase", "issue", "rdma")

    def __init__(self, eng, fn, dma, cost):
        self.eng, self.fn, self.dma, self.cost = eng, fn, dma, cost
        self.deps = []
        self.odeps = []
        self.ddeps = []
        self.dwaits = []
        self.sig = False
        self.sigidx = 0
        self.dma_cnt = 0
        self.pos = 0
        self.users = []
        self.nrem = 0
        self.ready = 0.0
        self.fin = 0.0
        self.lidx = 0
        self.tbl = None
        self.issue = None
        self.rdma = False


class Sched:
    ENGS = ("pe", "act", "dve", "pool", "sp")
    import os as _os4
    OOO = tuple(_os4.environ.get("K_OOO", "pe,dve").split(","))
    import os as _os3
    WINDOW = int(_os3.environ.get("K_WIN", "48"))
    MAXFILL = int(_os3.environ.get("K_MAXFILL", "12000"))
    FILLFRAC = float(_os3.environ.get("K_FILLFRAC", "0.95"))
    FILLCAP = int(_os3.environ.get("K_FILLCAP", "100"))

    def __init__(self):
        self.ops = {e: [] for e in self.ENGS}
        self.lastw = {}
        self.readers = {}
        self.dma_counts = {}
        self.dma_ops = {}
        self.filler = None
        self.nfill = 0
        self.nops = 0
        self.phase = "pro"

    def add(self, eng, fn, r=(), w=(), dma=None, cost=300.0, tbl=None):
        op = Op(eng, fn, dma, cost)
        op.tbl = tbl
        op.phase = self.phase
        op.lidx = self.nops
        self.nops += 1
        r = list(r)
        w = list(w)
        deps = {}

        def consider(d):
            if d is None or d is op:
                return
            deps[id(d)] = d

        for k in r:
            consider(self.lastw.get(k))
        for k in w:
            consider(self.lastw.get(k))
            for d in self.readers.get(k, ()):
                consider(d)
        for d in deps.values():
            if d.dma is not None:
                op.dwaits.append((d.dma, self.dma_counts[d.dma] * 16))
                op.ddeps.append(d)
                lastd = self.dma_ops[d.dma][-1]
                if lastd is not d and lastd is not op:
                    op.ddeps.append(lastd)
            elif d.eng == eng and eng == "pe" and op.dma is None:
                op.odeps.append(d)
            else:
                d.sig = True
                op.deps.append(d)
        for k in r:
            self.readers.setdefault(k, []).append(op)
        for k in w:
            self.lastw[k] = op
            self.readers[k] = []
        if dma is not None:
            self.dma_counts[dma] = self.dma_counts.get(dma, 0) + 1
            op.dma_cnt = self.dma_counts[dma]
            self.dma_ops.setdefault(dma, []).append(op)
        self.ops[eng].append(op)
        return op

    def schedule(self):
        allops = [op for e in self.ENGS for op in self.ops[e]]
        for op in allops:
            op.users = []
        for op in allops:
            ds = op.deps + op.odeps + op.ddeps
            op.nrem = len(ds)
            op.ready = 0.0
            for d in ds:
                d.users.append(op)
        pend = {e: list(self.ops[e]) for e in self.ENGS}
        new = {e: [] for e in self.ENGS}
        free = {e: 0.0 for e in self.ENGS}
        dma_free = [0.0]
        cur_tbl = [None]
        total = len(allops)
        done = 0
        while done < total:
            best = None
            for e in self.ENGS:
                q = pend[e]
                if not q:
                    continue
                cand = None
                if e in self.OOO:
                    lim = min(len(q), self.WINDOW)
                    if e == "act" and q[0].phase == "pro":
                        lim = 1
                    ft = free[e]
                    for i in range(lim):
                        op = q[i]
                        if op.nrem:
                            continue
                        st = op.ready if op.ready > ft else ft
                        if e == "act" and op.tbl is not None and op.tbl != cur_tbl[0]:
                            st += 1300.0
                        if cand is None or st < cand[0]:
                            cand = (st, i, op)
                            if st <= ft:
                                break
                else:
                    op = q[0]
                    if op.nrem == 0:
                        cand = (max(op.ready, free[e]), 0, op)
                if cand is not None and (best is None or cand[0] < best[1][0]):
                    best = (e, cand)
            assert best is not None, "scheduler deadlock (dependency cycle)"
            e, (st, i, op) = best
            if e == "pe" and self.filler is not None and self.nfill < self.MAXFILL:
                gap = st - free["pe"]
                if gap > 400.0 and free["pe"] > 0.0 and not op.rdma:
                    nf = min(int(gap * self.FILLFRAC / 215.0), self.FILLCAP)
                    for _ in range(nf):
                        fo = Op("pe", self.filler[0], None, 215.0)
                        fo.deps = [self.filler[1]]
                        fo.phase = "fill"
                        new["pe"].append(fo)
                        self.nfill += 1
            pend[e].pop(i)
            new[e].append(op)
            if op.dma is not None:
                free[e] = st + (op.issue if op.issue else (1100.0 if e == "pool" else 400.0))
                b = max(st + 1800.0, dma_free[0])
                op.fin = b + op.cost
                dma_free[0] = op.fin
            else:
                op.fin = st + op.cost
                free[e] = op.fin
                if e == "act" and op.tbl is not None:
                    cur_tbl[0] = op.tbl
            for u in op.users:
                u.nrem -= 1
                t = op.fin + (0.0 if u.eng == e and op.dma is None else 120.0)
                if t > u.ready:
                    u.ready = t
                    u.rdma = op.dma is not None
            done += 1
        self.ops = new
        self.sim_time = max(free.values())
        import os as _os
        if _os.environ.get("KDEBUG"):
            ph = {}
            for e in self.ENGS:
                for op in new[e]:
                    d = ph.setdefault(op.phase, {})
                    a = d.setdefault(e, [1e18, 0.0, 0.0])
                    a[0] = min(a[0], op.fin - op.cost)
                    a[1] = max(a[1], op.fin)
                    a[2] += op.cost if op.dma is None else 0.0
            for p, d in ph.items():
                print(f"{p:10s}", "  ".join(f"{e}:{a[0] / 1e3:7.0f}-{a[1] / 1e3:7.0f} busy{a[2] / 1e3:6.0f}" for e, a in d.items()))

    def finalize(self):
        for e in self.ENGS:
            c = 0
            for op in self.ops[e]:
                if op.dma is None and op.sig:
                    c += 1
                    op.sigidx = c

    def check(self):
        ptr = {e: 0 for e in self.ENGS}
        cnt = {("e", e): 0 for e in self.ENGS}
        seen = {e: {} for e in self.ENGS}
        total = sum(len(v) for v in self.ops.values())
        done = 0
        while done < total:
            prog = False
            for e in self.ENGS:
                while ptr[e] < len(self.ops[e]):
                    op = self.ops[e][ptr[e]]
                    ok = True
                    for d in op.deps:
                        if cnt[("e", d.eng)] < d.sigidx:
                            ok = False
                    for (sname, v) in op.dwaits:
                        if cnt.get(("d", sname), 0) < v:
                            ok = False
                    if not ok:
                        break
                    if op.dma is not None:
                        cnt[("d", op.dma)] = cnt.get(("d", op.dma), 0) + 16
                    elif op.sig:
                        cnt[("e", e)] += 1
                        assert cnt[("e", e)] == op.sigidx
                    ptr[e] += 1
                    done += 1
                    prog = True
            if not prog:
                for e in self.ENGS:
                    if ptr[e] < len(self.ops[e]):
                        op = self.ops[e][ptr[e]]
                        print("BLOCKED", e, ptr[e], op.phase, [(d.eng, d.sigidx, cnt[("e", d.eng)]) for d in op.deps],
                              [(s_, v, cnt.get(("d", s_), 0)) for s_, v in op.dwaits])
                raise AssertionError("abstract deadlock")
        return True

    def emit(self, eng, handle, esem, dsem):
        seen = {}
        for op in self.ops[eng]:
            waits = {}
            for d in op.deps:
                k = ("e", d.eng)
                waits[k] = max(waits.get(k, 0), d.sigidx)
            for (s, v) in op.dwaits:
                k = ("d", s)
                waits[k] = max(waits.get(k, 0), v)
            for k, v in waits.items():
                if seen.get(k, 0) >= v:
                    continue
                seen[k] = v
                sem = esem[k[1]] if k[0] == "e" else dsem[k[1]]
                handle.wait_ge(sem, v)
            ins = op.fn(handle)
            if op.dma is not None:
                ins.then_inc(dsem[op.dma], 16)
            elif op.sig:
                ins.then_inc(esem[eng], 1)


def ap_of(base, off, dims):
    return AP(base.tensor, base.offset + off, [list(base.ap[0])] + [list(d) for d in dims])


def build_nc():
    nc = bass.Bass("TRN2", target_bir_lowering=False)
    S = Sched()

    def din(name, shape):
        return nc.dram_tensor(name, list(shape), F32, kind="ExternalInput").ap()

    def dout(name, shape):
        return nc.dram_tensor(name, list(shape), F32, kind="ExternalOutput").ap()

    xin = {"P": din("xp", [1024, D]), "S": din("xs", [1024, D])}
    cvec = din("cvec", [2, D])
    ck_d = din("ck", [2, 256, 128])
    cv_d = din("cv", [2, 256, 128])
    st_d = din("st", [2, 2, 256])
    w_ada = din("w_ada", [2, D, 6 * D])
    b_ada = din("b_ada", [2, 6 * D])
    w_in = din("w_in", [2, D, 1792])
    q_g = din("q_norm_g", [2, 64])
    k_g = din("k_norm_g", [2, 64])
    conv_w = din("conv_w", [2, 4, 256])
    conv_b = din("conv_b", [2, 256])
    lru_wa = din("lru_wa", [2, 2, 4, 64, 64])
    lru_ba = din("lru_ba", [2, 2, 256])
    lru_wx = din("lru_wx", [2, 2, 4, 64, 64])
    lru_bx = din("lru_bx", [2, 2, 256])
    lru_lam = din("lru_lam", [2, 2, 256])
    mlp_g = din("mlp_norm_g", [2, 256])
    mlp_b = din("mlp_norm_b", [2, 256])
    mlp_ws = din("mlp_ws", [2, 4, 128, 128])
    mlp_bs = din("mlp_bs", [2, 4, 128])
    w_out = din("w_out", [2, D, D])
    ln_gb = {(1, 0): din("ln1_g", [2, D]), (1, 1): din("ln1_b", [2, D]),
             (2, 0): din("ln2_g", [2, D]), (2, 1): din("ln2_b", [2, D])}
    w_ff1 = din("w_ff1", [2, D, 4 * D])
    b_ff1 = din("b_ff1", [2, 4 * D])
    w_ff2 = din("w_ff2", [2, 4 * D, D])
    b_ff2 = din("b_ff2", [2, D])
    ident_d = din("ident", [128, 128])
    rope_d = din("rope", [1024, 64])

    yout = {"P": dout("yp", [1024, D]), "S": dout("ys", [1024, D])}
    nk_d = dout("nk", [4, 2, 256, 128])
    nv_d = dout("nv", [4, 2, 256, 128])
    ns_d = dout("ns", [4, 2, 2, 256])
    modD = nc.dram_tensor("modD", [2, 2, 6 * D], F32).ap()

    es = contextlib.ExitStack()
    with es:
        def sb(name, shape, dt):
            return es.enter_context(nc.sbuf_tensor(name, list(shape), dt))

        X = sb("X", [128, 8, D], F32)
        HT = sb("HT", [128, 8, 1024], BF16)
        MH = sb("MH", [128, 8, 1024], BF16)
        WB = [sb(f"WB{i}", [128, 4096], BF16) for i in range(NWB)]
        G1 = sb("G1", [128, D], F32)
        G2 = sb("G2", [128, D], F32)
        LNA = sb("LNA", [128, D], F32)
        LNB = sb("LNB", [128, D], F32)
        QT = sb("QT", [128, 4, 1024], BF16)
        KT = sb("KT", [128, 1280], BF16)
        V2 = sb("V2", [128, 10, 256], BF16)
        PT2 = [sb(f"PT{i}", [128, 1024], BF16) for i in range(2)]
        XR = sb("XR", [128, 2, 1024], F32)
        GG = sb("GG", [128, 2, 1024], BF16)
        UT = sb("UT", [128, 2, 1024], BF16)
        VN = sb("VN", [128, 8, 256], BF16)
        FS = [sb(f"FS{i}", [128, 512], F32) for i in range(3)]
        TG = sb("TG", [128, 1024], F32)
        QBS = [sb(f"QB{i}", [128, 512], BF16) for i in range(2)]
        KF = [sb(f"KF{i}", [128, 128], F32) for i in range(2)]
        VF = [sb(f"VF{i}", [128, 128], F32) for i in range(2)]
        KB = sb("KB", [128, 128], BF16)
        XHB = sb("XHB", [128, 1024], BF16)
        STTA = sb("STTA", [128, 8, 32], F32)
        MSB = sb("MSB", [2, 1024], F32)
        IDB = sb("IDB", [128, 128], BF16)
        ONES = sb("ONES", [128, 128], BF16)
        ROPE = sb("ROPE", [128, 8, 64], F32)
        CVF = sb("CVF", [128, 8, 2], F32)
        SCV = sb("SCV", [128, 8, 2], BF16)
        GQ = sb("GQ", [128, 2, 64], F32)
        GK8 = sb("GK8", [128, 2, 64], F32)
        CW = sb("CW", [128, 2, 4, 2], F32)
        CB = sb("CB", [128, 2, 2], F32)
        BD = sb("BD", [128, 2, 2, 2, 2, 128], BF16)
        LBA = sb("LBA", [128, 2, 2, 2, 2], F32)
        CL = sb("CL", [128, 2, 2, 2], F32)
        MG = sb("MG", [128, 2, 256], F32)
        MBt = sb("MBt", [128, 2, 256], F32)
        WST = sb("WST", [128, 2, 4, 128], BF16)
        BSB = sb("BSB", [128, 2, 2, 128], F32)
        B1 = sb("B1", [128, 2, 32], F32)
        STI = sb("STI", [128, 2, 2, 2], F32)
        LNF = sb("LNF", [128, 2, 4, 8], F32)
        MSC = sb("MSC", [128, 2, 4, 8], F32)
        GCB = sb("GCB", [128, 2, 2, 8], F32)
        MS = MSB[:, 0:512]
        BA = MSB[:, 512:1024]
        stt_n = [0]

        def stt_next():
            i = stt_n[0] % 8
            stt_n[0] += 1
            return STTA[:, i, 0:16], [f"STT{i}"]
        NSS = sb("NSS", [128, 2, 2, 4], F32)
        EPSB = sb("EPSB", [128, 2], F32)
        FILL = sb("FILL", [128, 512], BF16)

        PS2 = [es.enter_context(nc.psum_tensor(f"PS{i}", [128, 1024], F32)) for i in range(4)]
        PSB = [PS2[b // 2][:, (b % 2) * 512:(b % 2 + 1) * 512] for b in range(8)]

        ps_free = list(range(7))

        def ps_get():
            assert ps_free, "out of PSUM banks"
            return ps_free.pop(0)

        def ps_put(b):
            ps_free.append(b)

        def pk(b):
            return [f"ps{b}"]

        wb_state = {"n": 0}

        def wb_load(parts):
            i = wb_state["n"] % NWB
            wb_state["n"] += 1
            for (off, dims, src) in parts:
                dst = ap_of(WB[i][:, 0:1], off, dims)
                nel = 128
                for d_ in dims:
                    nel *= d_[1]
                S.add("pool", (lambda e, dst=dst, src=src: e.dma_start(out=dst, in_=src)),
                      w=[f"WB{i}"], dma=f"wb{i}", cost=nel * 4 / 180.0)
            return i

        def wsrc(wt, l, r0, c0, nk, ncol):
            return wt[l, r0:r0 + nk * 128, c0:c0 + ncol].rearrange("(k p) c -> p k c", p=128)

        def dve(fn, r, w, n=512):
            return S.add("dve", fn, r, w, cost=70.0 + 1.3 * n)

        def act(fn, r, w, n=512, tbl=None):
            return S.add("act", fn, r, w, cost=240.0 + 0.7 * n, tbl=tbl)

        def pe(fn, r, w, n=512):
            return S.add("pe", fn, r, w, cost=12.0 + 0.40 * n)

        def spdma(out, in_, r, w, sem, nbytes=65536):
            return S.add("sp", (lambda e: e.dma_start(out=out, in_=in_, allow_slow_non_contiguous=True)),
                         r, w, dma=sem, cost=nbytes / 180.0)

        def pooldma(out, in_, r, w, sem, nbytes=16384, issue=None):
            op = S.add("pool", (lambda e: e.dma_start(out=out, in_=in_, allow_slow_non_contiguous=True)),
                       r, w, dma=sem, cost=nbytes / 180.0)
            op.issue = issue
            return op

        def htk(ks, ts):
            return [f"HT{k}_{t}" for k in ks for t in ts]

        def mhk(ks, ts):
            return [f"MH{k}_{t}" for k in ks for t in ts]

        def qtk(js, ts):
            return [f"QT{j}_{t}" for j in js for t in ts]

        ALL8 = list(range(8))

        def lbuf(m):
            return HT[:, 2 * m:2 * m + 2, :].rearrange("p a b -> p (a b)").bitcast(F32)

        LA_, LI_, LT_, LH_ = lbuf(0), lbuf(1), lbuf(2), lbuf(3)
        LKEY = [htk([2 * m, 2 * m + 1], ALL8) for m in range(4)]

        def xc_ap(c):
            return QT[:, 2 * c:2 * c + 2, :].rearrange("p a b -> p (a b)").bitcast(F32)

        def xc_key(c):
            return qtk([2 * c, 2 * c + 1], ALL8)

        V2flat = V2[:, :, :].rearrange("p a b -> p (a b)")

        def xcb_ap(c):
            return V2flat[:, c * 1024:(c + 1) * 1024]

        def xcb_key(c):
            return [f"V2_{tt}" for tt in range(4 * c, 4 * c + 4)]

        for t in range(8):
            spdma(X[:, t, :], xin["P"][t * 128:(t + 1) * 128, :], [], [f"X{t}"], "xl", nbytes=524288)
        pooldma(IDB[:, :], ident_d[:, :], [], ["IDB"], "idb")
        for c_ in range(2):
            spdma(CVF[:, :, c_], cvec[c_].rearrange("(k p) -> p k", p=128), [], ["CVF"], "cvf", nbytes=4096)
        act(lambda e: e.activation(out=SCV[:, :, :], in_=CVF[:, :, :], func=AF.Silu), ["CVF"], ["SCV"], n=16)
        def mod_block(l, blk):
            i = wb_load([(0, [[512, 8], [1, 512]], wsrc(w_ada, l, 0, blk * 512, 8, 512))])
            spdma(BA, AP(b_ada.tensor, b_ada[l, blk * 512:blk * 512 + 1].offset, [[0, 2], [1, 512]]),
                  [], ["BA"], "ba", nbytes=4096)
            b = ps_get()
            for k in range(8):
                pe(lambda e, k=k, i=i, b=b: e.matmul(PSB[b][0:2, :], lhsT=SCV[:, k, :],
                                                     rhs=WB[i][:, k * 512:(k + 1) * 512],
                                                     start=(k == 0), stop=(k == 7)),
                   ["SCV", f"WB{i}"], pk(b))
            dve(lambda e, b=b: e.tensor_tensor(out=MS, in0=PSB[b][0:2, :], in1=BA, op=ALU.add),
                pk(b) + ["BA"], ["MS"])
            ps_put(b)
            spdma(modD[l, :, blk * 512:(blk + 1) * 512], MS, ["MS"], [f"modD{l}_{blk // 2}"], "ms", nbytes=4096)

        bg = [(0, blk) for blk in range(4, 12)] + [(1, blk) for blk in range(12)]

        def bg_step(n=1):
            for _ in range(n):
                if bg:
                    mod_block(*bg.pop(0))


        for blk in range(4):
            mod_block(0, blk)
        dve(lambda e: e.memset(ONES[:, :], 1.0), [], ["ONES"])
        fill_ms = dve(lambda e: e.memset(FILL[:, :], 0.5), [], ["FILL"])
        fill_ms.sig = True
        S.filler = (lambda e: e.matmul(PSB[7][:, :], lhsT=ONES[:, :], rhs=FILL[:, :], start=True, stop=True), fill_ms)
        dve(lambda e: e.memset(EPSB[:, 0:1], EPS), [], ["EPSB"])
        dve(lambda e: e.memset(EPSB[:, 1:2], 64 * EPS), [], ["EPSB"])
        for l in range(2):
            spdma(GQ[:, l, :], AP(q_g.tensor, q_g[l, 0:1].offset, [[0, 128], [1, 64]]), [], ["GQ"], "sm", nbytes=32768)
            spdma(GK8[:, l, :], AP(k_g.tensor, k_g[l, 0:1].offset, [[0, 128], [1, 64]]), [], ["GK8"], "sm", nbytes=32768)
            spdma(MG[:, l, :], AP(mlp_g.tensor, mlp_g[l, 0:1].offset, [[0, 128], [1, 256]]), [], ["MG"], "sm")
            spdma(MBt[:, l, :], AP(mlp_b.tensor, mlp_b[l, 0:1].offset, [[0, 128], [1, 256]]), [], ["MBt"], "sm")
        dve(lambda e: e.tensor_scalar(out=GK8[:, :, :], in0=GK8[:, :, :], scalar1=8.0, scalar2=None, op0=ALU.mult),
            ["GK8"], ["GK8"], n=128)

        def load_bd(l, d, gi, wt):
            for n in range(4):
                c, h = n // 2, n % 2
                pooldma(BD[h * 64:(h + 1) * 64, l, d, gi, c, h * 64:(h + 1) * 64],
                        wt[l, d, n, :, :], ["BD"], [f"BDx{l}"], f"bd{l}", issue=4000.0)

        def deferred_bd1():
            for d in range(2):
                for gi, wt in enumerate((lru_wa, lru_wx)):
                    load_bd(1, d, gi, wt)

        def deferred_cl():
            act(lambda e: e.activation(out=CL[:, :, :, :], in_=CL[:, :, :, :], func=AF.Exp, scale=-1.0), ["CL"], ["CL"],
                n=8, tbl="explog")
            act(lambda e: e.activation(out=CL[:, :, :, :], in_=CL[:, :, :, :], func=AF.Ln, bias=1.0), ["CL"], ["CL"],
                n=8, tbl="explog")
            dve(lambda e: e.tensor_scalar(out=CL[:, :, :, :], in0=CL[:, :, :, :], scalar1=-8.0, scalar2=None,
                                          op0=ALU.mult), ["CL"], ["CL"], n=8)

        def deferred_params():
            dve(lambda e: e.memset(BD[:, :, :, :, :, :].rearrange("p a b c d f -> p (a b c d f)"), 0.0), [], ["BD"],
                n=2048)
            spdma(ROPE[:, :, :], rope_d.rearrange("(t p) c -> p t c", p=128), [], ["ROPE"], "sm3", nbytes=8192)
            for l in range(2):
                for j_ in range(4):
                    spdma(CW[:, l, j_, :], conv_w[l, j_].rearrange("(c p) -> p c", p=128), [], ["CW"], "sm3", nbytes=8192)
                spdma(CB[:, l, :], conv_b[l].rearrange("(c p) -> p c", p=128), [], ["CB"], "sm3", nbytes=8192)
                for d in range(2):
                    for gi, (wt, bt) in enumerate(((lru_wa, lru_ba), (lru_wx, lru_bx))):
                        spdma(LBA[:, l, d, gi, :], bt[l, d].rearrange("(c p) -> p c", p=128), [], ["LBA"], "sm3", nbytes=8192)
                        if l == 0:
                            load_bd(l, d, gi, wt)
                    spdma(CL[:, l, d, :], lru_lam[l, d].rearrange("(c p) -> p c", p=128), [], ["CL"], "cl", nbytes=8192)
                    spdma(STI[:, l, d, :], st_d[l, d].rearrange("(c p) -> p c", p=128), [], ["STI"], "sm3", nbytes=8192)
                for g in range(4):
                    gi, c2 = g % 2, g // 2
                    spdma(BSB[gi * 64:(gi + 1) * 64, l, c2, :],
                          AP(mlp_bs.tensor, mlp_bs[l, g, 0:1].offset, [[0, 64], [1, 128]]), [], ["BSB"], "sm3", nbytes=8192)
                spdma(B1[:, l, :], b_ff1[l].rearrange("(c p) -> p c", p=128), [], ["B1"], "sm3", nbytes=8192)
                for j, key in enumerate(((1, 0), (1, 1), (2, 0), (2, 1))):
                    spdma(LNF[:, l, j, :], ln_gb[key][l].rearrange("(c p) -> p c", p=128), [], ["LNF"], "sm3", nbytes=8192)
            for l in range(2):
                i = wb_load([(0, [[128, 4], [1, 128]], mlp_ws[l].rearrange("g p q -> p g q"))])
                b = ps_get()
                pst = PSB[b][:, :].bitcast(BF16)
                for g in range(4):
                    pe(lambda e, g=g, i=i, pst=pst: e.transpose(out=pst[:, g * 128:(g + 1) * 128],
                                                                in_=WB[i][:, g * 128:(g + 1) * 128], identity=IDB[:, :]),
                       [f"WB{i}", "IDB"], pk(b), n=128)
                dve(lambda e, l=l, pst=pst: e.tensor_copy(out=WST[:, l, :, :].rearrange("p g q -> p (g q)"),
                                                           in_=pst[:, 0:512]), pk(b), ["WST"])
                ps_put(b)

        def load_msc(l, cond, half):
            lp = l % 2
            key = [f"MSC{lp}{half}"]
            for j, idx in (((0, 0), (1, 1)) if half == 0 else ((2, 3), (3, 4))):
                spdma(MSC[:, lp, j, :], modD[l, cond, idx * D:(idx + 1) * D].rearrange("(c p) -> p c", p=128),
                      [f"modD{l}_{idx}"], key, f"msc{lp}{half}", nbytes=4096)
            j = 1 if half == 0 else 3
            dve(lambda e: e.tensor_scalar(out=MSC[:, lp, j, :], in0=MSC[:, lp, j, :], scalar1=1.0,
                                          scalar2=None, op0=ALU.add), key, key, n=8)

        def bcast_row(dst, src_t, off_ap, key_r, key_w, sem):
            spdma(dst[:, :], AP(src_t.tensor, off_ap.offset, [[0, 128], [1, D]]), key_r, key_w, sem)

        def nmt_a1(t):
            xk = [f"X{t}"]
            xt = X[:, t, :]
            st, sk = stt_next()
            dve(lambda e: e.bn_stats(out=st[:, 0:6], in_=xt[:, 0:512]), xk, sk, n=450)
            dve(lambda e: e.bn_stats(out=st[:, 6:12], in_=xt[:, 512:1024]), xk, sk, n=450)
            dve(lambda e: e.bn_aggr(out=st[:, 12:14], in_=st[:, 0:12]), sk, sk, n=100)
            return st, sk

        def nmt_a2(t, stsk, final_store=None):
            st, sk = stsk
            xk = [f"X{t}"]
            xt = X[:, t, :]
            act(lambda e: e.activation(out=st[:, 14:15], in_=st[:, 13:14], func=AF.Sqrt, bias=EPSB[:, 0:1]),
                sk + ["EPSB"], sk, n=60, tbl="sqrt")
            dve(lambda e: e.reciprocal(out=st[:, 14:15], in_=st[:, 14:15]), sk, sk, n=80)
            if final_store is None:
                dve(lambda e: e.tensor_scalar(out=st[:, 15:16], in0=st[:, 12:13], scalar1=st[:, 14:15], scalar2=-1.0,
                                              op0=ALU.mult, op1=ALU.mult), sk, sk, n=8)
                act(lambda e: e.activation(out=XHB[:, :], in_=xt, func=AF.Identity, scale=st[:, 14:15],
                                           bias=st[:, 15:16]), xk + sk, ["XHB"], n=1024)
            dve(lambda e: e.scalar_tensor_tensor(out=xt, in0=xt, scalar=st[:, 12:13], in1=LNA[:, :],
                                                 op0=ALU.subtract, op1=ALU.mult), xk + sk + ["LNA"], xk, n=900)
            dve(lambda e: e.scalar_tensor_tensor(out=xt, in0=xt, scalar=st[:, 14:15], in1=LNB[:, :],
                                                 op0=ALU.mult, op1=ALU.add), xk + sk + ["LNB"], xk, n=900)
            if final_store is not None:
                spdma(final_store, xt, xk, [], "yst", nbytes=524288)

        def nmt_b(t, which):
            b = ps_get()
            pst = PSB[b][:, :].bitcast(BF16)
            for k in range(8):
                pe(lambda e, k=k, pst=pst: e.transpose(out=pst[:, k * 128:(k + 1) * 128],
                                                       in_=XHB[:, k * 128:(k + 1) * 128], identity=IDB[:, :]),
                   ["XHB", "IDB"], pk(b), n=128)
            for k in range(8):
                act(lambda e, k=k, pst=pst: e.activation(out=HT[:, k, t * 128:(t + 1) * 128],
                                                         in_=pst[:, k * 128:(k + 1) * 128], func=AF.Identity,
                                                         scale=GCB[:, which, 0, k:k + 1],
                                                         bias=GCB[:, which, 1, k:k + 1]),
                    pk(b) + [f"GCB{which}"], htk([k], [t]), n=128)
            ps_put(b)

        def nmt(t, l, which, norm, la_lb, final_store=None):
            xk = [f"X{t}"]
            xt = X[:, t, :]
            if norm:
                nmt_a2(t, nmt_a1(t), final_store)
            else:
                act(lambda e: e.copy(out=XHB[:, :], in_=xt), xk, ["XHB"], n=1024)
                dve(lambda e: e.tensor_scalar(out=xt, in0=xt, scalar1=ALPHA, scalar2=None, op0=ALU.mult), xk, xk, n=600)
            if final_store is None:
                nmt_b(t, which)

        def qk_norm1(src_ps, nh, bkeys, sq, sqk):
            n = nh * 64
            st, sk = stt_next()
            act(lambda e: e.activation(out=sq[:, 0:n], in_=src_ps, func=AF.Square), bkeys, sqk, n=n)
            dve(lambda e: e.tensor_reduce(out=st[:, 0:nh], in_=sq[:, 0:n].rearrange("p (h d) -> p h d", h=nh),
                                          axis=AX.X, op=ALU.add), sqk, sk, n=n)
            return st, sk

        def qk_norm2(stsk, src_ps, nh, dst_f, gtile, bkeys, dkeys, gkey, out_ap=None, okeys=None):
            st, sk = stsk
            n = nh * 64
            act(lambda e: e.activation(out=st[:, 0:nh], in_=st[:, 0:nh], func=AF.Sqrt, bias=EPSB[:, 1:2]),
                sk + ["EPSB"], sk, n=60, tbl="sqrt")
            dve(lambda e: e.reciprocal(out=st[:, 0:nh], in_=st[:, 0:nh]), sk, sk, n=80)
            dve(lambda e: e.tensor_tensor(out=dst_f.rearrange("p (h d) -> p h d", h=nh),
                                          in0=src_ps.rearrange("p (h d) -> p h d", h=nh),
                                          in1=ap_of(st[:, 0:1], 0, [[1, nh], [0, 64]]), op=ALU.mult),
                bkeys + sk, dkeys, n=n)
            if out_ap is None:
                d3 = dst_f.rearrange("p (h d) -> p h d", h=nh)
                g3 = ap_of(gtile[:, 0:1], 0, [[0, nh], [1, 64]])
                dve(lambda e: e.tensor_tensor(out=d3, in0=d3, in1=g3, op=ALU.mult), dkeys + [gkey], dkeys, n=n)
            else:
                o_, i0_, i1_ = out_ap
                dve(lambda e: e.tensor_tensor(out=o_, in0=i0_, in1=i1_, op=ALU.mult), dkeys + [gkey], okeys, n=n)

        def rope(src_f, nh, t, dst_b, dst_dims_even, skeys, dkeys):
            T = [TG[:, i * 256:i * 256 + nh * 32].rearrange("p (h i) -> p h i", h=nh) for i in range(4)]
            x1 = ap_of(src_f[:, 0:1], 0, [[64, nh], [2, 32]])
            x2 = ap_of(src_f[:, 0:1], 1, [[64, nh], [2, 32]])
            cs = ap_of(ROPE[:, t, 0:1], 0, [[0, nh], [1, 32]])
            sn = ap_of(ROPE[:, t, 0:1], 32, [[0, nh], [1, 32]])
            dve(lambda e: e.tensor_tensor(out=T[0], in0=x1, in1=cs, op=ALU.mult), skeys + ["ROPE"], ["TG"])
            dve(lambda e: e.tensor_tensor(out=T[1], in0=x2, in1=sn, op=ALU.mult), skeys + ["ROPE"], ["TG"])
            dve(lambda e: e.tensor_tensor(out=T[2], in0=x1, in1=sn, op=ALU.mult), skeys + ["ROPE"], ["TG"])
            dve(lambda e: e.tensor_tensor(out=T[3], in0=x2, in1=cs, op=ALU.mult), skeys + ["ROPE"], ["TG"])
            if nh == 8:
                Tv = [ap_of(TG[:, 0:1], i * 256, [[128, 2], [32, 4], [1, 32]]) for i in range(4)]
            else:
                Tv = T
            de = ap_of(dst_b, 0, dst_dims_even)
            do = ap_of(dst_b, 1, dst_dims_even)
            dve(lambda e: e.tensor_tensor(out=de, in0=Tv[0], in1=Tv[1], op=ALU.subtract), ["TG"], dkeys)
            dve(lambda e: e.tensor_tensor(out=do, in0=Tv[2], in1=Tv[3], op=ALU.add), ["TG"], dkeys)

        class _Stop(Exception):
            pass

        import os as _os2
        _stop = _os2.environ.get("K_STOP")

        def stop_at(name):
            if _stop == name:
                raise _Stop()

        def group_layer(grp, l, first, last):
            is_s = grp == "S"
            cond = 1 if is_s else 0
            lp = l % 2
            nseq = 1 if is_s else 4
            L = 1024 // nseq
            ktoff = 256 if is_s else 0
            vtoff = 2 if is_s else 0

            dve(lambda e: e.memset(ap_of(V2[:, 0, 0:1], 64, [[128, 20], [1, 64]]), 1.0), [],
                [f"V2_{tt}" for tt in range(10)], n=1280)
            if first:
                load_msc(l, cond, 0)
                if is_s:
                    for t in range(8):
                        spdma(X[:, t, :], xin[grp][t * 128:(t + 1) * 128, :], [], [f"X{t}"], f"xs{t}", nbytes=524288)
                dve(lambda e: e.tensor_copy(out=GCB[:, 0, 0, :], in_=MSC[:, lp, 1, :]), [f"MSC{lp}0"], ["GCB0"], n=8)
                dve(lambda e: e.tensor_copy(out=GCB[:, 0, 1, :], in_=MSC[:, lp, 0, :]), [f"MSC{lp}0"], ["GCB0"], n=8)
            if is_s:
                for tt in range(2):
                    spdma(KF[tt][:, :], ck_d[l, tt * 128:(tt + 1) * 128, :], [], [f"KF{tt}"], f"kst{tt}")
                    spdma(VF[tt][:, :], cv_d[l, tt * 128:(tt + 1) * 128, :], [], [f"VF{tt}"], f"vst{tt}")

            def late_setup():
                mk = [f"MSC{lp}1"]
                load_msc(l, cond, 1)
                bcast_row(G1, modD, modD[l, cond, 2 * D:2 * D + 1], [f"modD{l}_2"], ["G1"], "g1")
                dve(lambda e: e.tensor_tensor(out=GCB[:, 1, 0, :], in0=LNF[:, l, 0, :], in1=MSC[:, lp, 3, :],
                                              op=ALU.mult), ["LNF"] + mk, ["GCB1"], n=8)
                dve(lambda e: e.tensor_tensor(out=GCB[:, 1, 1, :], in0=LNF[:, l, 1, :], in1=MSC[:, lp, 3, :],
                                              op=ALU.mult), ["LNF"] + mk, ["GCB1"], n=8)
                dve(lambda e: e.tensor_tensor(out=GCB[:, 1, 1, :], in0=GCB[:, 1, 1, :], in1=MSC[:, lp, 2, :],
                                              op=ALU.add), ["GCB1"] + mk, ["GCB1"], n=8)
                bcast_row(G2, modD, modD[l, cond, 5 * D:5 * D + 1], [f"modD{l}_5"], ["G2"], "g2")
                bcast_row(LNA, ln_gb[(1, 0)], ln_gb[(1, 0)][l, 0:1], [], ["LNA"], "lna")
                bcast_row(LNB, ln_gb[(1, 1)], ln_gb[(1, 1)][l, 0:1], [], ["LNB"], "lnb")
                bcast_row(TG, b_ff2, b_ff2[l, 0:1], [], ["TG", "TGv0", "TGv1"], "lnc")
                dve(lambda e: e.tensor_scalar(out=LNA[:, :], in0=LNA[:, :], scalar1=ALPHA, scalar2=None, op0=ALU.mult),
                    ["LNA"], ["LNA"], n=1024)
                dve(lambda e: e.tensor_tensor(out=TG[:, :], in0=TG[:, :], in1=G2[:, :], op=ALU.mult), ["TG", "G2"],
                    ["TG"], n=1024)
                dve(lambda e: e.scalar_tensor_tensor(out=LNB[:, :], in0=LNB[:, :], scalar=ALPHA, in1=TG[:, :],
                                                     op0=ALU.mult, op1=ALU.add), ["LNB", "TG"], ["LNB"], n=1024)

            stop_at(f"{grp}{l}:start")
            S.phase = f"{grp}{l}:win"
            if first:
                for t in range(2):
                    nmt(t, l, 0, norm=False, la_lb=False)

            stop_at(f"{grp}{l}:h1")
            iA = wb_load([(0, [[512, 8], [1, 512]], wsrc(w_in, l, 0, 0, 8, 512))])
            iB = wb_load([(0, [[512, 8], [1, 256]], wsrc(w_in, l, 0, 512, 8, 256)),
                          (256, [[512, 8], [1, 256]], wsrc(w_in, l, 0, 1536, 8, 256))])
            iC = wb_load([(0, [[512, 8], [1, 512]], wsrc(w_in, l, 0, 768, 8, 512))])
            iD = wb_load([(0, [[512, 8], [1, 256]], wsrc(w_in, l, 0, 1280, 8, 256))])

            if is_s:
                for tt in range(2):
                    act(lambda e, tt=tt: e.copy(out=KB[:, :], in_=KF[tt][:, :]), [f"KF{tt}"], ["KB"])
                    b = ps_get()
                    pst = PSB[b][:, :].bitcast(BF16)
                    pe(lambda e, pst=pst: e.transpose(out=pst[:, 0:128], in_=KB[:, :], identity=IDB[:, :]),
                       ["KB", "IDB"], pk(b), n=128)
                    dve(lambda e, tt=tt, pst=pst: e.tensor_copy(out=KT[:, tt * 128:(tt + 1) * 128], in_=pst[:, 0:128]),
                        pk(b), [f"KT{tt}"])
                    ps_put(b)
                    dve(lambda e, tt=tt: e.tensor_copy(
                        out=ap_of(V2[:, tt, 0:1], 0, [[128, 2], [1, 64]]),
                        in_=VF[tt][:, :].rearrange("p (k d) -> p k d", k=2)), [f"VF{tt}"], [f"V2_{tt}"], n=128)

            pm_dims = [[64, 2], [128, 4], [1, 64]]
            nat_dims = [[256, 2], [64, 4], [1, 64]]
            qst = {}

            def q_a1(t):
                if first and t + 2 < 8:
                    nmt(t + 2, l, 0, norm=False, la_lb=False)
                b = ps_get()
                for k in range(8):
                    pe(lambda e, k=k, b=b, t=t: e.matmul(PSB[b][:, :], lhsT=HT[:, k, t * 128:(t + 1) * 128],
                                                         rhs=WB[iA][:, k * 512:(k + 1) * 512],
                                                         start=(k == 0), stop=(k == 7)),
                       htk([k], [t]) + [f"WB{iA}"], pk(b))
                qst[t] = (b, qk_norm1(PSB[b][:, :], 8, pk(b), FS[2], ["FS2"]))

            def q_a2(t):
                b, stsk = qst[t]
                qf = FS[t % 2]
                qfk = [f"FS{t % 2}"]
                QB = QBS[t % 2]
                qbk = [f"QB{t % 2}"]
                if is_s:
                    qk_norm2(stsk, PSB[b][:, :], 8, qf[:, :], GQ[:, l, :], pk(b), qfk, "GQ")
                    ps_put(b)
                    rope(qf[:, :], 8, t, QB[:, 0:1], [[64, 2], [128, 4], [2, 32]], qfk, qbk)
                else:
                    qk_norm2(stsk, PSB[b][:, :], 8, qf[:, :], GQ[:, l, :], pk(b), qfk, "GQ",
                             out_ap=(ap_of(QB[:, 0:1], 0, pm_dims), ap_of(qf[:, 0:1], 0, nat_dims),
                                     ap_of(GQ[:, l, 0:1], 0, [[0, 2], [0, 4], [1, 64]])), okeys=qbk)
                    ps_put(b)

            def q_b(t):
                QB = QBS[t % 2]
                qbk = [f"QB{t % 2}"]
                b2 = ps_get()
                pst = PSB[b2][:, :].bitcast(BF16)
                for j in range(4):
                    pe(lambda e, j=j, pst=pst, QB=QB: e.transpose(out=pst[:, j * 128:(j + 1) * 128],
                                                                  in_=QB[:, j * 128:(j + 1) * 128], identity=IDB[:, :]),
                       qbk + ["IDB"], pk(b2), n=128)
                act(lambda e, t=t, pst=pst: e.copy(out=QT[:, :, t * 128:(t + 1) * 128],
                                                   in_=pst[:, 0:512].rearrange("p (j q) -> p j q", j=4)),
                    pk(b2), qtk(range(4), [t]))
                ps_put(b2)

            kst = {}

            def kv_a1(t):
                b = ps_get()
                for k in range(8):
                    pe(lambda e, k=k, b=b, t=t: e.matmul(PSB[b][:, :], lhsT=HT[:, k, t * 128:(t + 1) * 128],
                                                         rhs=WB[iB][:, k * 512:(k + 1) * 512],
                                                         start=(k == 0), stop=(k == 7)),
                       htk([k], [t]) + [f"WB{iB}"], pk(b))
                sq = FS[2][:, (t % 2) * 128:(t % 2 + 1) * 128]
                stsk = qk_norm1(PSB[b][:, 0:128], 2, pk(b), sq, [f"FS2k{t % 2}", "FS2"])
                tv = vtoff + t
                if not is_s:
                    vf = VF[t % 2]
                    act(lambda e, vf=vf, b=b: e.copy(out=vf[:, :], in_=PSB[b][:, 128:256]), pk(b), [f"VF{t % 2}"], n=128)
                    s_, tt = t // 2, t % 2
                    spdma(nv_d[s_, l, tt * 128:(tt + 1) * 128, :], vf[:, :], [f"VF{t % 2}"], [], f"vst{t % 2}")
                dve(lambda e, tv=tv, b=b: e.tensor_copy(
                    out=ap_of(V2[:, tv, 0:1], 0, [[128, 2], [1, 64]]),
                    in_=PSB[b][:, 128:256].rearrange("p (k d) -> p k d", k=2)), pk(b), [f"V2_{tv}"], n=128)
                vg = TG[:, (t % 2) * 256:(t % 2 + 1) * 256] if not is_s else FS[t % 2][:, 0:256]
                vgk = [f"TGv{t % 2}"] if not is_s else [f"FS{t % 2}"]
                act(lambda e, b=b, vg=vg: e.activation(out=vg, in_=PSB[b][:, 256:512], func=AF.Gelu_apprx_tanh),
                    pk(b), vgk, n=256, tbl="gelu")
                st2, sk2 = stt_next()
                dve(lambda e, st2=st2, vg=vg: e.bn_stats(out=st2[:, 0:6], in_=vg), vgk, sk2, n=256)
                dve(lambda e, st2=st2: e.bn_aggr(out=st2[:, 12:14], in_=st2[:, 0:6]), sk2, sk2, n=60)
                kst[t] = (b, stsk, vg, vgk, st2, sk2)

            def kv_a2(t):
                b, stsk, vg, vgk, st2, sk2 = kst[t]
                kf = KF[t % 2]
                kfk = [f"KF{t % 2}"]
                qk_norm2(stsk, PSB[b][:, 0:128], 2, kf[:, :], GK8[:, l, :], pk(b), kfk, "GK8")
                ps_put(b)
                act(lambda e, st2=st2: e.activation(out=st2[:, 14:15], in_=st2[:, 13:14], func=AF.Sqrt,
                                                    bias=EPSB[:, 0:1]), sk2 + ["EPSB"], sk2, n=60, tbl="sqrt")
                dve(lambda e, st2=st2: e.reciprocal(out=st2[:, 14:15], in_=st2[:, 14:15]), sk2, sk2, n=80)
                dve(lambda e, st2=st2, vg=vg: e.tensor_scalar(out=vg, in0=vg, scalar1=st2[:, 12:13],
                                                              scalar2=st2[:, 14:15], op0=ALU.subtract, op1=ALU.mult),
                    vgk + sk2, vgk, n=256)
                dve(lambda e, vg=vg: e.tensor_tensor(out=vg, in0=vg, in1=MG[:, l, :], op=ALU.mult), vgk + ["MG"], vgk,
                    n=256)
                dve(lambda e, t=t, vg=vg: e.tensor_tensor(out=VN[:, t, :], in0=vg, in1=MBt[:, l, :], op=ALU.add),
                    vgk + ["MBt"], [f"VN{t}"], n=256)
                if is_s:
                    rope(kf[:, :], 2, t, KB[:, 0:1], [[64, 2], [2, 32]], kfk, ["KB"])

            def kv_b(t):
                kf = KF[t % 2]
                kfk = [f"KF{t % 2}"]
                if not is_s:
                    s_, tt = t // 2, t % 2
                    spdma(nk_d[s_, l, tt * 128:(tt + 1) * 128, :], kf[:, :], kfk, [], f"kst{t % 2}")
                    act(lambda e, kf=kf: e.copy(out=KB[:, :], in_=kf[:, :]), kfk, ["KB"], n=128)
                b2 = ps_get()
                pst = PSB[b2][:, :].bitcast(BF16)
                pe(lambda e, pst=pst: e.transpose(out=pst[:, 0:128], in_=KB[:, :], identity=IDB[:, :]),
                   ["KB", "IDB"], pk(b2), n=128)
                kc = ktoff + t * 128
                dve(lambda e, kc=kc, pst=pst: e.tensor_copy(out=KT[:, kc:kc + 128], in_=pst[:, 0:128]),
                    pk(b2), [f"KT{kc // 128}"], n=128)
                ps_put(b2)

            def skewed(a1, a2, bb):
                for step in range(8 + 2):
                    if step < 8:
                        a1(step)
                    if 0 <= step - 2 < 8:
                        bb(step - 2)
                    if 0 <= step - 1 < 8:
                        a2(step - 1)

            skewed(q_a1, q_a2, q_b)
            stop_at(f"{grp}{l}:q")
            skewed(kv_a1, kv_a2, kv_b)
            stop_at(f"{grp}{l}:kv")

            for (slot, ncol, j) in [(iC, 512, 0), (iC, 512, 1), (iC, 512, 2), (iC, 512, 3), (iD, 256, 0), (iD, 256, 1)]:
                for tg in range(2):
                    b = ps_get()
                    for k in range(8):
                        pe(lambda e, k=k, b=b, tg=tg, slot=slot, ncol=ncol, j=j: e.matmul(
                            PSB[b][:, :], lhsT=WB[slot][:, k * 512 + j * 128:k * 512 + (j + 1) * 128],
                            rhs=HT[:, k, tg * 512:(tg + 1) * 512], start=(k == 0), stop=(k == 7)),
                           htk([k], range(tg * 4, tg * 4 + 4)) + [f"WB{slot}"], pk(b))
                    cols = slice(tg * 512, (tg + 1) * 512)
                    if slot == iC and j < 2:
                        act(lambda e, b=b, j=j, cols=cols: e.copy(out=XR[:, j, cols], in_=PSB[b][:, :]),
                            pk(b), [f"XR{j}"])
                    elif slot == iC:
                        act(lambda e, b=b, j=j, cols=cols: e.activation(out=GG[:, j - 2, cols], in_=PSB[b][:, :],
                                                                        func=AF.Gelu_apprx_tanh),
                            pk(b), [f"GG{j - 2}"], tbl="gelu")
                    else:
                        act(lambda e, b=b, j=j, cols=cols: e.activation(out=UT[:, j, cols], in_=PSB[b][:, :],
                                                                        func=AF.Gelu_apprx_tanh),
                            pk(b), [f"UT{j}"], tbl="gelu")
                    ps_put(b)


            if grp == "P" and l == 0:
                S.phase = "P0:defer"
                deferred_params()
            stop_at(f"{grp}{l}:att")
            S.phase = f"{grp}{l}:att"
            npt = 0
            nunit = 0
            npair = [0]
            if is_s:
                for b_ in (0, 1, 2, 3):
                    ps_free.remove(b_)
            for u in range(nseq if not is_s else 2):
                if is_s:
                    q0, N = u * 512, 512
                    tts = list(range(10))
                else:
                    q0, N = u * 256, 256
                    tts = [2 * u, 2 * u + 1]
                qts = list(range(q0 // 128, (q0 + N) // 128))
                for j in range(4):
                    for a in range(2):
                        h = j + 4 * a
                        rows = slice(a * 64, a * 64 + 64)
                        bo = ps_get()
                        if is_s:
                            for pi in range(5):
                                kp = npair[0] % 2
                                npair[0] += 1
                                b0_, b1_ = 2 * kp, 2 * kp + 1
                                for hb_, tt in ((b0_, 2 * pi), (b1_, 2 * pi + 1)):
                                    kc = tt * 128
                                    pe(lambda e, hb_=hb_, kc=kc, rows=rows, j=j, q0=q0, N=N: e.matmul(
                                        PSB[hb_][:, 0:N], lhsT=KT[rows, kc:kc + 128], rhs=QT[rows, j, q0:q0 + N],
                                        start=True, stop=True),
                                       [f"KT{kc // 128}"] + qtk([j], qts), pk(hb_), n=N)
                                pt = PT2[npt % 2]
                                ptk = [f"PT{npt % 2}_{k_}" for k_ in range(4)]
                                npt += 1
                                act(lambda e, kp=kp, pt=pt: e.activation(out=pt[:, :], in_=PS2[kp][:, :], func=AF.Exp),
                                    pk(b0_) + pk(b1_), ptk, n=850, tbl="explog")
                                for hi_, tt in ((0, 2 * pi), (1, 2 * pi + 1)):
                                    pe(lambda e, bo=bo, tt=tt, a=a, pt=pt, hi_=hi_: e.matmul(
                                        PSB[bo][:, 0:512], lhsT=V2[:, tt, a * 128:(a + 1) * 128],
                                        rhs=pt[:, hi_ * 512:(hi_ + 1) * 512],
                                        start=(tt == 0), stop=(tt == 9)), [f"V2_{tt}"] + ptk, pk(bo), n=512)
                        else:
                            for ti, tt in enumerate(tts):
                                bs_ = ps_get()
                                kc = tt * 128
                                pe(lambda e, bs_=bs_, kc=kc, rows=rows, j=j, q0=q0, N=N: e.matmul(
                                    PSB[bs_][:, 0:N], lhsT=KT[rows, kc:kc + 128], rhs=QT[rows, j, q0:q0 + N],
                                    start=True, stop=True),
                                   [f"KT{kc // 128}"] + qtk([j], qts), pk(bs_), n=N)
                                pt = PT2[(npt // 4) % 2][:, (npt % 4) * 256:(npt % 4 + 1) * 256]
                                ptk = [f"PT{(npt // 4) % 2}_{npt % 4}"]
                                npt += 1
                                act(lambda e, bs_=bs_, pt=pt, N=N: e.activation(out=pt[:, 0:N], in_=PSB[bs_][:, 0:N],
                                                                                func=AF.Exp), pk(bs_), ptk, n=N,
                                    tbl="explog")
                                ps_put(bs_)
                                pe(lambda e, bo=bo, tt=tt, a=a, pt=pt, N=N, ti=ti, nt=len(tts): e.matmul(
                                    PSB[bo][:, 0:N], lhsT=V2[:, tt, a * 128:(a + 1) * 128], rhs=pt[:, 0:N],
                                    start=(ti == 0), stop=(ti == nt - 1)), [f"V2_{tt}"] + ptk, pk(bo), n=N)
                        pr = slice((h % 2) * 64, (h % 2) * 64 + 64)
                        rz = FS[h % 2]
                        rzk = [f"FS{h % 2}"]
                        if is_s or nunit % 3 == 2:
                            dve(lambda e, bo=bo, rz=rz, N=N: e.reciprocal(out=rz[0:64, 0:N], in_=PSB[bo][64:128, 0:N]),
                                pk(bo), rzk, n=int(5.0 * N))
                        else:
                            act(lambda e, bo=bo, rz=rz, N=N: e.activation(out=rz[0:64, 0:N], in_=PSB[bo][64:128, 0:N],
                                                                          func=AF.Ln), pk(bo), rzk, n=N, tbl="explog")
                            act(lambda e, rz=rz, N=N: e.activation(out=rz[0:64, 0:N], in_=rz[0:64, 0:N], func=AF.Exp,
                                                                   scale=-1.0), rzk, rzk, n=N, tbl="explog")
                        dve(lambda e, bo=bo, rz=rz, pr=pr, N=N, h=h, q0=q0: e.tensor_tensor(
                            out=MH[pr, h // 2, q0:q0 + N], in0=PSB[bo][0:64, 0:N], in1=rz[0:64, 0:N], op=ALU.mult),
                            pk(bo) + rzk, mhk([h // 2], qts), n=N)
                        ps_put(bo)
                        nunit += 1

            if is_s:
                ps_free.extend([0, 1, 2, 3])
            stop_at(f"{grp}{l}:lru")
            S.phase = f"{grp}{l}:lru"
            if grp == "P" and l == 0:
                deferred_cl()
            def lru_chunk(c):
                xr = XR[:, c, :]
                xc = xc_ap(c)
                xck = xc_key(c)
                xrk = [f"XR{c}"]
                w_ = lambda j: CW[:, l, j, c:c + 1]
                xr3 = xr.rearrange("p (s t) -> p s t", s=nseq)
                xc3 = xc.rearrange("p (s t) -> p s t", s=nseq)
                dve(lambda e: e.tensor_scalar(out=xc, in0=xr, scalar1=w_(1), scalar2=CB[:, l, c:c + 1],
                                              op0=ALU.mult, op1=ALU.add), xrk + ["CW", "CB"], xck)
                dve(lambda e: e.scalar_tensor_tensor(out=xc3[:, :, 1:L], in0=xr3[:, :, 0:L - 1], scalar=w_(0),
                                                     in1=xc3[:, :, 1:L], op0=ALU.mult, op1=ALU.add),
                    xrk + xck + ["CW"], xck)
                dve(lambda e: e.scalar_tensor_tensor(out=xc3[:, :, 0:L - 1], in0=xr3[:, :, 1:L], scalar=w_(2),
                                                     in1=xc3[:, :, 0:L - 1], op0=ALU.mult, op1=ALU.add),
                    xrk + xck + ["CW"], xck)
                dve(lambda e: e.scalar_tensor_tensor(out=xc3[:, :, 0:L - 2], in0=xr3[:, :, 2:L], scalar=w_(3),
                                                     in1=xc3[:, :, 0:L - 2], op0=ALU.mult, op1=ALU.add),
                    xrk + xck + ["CW"], xck)
                xcb = xcb_ap(c)
                xcbk = xcb_key(c)
                act(lambda e: e.copy(out=xcb, in_=xc), xck, xcbk)
                def lru_dir(d):
                    for gi, (dst, dkey) in enumerate(((LA_, LKEY[0]), (LI_, LKEY[1]))):
                        for tg in range(2):
                            b = ps_get()
                            pe(lambda e, b=b, gi=gi, tg=tg: e.matmul(
                                PSB[b][:, :], lhsT=BD[:, l, d, gi, c, :], rhs=xcb[:, tg * 512:(tg + 1) * 512],
                                start=True, stop=True), [f"BDx{l}"] + xcbk, pk(b))
                            act(lambda e, b=b, gi=gi, tg=tg, dst=dst: e.activation(
                                out=dst[:, tg * 512:(tg + 1) * 512], in_=PSB[b][:, :], func=AF.Sigmoid,
                                bias=LBA[:, l, d, gi, c:c + 1]), pk(b) + ["LBA"], dkey, tbl="sig")
                            ps_put(b)
                    act(lambda e: e.activation(out=LA_, in_=LA_, func=AF.Exp, scale=CL[:, l, d, c:c + 1]),
                        LKEY[0] + ["CL"], LKEY[0], n=1024, tbl="explog")
                    dve(lambda e: e.tensor_tensor(out=LT_, in0=LA_, in1=LA_, op=ALU.mult), LKEY[0], LKEY[2], n=1024)
                    act(lambda e: e.activation(out=LT_, in_=LT_, func=AF.Ln, scale=-1.0, bias=1.0), LKEY[2], LKEY[2],
                        n=1024, tbl="explog")
                    act(lambda e: e.activation(out=LT_, in_=LT_, func=AF.Exp, scale=0.5), LKEY[2], LKEY[2],
                        n=1024, tbl="explog")
                    dve(lambda e: e.tensor_tensor(out=LI_, in0=LI_, in1=LT_, op=ALU.mult), LKEY[1] + LKEY[2], LKEY[1],
                        n=1024)
                    dve(lambda e: e.tensor_tensor(out=LI_, in0=LI_, in1=xc, op=ALU.mult), LKEY[1] + xck, LKEY[1], n=1024)
                    hdst = xr if d == 0 else LH_
                    hkey = xrk if d == 0 else LKEY[3]
                    for s_ in range(nseq):
                        if d == 0:
                            sl = lambda tns: tns[:, s_ * L:(s_ + 1) * L]
                        else:
                            def sl(tns, s_=s_):
                                base = tns[:, (s_ + 1) * L - 1:(s_ + 1) * L]
                                return AP(base.tensor, base.offset, [list(base.ap[0]), [-1, L]])
                        init = STI[:, l, d, c:c + 1] if is_s else 0.0
                        o_, a_, u_ = sl(hdst), sl(LA_), sl(LI_)
                        dve(lambda e, o_=o_, a_=a_, u_=u_, init=init: e.tensor_tensor_scan(
                            out=o_, data0=a_, data1=u_, initial=init, op0=ALU.mult, op1=ALU.add),
                            LKEY[0] + LKEY[1] + ["STI"], hkey)
                    if not is_s:
                        fin = (L - 1) if d == 0 else 0
                        dve(lambda e, hdst=hdst, fin=fin, d=d: e.tensor_copy(
                            out=NSS[:, d, c, :], in_=ap_of(hdst[:, 0:1], fin, [[L, 4]])), hkey, ["NSS"])
                for d in range(2):
                    bg_step(2)
                    lru_dir(d)
                dve(lambda e: e.tensor_tensor(out=xr, in0=xr, in1=LH_, op=ALU.add), xrk + LKEY[3], xrk)
                dve(lambda e: e.tensor_tensor(out=MH[:, 4 + c, :], in0=xr, in1=GG[:, c, :], op=ALU.mult),
                    xrk + [f"GG{c}"], mhk([4 + c], ALL8))
            for c in range(2):
                lru_chunk(c)
            while bg and bg[0][0] <= l:
                bg_step()
            late_setup()
            iE = wb_load([(0, [[512, 8], [1, 512]], wsrc(w_out, l, 0, 0, 8, 512))])
            iF = wb_load([(0, [[512, 8], [1, 512]], wsrc(w_out, l, 0, 512, 8, 512))])
            if not is_s:
                for s_ in range(4):
                    for d in range(2):
                        spdma(ns_d[s_, l, d, :].rearrange("(c p) -> p c", p=128), NSS[:, d, :, s_], ["NSS"], [], "nst")

            stop_at(f"{grp}{l}:mlp")
            S.phase = f"{grp}{l}:mlp"
            def gmlp_mix(c2, gi):
                if True:
                    g = 2 * c2 + gi
                    b0, b1 = ps_get(), ps_get()
                    for t in range(8):
                        bb = b0 if t < 4 else b1
                        pe(lambda e, bb=bb, t=t, g=g: e.matmul(
                            PSB[bb][:, (t % 4) * 128:(t % 4 + 1) * 128], lhsT=VN[:, t, c2 * 128:(c2 + 1) * 128],
                            rhs=WST[:, l, g, :], start=True, stop=True), [f"VN{t}", "WST"], pk(bb))
                    pr = slice(gi * 64, gi * 64 + 64)
                    for hh, bb in enumerate((b0, b1)):
                        cols = slice(hh * 512, (hh + 1) * 512)
                        dve(lambda e, bb=bb, pr=pr, cols=cols: e.tensor_tensor(
                            out=TG[pr, cols].rearrange("p (t q) -> p t q", t=4),
                            in0=PSB[bb][pr, :].rearrange("p (t q) -> p t q", t=4),
                            in1=ap_of(BSB[pr, l, c2, 0:1], 0, [[0, 4], [1, 128]]), op=ALU.add),
                            pk(bb) + ["BSB"], ["TG"])
                        dve(lambda e, pr=pr, cols=cols: e.tensor_tensor(
                            out=MH[pr, 6 + c2, cols], in0=TG[pr, cols], in1=UT[pr, c2, cols], op=ALU.mult),
                            ["TG", f"UT{c2}"], mhk([6 + c2], range(hh * 4, hh * 4 + 4)))
                    ps_put(b0)
                    ps_put(b1)

            for c2 in range(2):
                for gi in range(2):
                    bg_step(1)
                    gmlp_mix(c2, gi)

            stop_at(f"{grp}{l}:wout")
            S.phase = f"{grp}{l}:wout"
            wst_ = {}

            def wo_a1(t):
                for hf, slot in enumerate((iE, iF)):
                    b = ps_get()
                    for k in range(8):
                        pe(lambda e, k=k, b=b, t=t, slot=slot: e.matmul(
                            PSB[b][:, :], lhsT=MH[:, k, t * 128:(t + 1) * 128], rhs=WB[slot][:, k * 512:(k + 1) * 512],
                            start=(k == 0), stop=(k == 7)), mhk([k], [t]) + [f"WB{slot}"], pk(b))
                    cols = slice(hf * 512, (hf + 1) * 512)
                    tmp = FS[(2 * t + hf) % 3]
                    tk = [f"FS{(2 * t + hf) % 3}"]
                    dve(lambda e, b=b, cols=cols, tmp=tmp: e.tensor_tensor(out=tmp[:, :], in0=PSB[b][:, :],
                                                                           in1=G1[:, cols], op=ALU.mult),
                        pk(b) + ["G1"], tk)
                    ps_put(b)
                    dve(lambda e, t=t, cols=cols, tmp=tmp: e.tensor_tensor(out=X[:, t, cols], in0=X[:, t, cols],
                                                                           in1=tmp[:, :], op=ALU.add),
                        tk + [f"X{t}"], [f"X{t}"])
                wst_[t] = nmt_a1(t)

            skewed(wo_a1, lambda t: nmt_a2(t, wst_[t]), lambda t: nmt_b(t, 1))

            if not last:
                load_msc(l + 1, cond, 0)
                nlp = (l + 1) % 2
                dve(lambda e: e.tensor_tensor(out=GCB[:, 0, 0, :], in0=LNF[:, l, 2, :], in1=MSC[:, nlp, 1, :],
                                              op=ALU.mult), ["LNF", f"MSC{nlp}0"], ["GCB0"], n=8)
                dve(lambda e: e.tensor_tensor(out=GCB[:, 0, 1, :], in0=LNF[:, l, 3, :], in1=MSC[:, nlp, 1, :],
                                              op=ALU.mult), ["LNF", f"MSC{nlp}0"], ["GCB0"], n=8)
                dve(lambda e: e.tensor_tensor(out=GCB[:, 0, 1, :], in0=GCB[:, 0, 1, :], in1=MSC[:, nlp, 0, :],
                                              op=ALU.add), ["GCB0", f"MSC{nlp}0"], ["GCB0"], n=8)
            bcast_row(LNA, ln_gb[(2, 0)], ln_gb[(2, 0)][l, 0:1], [], ["LNA"], "lna")
            bcast_row(LNB, ln_gb[(2, 1)], ln_gb[(2, 1)][l, 0:1], [], ["LNB"], "lnb")
            if not last:
                dve(lambda e: e.tensor_scalar(out=LNA[:, :], in0=LNA[:, :], scalar1=ALPHA, scalar2=None,
                                              op0=ALU.mult), ["LNA"], ["LNA"])
                dve(lambda e: e.tensor_scalar(out=LNB[:, :], in0=LNB[:, :], scalar1=ALPHA, scalar2=None,
                                              op0=ALU.mult), ["LNB"], ["LNB"])

            stop_at(f"{grp}{l}:ffn")
            S.phase = f"{grp}{l}:ffn"
            if grp == "P" and l == 0:
                deferred_bd1()
            nrl = 0
            for qd in range(4):
                i1 = [wb_load([(0, [[512, 8], [1, 512]], wsrc(w_ff1, l, 0, qd * 1024 + hb * 512, 8, 512))])
                      for hb in range(2)]
                i2 = [wb_load([(0, [[1024, 4], [1, 1024]], wsrc(w_ff2, l, qd * 1024 + hb * 512, 0, 4, 1024))])
                      for hb in range(2)]
                for tg in range(2):
                    for hc in range(8):
                        slot = i1[hc // 4]
                        cc = (hc % 4) * 128
                        b = ps_get()
                        for k in range(8):
                            pe(lambda e, k=k, b=b, tg=tg, slot=slot, cc=cc: e.matmul(
                                PSB[b][:, :], lhsT=WB[slot][:, k * 512 + cc:k * 512 + cc + 128],
                                rhs=HT[:, k, tg * 512:(tg + 1) * 512], start=(k == 0), stop=(k == 7)),
                               htk([k], range(tg * 4, tg * 4 + 4)) + [f"WB{slot}"], pk(b))
                        rl = FS[nrl % 3]
                        rk = [f"FS{nrl % 3}"]
                        nrl += 1
                        chunk = qd * 8 + hc
                        act(lambda e, b=b, rl=rl, chunk=chunk: e.activation(out=rl[:, :], in_=PSB[b][:, :], func=AF.Relu,
                                                                            bias=B1[:, l, chunk:chunk + 1]),
                            pk(b) + ["B1"], rk)
                        ps_put(b)
                        act(lambda e, rl=rl, hc=hc, tg=tg: e.activation(out=MH[:, hc, tg * 512:(tg + 1) * 512],
                                                                        in_=rl[:, :], func=AF.Square),
                            rk, mhk([hc], range(tg * 4, tg * 4 + 4)))
                for tg in range(2):
                    for t in range(tg * 4, tg * 4 + 4):
                        for hf in range(2):
                            b = ps_get()
                            for hc in range(8):
                                slot = i2[hc // 4]
                                pe(lambda e, hc=hc, b=b, t=t, slot=slot, hf=hf: e.matmul(
                                    PSB[b][:, :], lhsT=MH[:, hc, t * 128:(t + 1) * 128],
                                    rhs=WB[slot][:, (hc % 4) * 1024 + hf * 512:(hc % 4) * 1024 + (hf + 1) * 512],
                                    start=(hc == 0), stop=(hc == 7)), mhk([hc], [t]) + [f"WB{slot}"], pk(b))
                            cols = slice(hf * 512, (hf + 1) * 512)
                            tmp = FS[nrl % 3]
                            tk = [f"FS{nrl % 3}"]
                            nrl += 1
                            dve(lambda e, b=b, cols=cols, tmp=tmp: e.tensor_tensor(out=tmp[:, :], in0=PSB[b][:, :],
                                                                                   in1=G2[:, cols], op=ALU.mult),
                                pk(b) + ["G2"], tk)
                            ps_put(b)
                            dve(lambda e, t=t, cols=cols, tmp=tmp: e.tensor_tensor(out=X[:, t, cols], in0=X[:, t, cols],
                                                                                   in1=tmp[:, :], op=ALU.add),
                                tk + [f"X{t}"], [f"X{t}"])
                        if qd == 3:
                            S.phase = f"{grp}{l}:ln2"
                            if last:
                                nmt(t, l, 0, norm=True, la_lb=True, final_store=yout[grp][t * 128:(t + 1) * 128, :])
                            else:
                                nmt(t, l, 0, norm=True, la_lb=True)
                            S.phase = f"{grp}{l}:ffn"

        try:
            for grp in ("P", "S"):
                for l in range(2):
                    group_layer(grp, l, first=(l == 0), last=(l == 1))
        except _Stop:
            pass

        final_sems = ["yst", "nst", "kst0", "kst1", "vst0", "vst1"]
        import os as _os
        S.schedule()
        if _os.environ.get("KDEBUG"):
            print("ops", {e: len(v) for e, v in S.ops.items()}, "sim_us", S.sim_time / 1e3,
                  "sbuf_left", nc.sbuf_bytes_remaining)
        S.finalize()
        S.check()
        if _os.environ.get("KDUMP"):
            for i, op in enumerate(S.ops[_os.environ["KDUMP"]]):
                print(i, "lidx", op.lidx, "tbl", op.tbl, "cost", int(op.cost), "sig", op.sigidx if op.sig else None,
                      "deps", [(d.eng, d.sigidx, d.lidx) for d in op.deps], "dw", op.dwaits)

        dma_names = sorted(S.dma_counts.keys())
        sems = {}
        for n in list(dma_names) + ["e_pe", "e_act", "e_dve", "e_pool", "e_sp"]:
            sems[n] = es.enter_context(nc.semaphore(n))
        esem = {e: sems["e_" + e] for e in Sched.ENGS}
        dsem = {n: sems[n] for n in dma_names}

        with nc.Block() as block:
            @block.tensor
            def _(e):
                S.emit("pe", e, esem, dsem)

            @block.scalar
            def _(e):
                S.emit("act", e, esem, dsem)

            @block.vector
            def _(e):
                S.emit("dve", e, esem, dsem)

            @block.gpsimd
            def _(e):
                S.emit("pool", e, esem, dsem)

            @block.sync
            def _(e):
                S.emit("sp", e, esem, dsem)
                for n in final_sems:
                    if n in dsem:
                        e.wait_ge(dsem[n], S.dma_counts[n] * 16)
    return nc


_NC_CACHE = {}


def _rope_table():
    pos = np.arange(1024)
    pr = (pos // 64).astype(np.float32)
    pc = (pos % 64).astype(np.float32)
    inv = (10000.0 ** (-np.arange(16, dtype=np.float32) / 16)).astype(np.float32)
    ang = np.concatenate([pr[:, None] * inv, pc[:, None] * inv], -1).astype(np.float32)
    return np.concatenate([np.cos(ang), np.sin(ang)], -1).astype(np.float32)


def kernel(x_prompt, x_sample, c, cache_k, cache_v, state_lru, c_ctx, w_ada, b_ada, w_in,
           q_norm_g, k_norm_g, conv_w, conv_b, lru_wa, lru_ba, lru_wx, lru_bx, lru_lam,
           mlp_norm_g, mlp_norm_b, mlp_ws, mlp_bs, w_out, ln1_g, ln1_b, w_ff1, b_ff1,
           w_ff2, b_ff2, ln2_g, ln2_b):
    f = lambda a: np.ascontiguousarray(np.asarray(a, dtype=np.float32))
    if "nc" not in _NC_CACHE:
        _NC_CACHE["nc"] = build_nc()
    nc = _NC_CACHE["nc"]
    shared = dict(w_ada=f(w_ada), b_ada=f(b_ada), w_in=f(w_in), q_norm_g=f(q_norm_g), k_norm_g=f(k_norm_g),
                  conv_w=f(conv_w), conv_b=f(conv_b), lru_wa=f(lru_wa), lru_ba=f(lru_ba), lru_wx=f(lru_wx),
                  lru_bx=f(lru_bx), lru_lam=f(lru_lam), mlp_norm_g=f(mlp_norm_g), mlp_norm_b=f(mlp_norm_b),
                  mlp_ws=f(mlp_ws), mlp_bs=f(mlp_bs), w_out=f(w_out), ln1_g=f(ln1_g), ln1_b=f(ln1_b),
                  w_ff1=f(w_ff1), b_ff1=f(b_ff1), w_ff2=f(w_ff2), b_ff2=f(b_ff2), ln2_g=f(ln2_g), ln2_b=f(ln2_b),
                  ident=np.eye(128, dtype=np.float32), rope=_rope_table())
    x_prompt, x_sample, c, c_ctx = f(x_prompt), f(x_sample), f(c), f(c_ctx)
    cache_k, cache_v, state_lru = f(cache_k), f(cache_v), f(state_lru)
    in_maps = []
    for i in range(8):
        m = dict(shared)
        m["xp"] = np.ascontiguousarray(x_prompt[4 * i:4 * i + 4].reshape(1024, D))
        m["xs"] = np.ascontiguousarray(x_sample[i])
        m["cvec"] = np.ascontiguousarray(np.stack([c_ctx, c[i]], 0))
        m["ck"] = np.ascontiguousarray(cache_k[i].reshape(2, 256, 128))
        m["cv"] = np.ascontiguousarray(cache_v[i].reshape(2, 256, 128))
        m["st"] = np.ascontiguousarray(state_lru[i])
        in_maps.append(m)
    res = run_bass_kernel_spmd(nc, in_maps, core_ids=list(range(8)))
    R = res.results
    y_prompt = np.concatenate([r["yp"].reshape(4, 256, D) for r in R], 0)
    y_sample = np.stack([r["ys"] for r in R], 0)
    nk = np.concatenate([r["nk"].reshape(4, 2, 256, 2, 64) for r in R], 0)
    nv = np.concatenate([r["nv"].reshape(4, 2, 256, 2, 64) for r in R], 0)
    ns = np.concatenate([r["ns"] for r in R], 0)
    return (y_prompt.astype(np.float32), y_sample.astype(np.float32), nk.astype(np.float32),
            nv.astype(np.float32), ns.astype(np.float32))
```

```python
import contextlib
import numpy as np
import concourse.bass as bass
import concourse.mybir as mybir
from concourse.bass_utils import run_bass_kernel_spmd
from concourse.ap import AP

F32 = mybir.dt.float32
BF16 = mybir.dt.bfloat16
AF = mybir.ActivationFunctionType
ALU = mybir.AluOpType
AX = mybir.AxisListType

D = 1024
ALPHA = 4.0 ** 0.25
EPS = 1e-6
NWB = 6


class Op:
    __slots__ = ("eng", "fn", "deps", "odeps", "ddeps", "dwaits", "sig", "sigidx", "dma", "dma_cnt",
                 "pos", "cost", "users", "nrem", "ready", "fin", "lidx", "tbl", "phase", "issue", "rdma")

    def __init__(self, eng, fn, dma, cost):
        self.eng, self.fn, self.dma, self.cost = eng, fn, dma, cost
        self.deps = []
        self.odeps = []
        self.ddeps = []
        self.dwaits = []
        self.sig = False
        self.sigidx = 0
        self.dma_cnt = 0
        self.pos = 0
        self.users = []
        self.nrem = 0
        self.ready = 0.0
        self.fin = 0.0
        self.lidx = 0
        self.tbl = None
        self.issue = None
        self.rdma = False


class Sched:
    ENGS = ("pe", "act", "dve", "pool", "sp")
    import os as _os4
    OOO = tuple(_os4.environ.get("K_OOO", "pe,dve").split(","))
    import os as _os3
    WINDOW = int(_os3.environ.get("K_WIN", "32"))
    MAXFILL = int(_os3.environ.get("K_MAXFILL", "12000"))
    FILLFRAC = float(_os3.environ.get("K_FILLFRAC", "0.95"))
    FILLCAP = int(_os3.environ.get("K_FILLCAP", "100"))

    def __init__(self):
        self.ops = {e: [] for e in self.ENGS}
        self.lastw = {}
        self.readers = {}
        self.dma_counts = {}
        self.dma_ops = {}
        self.filler = None
        self.nfill = 0
        self.nops = 0
        self.phase = "pro"

    def add(self, eng, fn, r=(), w=(), dma=None, cost=300.0, tbl=None):
        op = Op(eng, fn, dma, cost)
        op.tbl = tbl
        op.phase = self.phase
        op.lidx = self.nops
        self.nops += 1
        r = list(r)
        w = list(w)
        deps = {}

        def consider(d):
            if d is None or d is op:
                return
            deps[id(d)] = d

        for k in r:
            consider(self.lastw.get(k))
        for k in w:
            consider(self.lastw.get(k))
            for d in self.readers.get(k, ()):
                consider(d)
        for d in deps.values():
            if d.dma is not None:
                op.dwaits.append((d.dma, self.dma_counts[d.dma] * 16))
                op.ddeps.append(d)
                lastd = self.dma_ops[d.dma][-1]
                if lastd is not d and lastd is not op:
                    op.ddeps.append(lastd)
            elif d.eng == eng and eng == "pe" and op.dma is None:
                op.odeps.append(d)
            else:
                d.sig = True
                op.deps.append(d)
        for k in r:
            self.readers.setdefault(k, []).append(op)
        for k in w:
            self.lastw[k] = op
            self.readers[k] = []
        if dma is not None:
            self.dma_counts[dma] = self.dma_counts.get(dma, 0) + 1
            op.dma_cnt = self.dma_counts[dma]
            self.dma_ops.setdefault(dma, []).append(op)
        self.ops[eng].append(op)
        return op

    def schedule(self):
        allops = [op for e in self.ENGS for op in self.ops[e]]
        for op in allops:
            op.users = []
        for op in allops:
            ds = op.deps + op.odeps + op.ddeps
            op.nrem = len(ds)
            op.ready = 0.0
            for d in ds:
                d.users.append(op)
        pend = {e: list(self.ops[e]) for e in self.ENGS}
        new = {e: [] for e in self.ENGS}
        free = {e: 0.0 for e in self.ENGS}
        dma_free = [0.0]
        cur_tbl = [None]
        total = len(allops)
        done = 0
        while done < total:
            best = None
            for e in self.ENGS:
                q = pend[e]
                if not q:
                    continue
                cand = None
                if e in self.OOO:
                    lim = min(len(q), self.WINDOW)
                    if e == "act" and q[0].phase == "pro":
                        lim = 1
                    ft = free[e]
                    for i in range(lim):
                        op = q[i]
                        if op.nrem:
                            continue
                        st = op.ready if op.ready > ft else ft
                        if e == "act" and op.tbl is not None and op.tbl != cur_tbl[0]:
                            st += 1300.0
                        if cand is None or st < cand[0]:
                            cand = (st, i, op)
                            if st <= ft:
                                break
                else:
                    op = q[0]
                    if op.nrem == 0:
                        cand = (max(op.ready, free[e]), 0, op)
                if cand is not None and (best is None or cand[0] < best[1][0]):
                    best = (e, cand)
            assert best is not None, "scheduler deadlock (dependency cycle)"
            e, (st, i, op) = best
            if e == "pe" and self.filler is not None and self.nfill < self.MAXFILL:
                gap = st - free["pe"]
                if gap > 400.0 and free["pe"] > 0.0 and not op.rdma:
                    nf = min(int(gap * self.FILLFRAC / 215.0), self.FILLCAP)
                    for _ in range(nf):
                        fo = Op("pe", self.filler[0], None, 215.0)
                        fo.deps = [self.filler[1]]
                        fo.phase = "fill"
                        new["pe"].append(fo)
                        self.nfill += 1
            pend[e].pop(i)
            new[e].append(op)
            if op.dma is not None:
                free[e] = st + (op.issue if op.issue else (1100.0 if e == "pool" else 400.0))
                b = max(st + 1800.0, dma_free[0])
                op.fin = b + op.cost
                dma_free[0] = op.fin
            else:
                op.fin = st + op.cost
                free[e] = op.fin
                if e == "act" and op.tbl is not None:
                    cur_tbl[0] = op.tbl
            for u in op.users:
                u.nrem -= 1
                t = op.fin + (0.0 if u.eng == e and op.dma is None else 120.0)
                if t > u.ready:
                    u.ready = t
                    u.rdma = op.dma is not None
            done += 1
        self.ops = new
        self.sim_time = max(free.values())
        import os as _os
        if _os.environ.get("KDEBUG"):
            ph = {}
            for e in self.ENGS:
                for op in new[e]:
                    d = ph.setdefault(op.phase, {})
                    a = d.setdefault(e, [1e18, 0.0, 0.0])
                    a[0] = min(a[0], op.fin - op.cost)
                    a[1] = max(a[1], op.fin)
                    a[2] += op.cost if op.dma is None else 0.0
            for p, d in ph.items():
                print(f"{p:10s}", "  ".join(f"{e}:{a[0] / 1e3:7.0f}-{a[1] / 1e3:7.0f} busy{a[2] / 1e3:6.0f}" for e, a in d.items()))

    def finalize(self):
        for e in self.ENGS:
            c = 0
            for op in self.ops[e]:
                if op.dma is None and op.sig:
                    c += 1
                    op.sigidx = c

    def check(self):
        ptr = {e: 0 for e in self.ENGS}
        cnt = {("e", e): 0 for e in self.ENGS}
        seen = {e: {} for e in self.ENGS}
        total = sum(len(v) for v in self.ops.values())
        done = 0
        while done < total:
            prog = False
            for e in self.ENGS:
                while ptr[e] < len(self.ops[e]):
                    op = self.ops[e][ptr[e]]
                    ok = True
                    for d in op.deps:
                        if cnt[("e", d.eng)] < d.sigidx:
                            ok = False
                    for (sname, v) in op.dwaits:
                        if cnt.get(("d", sname), 0) < v:
                            ok = False
                    if not ok:
                        break
                    if op.dma is not None:
                        cnt[("d", op.dma)] = cnt.get(("d", op.dma), 0) + 16
                    elif op.sig:
                        cnt[("e", e)] += 1
                        assert cnt[("e", e)] == op.sigidx
                    ptr[e] += 1
                    done += 1
                    prog = True
            if not prog:
                for e in self.ENGS:
                    if ptr[e] < len(self.ops[e]):
                        op = self.ops[e][ptr[e]]
                        print("BLOCKED", e, ptr[e], op.phase, [(d.eng, d.sigidx, cnt[("e", d.eng)]) for d in op.deps],
                              [(s_, v, cnt.get(("d", s_), 0)) for s_, v in op.dwaits])
                raise AssertionError("abstract deadlock")
        return True

    def emit(self, eng, handle, esem, dsem):
        seen = {}
        for op in self.ops[eng]:
            waits = {}
            for d in op.deps:
                k = ("e", d.eng)
                waits[k] = max(waits.get(k, 0), d.sigidx)
            for (s, v) in op.dwaits:
                k = ("d", s)
                waits[k] = max(waits.get(k, 0), v)
            for k, v in waits.items():
                if seen.get(k, 0) >= v:
                    continue
                seen[k] = v
                sem = esem[k[1]] if k[0] == "e" else dsem[k[1]]
                handle.wait_ge(sem, v)
            ins = op.fn(handle)
            if op.dma is not None:
                ins.then_inc(dsem[op.dma], 16)
            elif op.sig:
                ins.then_inc(esem[eng], 1)


def ap_of(base, off, dims):
    return AP(base.tensor, base.offset + off, [list(base.ap[0])] + [list(d) for d in dims])


def build_nc():
    nc = bass.Bass("TRN2", target_bir_lowering=False)
    S = Sched()

    def din(name, shape):
        return nc.dram_tensor(name, list(shape), F32, kind="ExternalInput").ap()

    def dout(name, shape):
        return nc.dram_tensor(name, list(shape), F32, kind="ExternalOutput").ap()

    xin = {"P": din("xp", [1024, D]), "S": din("xs", [1024, D])}
    cvec = din("cvec", [2, D])
    ck_d = din("ck", [2, 256, 128])
    cv_d = din("cv", [2, 256, 128])
    st_d = din("st", [2, 2, 256])
    w_ada = din("w_ada", [2, D, 6 * D])
    b_ada = din("b_ada", [2, 6 * D])
    w_in = din("w_in", [2, D, 1792])
    q_g = din("q_norm_g", [2, 64])
    k_g = din("k_norm_g", [2, 64])
    conv_w = din("conv_w", [2, 4, 256])
    conv_b = din("conv_b", [2, 256])
    lru_wa = din("lru_wa", [2, 2, 4, 64, 64])
    lru_ba = din("lru_ba", [2, 2, 256])
    lru_wx = din("lru_wx", [2, 2, 4, 64, 64])
    lru_bx = din("lru_bx", [2, 2, 256])
    lru_lam = din("lru_lam", [2, 2, 256])
    mlp_g = din("mlp_norm_g", [2, 256])
    mlp_b = din("mlp_norm_b", [2, 256])
    mlp_ws = din("mlp_ws", [2, 4, 128, 128])
    mlp_bs = din("mlp_bs", [2, 4, 128])
    w_out = din("w_out", [2, D, D])
    ln_gb = {(1, 0): din("ln1_g", [2, D]), (1, 1): din("ln1_b", [2, D]),
             (2, 0): din("ln2_g", [2, D]), (2, 1): din("ln2_b", [2, D])}
    w_ff1 = din("w_ff1", [2, D, 4 * D])
    b_ff1 = din("b_ff1", [2, 4 * D])
    w_ff2 = din("w_ff2", [2, 4 * D, D])
    b_ff2 = din("b_ff2", [2, D])
    ident_d = din("ident", [128, 128])
    rope_d = din("rope", [1024, 64])

    yout = {"P": dout("yp", [1024, D]), "S": dout("ys", [1024, D])}
    nk_d = dout("nk", [4, 2, 256, 128])
    nv_d = dout("nv", [4, 2, 256, 128])
    ns_d = dout("ns", [4, 2, 2, 256])
    modD = nc.dram_tensor("modD", [2, 2, 6 * D], F32).ap()

    es = contextlib.ExitStack()
    with es:
        def sb(name, shape, dt):
            return es.enter_context(nc.sbuf_tensor(name, list(shape), dt))

        X = sb("X", [128, 8, D], F32)
        HT = sb("HT", [128, 8, 1024], BF16)
        MH = sb("MH", [128, 8, 1024], BF16)
        WB = [sb(f"WB{i}", [128, 4096], BF16) for i in range(NWB)]
        G1 = sb("G1", [128, D], F32)
        G2 = sb("G2", [128, D], F32)
        LNA = sb("LNA", [128, D], F32)
        LNB = sb("LNB", [128, D], F32)
        QT = sb("QT", [128, 4, 1024], BF16)
        KT = sb("KT", [128, 1280], BF16)
        V2 = sb("V2", [128, 10, 256], BF16)
        PT2 = [sb(f"PT{i}", [128, 1024], BF16) for i in range(2)]
        XR = sb("XR", [128, 2, 1024], F32)
        GG = sb("GG", [128, 2, 1024], BF16)
        UT = sb("UT", [128, 2, 1024], BF16)
        VN = sb("VN", [128, 8, 256], BF16)
        FS = [sb(f"FS{i}", [128, 512], F32) for i in range(3)]
        TG = sb("TG", [128, 1024], F32)
        QBS = [sb(f"QB{i}", [128, 512], BF16) for i in range(2)]
        KF = [sb(f"KF{i}", [128, 128], F32) for i in range(2)]
        VF = [sb(f"VF{i}", [128, 128], F32) for i in range(2)]
        KB = sb("KB", [128, 128], BF16)
        XHB = sb("XHB", [128, 1024], BF16)
        STTA = sb("STTA", [128, 8, 32], F32)
        MSB = sb("MSB", [2, 1024], F32)
        IDB = sb("IDB", [128, 128], BF16)
        ONES = sb("ONES", [128, 128], BF16)
        ROPE = sb("ROPE", [128, 8, 64], F32)
        CVF = sb("CVF", [128, 8, 2], F32)
        SCV = sb("SCV", [128, 8, 2], BF16)
        GQ = sb("GQ", [128, 2, 64], F32)
        GK8 = sb("GK8", [128, 2, 64], F32)
        CW = sb("CW", [128, 2, 4, 2], F32)
        CB = sb("CB", [128, 2, 2], F32)
        BD = sb("BD", [128, 2, 2, 2, 2, 128], BF16)
        LBA = sb("LBA", [128, 2, 2, 2, 2], F32)
        CL = sb("CL", [128, 2, 2, 2], F32)
        MG = sb("MG", [128, 2, 256], F32)
        MBt = sb("MBt", [128, 2, 256], F32)
        WST = sb("WST", [128, 2, 4, 128], BF16)
        BSB = sb("BSB", [128, 2, 2, 128], F32)
        B1 = sb("B1", [128, 2, 32], F32)
        STI = sb("STI", [128, 2, 2, 2], F32)
        LNF = sb("LNF", [128, 2, 4, 8], F32)
        MSC = sb("MSC", [128, 2, 4, 8], F32)
        GCB = sb("GCB", [128, 2, 2, 8], F32)
        MS = MSB[:, 0:512]
        BA = MSB[:, 512:1024]
        stt_n = [0]

        def stt_next():
            i = stt_n[0] % 8
            stt_n[0] += 1
            return STTA[:, i, 0:16], [f"STT{i}"]
        NSS = sb("NSS", [128, 2, 2, 4], F32)
        EPSB = sb("EPSB", [128, 2], F32)
        FILL = sb("FILL", [128, 512], BF16)

        PS2 = [es.enter_context(nc.psum_tensor(f"PS{i}", [128, 1024], F32)) for i in range(4)]
        PSB = [PS2[b // 2][:, (b % 2) * 512:(b % 2 + 1) * 512] for b in range(8)]

        ps_free = list(range(7))

        def ps_get():
            assert ps_free, "out of PSUM banks"
            return ps_free.pop(0)

        def ps_put(b):
            ps_free.append(b)

        def pk(b):
            return [f"ps{b}"]

        wb_state = {"n": 0}

        def wb_load(parts):
            i = wb_state["n"] % NWB
            wb_state["n"] += 1
            for (off, dims, src) in parts:
                dst = ap_of(WB[i][:, 0:1], off, dims)
                nel = 128
                for d_ in dims:
                    nel *= d_[1]
                S.add("pool", (lambda e, dst=dst, src=src: e.dma_start(out=dst, in_=src)),
                      w=[f"WB{i}"], dma=f"wb{i}", cost=nel * 4 / 180.0)
            return i

        def wsrc(wt, l, r0, c0, nk, ncol):
            return wt[l, r0:r0 + nk * 128, c0:c0 + ncol].rearrange("(k p) c -> p k c", p=128)

        def dve(fn, r, w, n=512):
            return S.add("dve", fn, r, w, cost=70.0 + 1.3 * n)

        def act(fn, r, w, n=512, tbl=None):
            return S.add("act", fn, r, w, cost=240.0 + 0.7 * n, tbl=tbl)

        def pe(fn, r, w, n=512):
            return S.add("pe", fn, r, w, cost=12.0 + 0.40 * n)

        def spdma(out, in_, r, w, sem, nbytes=65536):
            return S.add("sp", (lambda e: e.dma_start(out=out, in_=in_, allow_slow_non_contiguous=True)),
                         r, w, dma=sem, cost=nbytes / 180.0)

        def pooldma(out, in_, r, w, sem, nbytes=16384, issue=None):
            op = S.add("pool", (lambda e: e.dma_start(out=out, in_=in_, allow_slow_non_contiguous=True)),
                       r, w, dma=sem, cost=nbytes / 180.0)
            op.issue = issue
            return op

        def htk(ks, ts):
            return [f"HT{k}_{t}" for k in ks for t in ts]

        def mhk(ks, ts):
            return [f"MH{k}_{t}" for k in ks for t in ts]

        def qtk(js, ts):
            return [f"QT{j}_{t}" for j in js for t in ts]

        ALL8 = list(range(8))

        def lbuf(m):
            return HT[:, 2 * m:2 * m + 2, :].rearrange("p a b -> p (a b)").bitcast(F32)

        LA_, LI_, LT_, LH_ = lbuf(0), lbuf(1), lbuf(2), lbuf(3)
        LKEY = [htk([2 * m, 2 * m + 1], ALL8) for m in range(4)]

        def xc_ap(c):
            return QT[:, 2 * c:2 * c + 2, :].rearrange("p a b -> p (a b)").bitcast(F32)

        def xc_key(c):
            return qtk([2 * c, 2 * c + 1], ALL8)

        V2flat = V2[:, :, :].rearrange("p a b -> p (a b)")

        def xcb_ap(c):
            return V2flat[:, c * 1024:(c + 1) * 1024]

        def xcb_key(c):
            return [f"V2_{tt}" for tt in range(4 * c, 4 * c + 4)]

        for t in range(8):
            spdma(X[:, t, :], xin["P"][t * 128:(t + 1) * 128, :], [], [f"X{t}"], "xl", nbytes=524288)
        pooldma(IDB[:, :], ident_d[:, :], [], ["IDB"], "idb")
        for c_ in range(2):
            spdma(CVF[:, :, c_], cvec[c_].rearrange("(k p) -> p k", p=128), [], ["CVF"], "cvf", nbytes=4096)
        act(lambda e: e.activation(out=SCV[:, :, :], in_=CVF[:, :, :], func=AF.Silu), ["CVF"], ["SCV"], n=16)
        def mod_block(l, blk):
            i = wb_load([(0, [[512, 8], [1, 512]], wsrc(w_ada, l, 0, blk * 512, 8, 512))])
            spdma(BA, AP(b_ada.tensor, b_ada[l, blk * 512:blk * 512 + 1].offset, [[0, 2], [1, 512]]),
                  [], ["BA"], "ba", nbytes=4096)
            b = ps_get()
            for k in range(8):
                pe(lambda e, k=k, i=i, b=b: e.matmul(PSB[b][0:2, :], lhsT=SCV[:, k, :],
                                                     rhs=WB[i][:, k * 512:(k + 1) * 512],
                                                     start=(k == 0), stop=(k == 7)),
                   ["SCV", f"WB{i}"], pk(b))
            dve(lambda e, b=b: e.tensor_tensor(out=MS, in0=PSB[b][0:2, :], in1=BA, op=ALU.add),
                pk(b) + ["BA"], ["MS"])
            ps_put(b)
            spdma(modD[l, :, blk * 512:(blk + 1) * 512], MS, ["MS"], [f"modD{l}_{blk // 2}"], "ms", nbytes=4096)

        bg = [(0, blk) for blk in range(4, 12)] + [(1, blk) for blk in range(12)]

        def bg_step(n=1):
            for _ in range(n):
                if bg:
                    mod_block(*bg.pop(0))


        for blk in range(4):
            mod_block(0, blk)
        dve(lambda e: e.memset(ONES[:, :], 1.0), [], ["ONES"])
        fill_ms = dve(lambda e: e.memset(FILL[:, :], 0.5), [], ["FILL"])
        fill_ms.sig = True
        S.filler = (lambda e: e.matmul(PSB[7][:, :], lhsT=ONES[:, :], rhs=FILL[:, :], start=True, stop=True), fill_ms)
        dve(lambda e: e.memset(EPSB[:, 0:1], EPS), [], ["EPSB"])
        dve(lambda e: e.memset(EPSB[:, 1:2], 64 * EPS), [], ["EPSB"])
        for l in range(2):
            spdma(GQ[:, l, :], AP(q_g.tensor, q_g[l, 0:1].offset, [[0, 128], [1, 64]]), [], ["GQ"], "sm", nbytes=32768)
            spdma(GK8[:, l, :], AP(k_g.tensor, k_g[l, 0:1].offset, [[0, 128], [1, 64]]), [], ["GK8"], "sm", nbytes=32768)
            spdma(MG[:, l, :], AP(mlp_g.tensor, mlp_g[l, 0:1].offset, [[0, 128], [1, 256]]), [], ["MG"], "sm")
            spdma(MBt[:, l, :], AP(mlp_b.tensor, mlp_b[l, 0:1].offset, [[0, 128], [1, 256]]), [], ["MBt"], "sm")
        dve(lambda e: e.tensor_scalar(out=GK8[:, :, :], in0=GK8[:, :, :], scalar1=8.0, scalar2=None, op0=ALU.mult),
            ["GK8"], ["GK8"], n=128)

        def load_bd(l, d, gi, wt):
            for n in range(4):
                c, h = n // 2, n % 2
                pooldma(BD[h * 64:(h + 1) * 64, l, d, gi, c, h * 64:(h + 1) * 64],
                        wt[l, d, n, :, :], ["BD"], [f"BDx{l}"], f"bd{l}", issue=4000.0)

        def deferred_bd1():
            for d in range(2):
                for gi, wt in enumerate((lru_wa, lru_wx)):
                    load_bd(1, d, gi, wt)

        def deferred_cl():
            act(lambda e: e.activation(out=CL[:, :, :, :], in_=CL[:, :, :, :], func=AF.Exp, scale=-1.0), ["CL"], ["CL"],
                n=8, tbl="explog")
            act(lambda e: e.activation(out=CL[:, :, :, :], in_=CL[:, :, :, :], func=AF.Ln, bias=1.0), ["CL"], ["CL"],
                n=8, tbl="explog")
            dve(lambda e: e.tensor_scalar(out=CL[:, :, :, :], in0=CL[:, :, :, :], scalar1=-8.0, scalar2=None,
                                          op0=ALU.mult), ["CL"], ["CL"], n=8)

        def deferred_params():
            dve(lambda e: e.memset(BD[:, :, :, :, :, :].rearrange("p a b c d f -> p (a b c d f)"), 0.0), [], ["BD"],
                n=2048)
            spdma(ROPE[:, :, :], rope_d.rearrange("(t p) c -> p t c", p=128), [], ["ROPE"], "sm3", nbytes=8192)
            for l in range(2):
                for j_ in range(4):
                    spdma(CW[:, l, j_, :], conv_w[l, j_].rearrange("(c p) -> p c", p=128), [], ["CW"], "sm3", nbytes=8192)
                spdma(CB[:, l, :], conv_b[l].rearrange("(c p) -> p c", p=128), [], ["CB"], "sm3", nbytes=8192)
                for d in range(2):
                    for gi, (wt, bt) in enumerate(((lru_wa, lru_ba), (lru_wx, lru_bx))):
                        spdma(LBA[:, l, d, gi, :], bt[l, d].rearrange("(c p) -> p c", p=128), [], ["LBA"], "sm3", nbytes=8192)
                        if l == 0:
                            load_bd(l, d, gi, wt)
                    spdma(CL[:, l, d, :], lru_lam[l, d].rearrange("(c p) -> p c", p=128), [], ["CL"], "cl", nbytes=8192)
                    spdma(STI[:, l, d, :], st_d[l, d].rearrange("(c p) -> p c", p=128), [], ["STI"], "sm3", nbytes=8192)
                for g in range(4):
                    gi, c2 = g % 2, g // 2
                    spdma(BSB[gi * 64:(gi + 1) * 64, l, c2, :],
                          AP(mlp_bs.tensor, mlp_bs[l, g, 0:1].offset, [[0, 64], [1, 128]]), [], ["BSB"], "sm3", nbytes=8192)
                spdma(B1[:, l, :], b_ff1[l].rearrange("(c p) -> p c", p=128), [], ["B1"], "sm3", nbytes=8192)
                for j, key in enumerate(((1, 0), (1, 1), (2, 0), (2, 1))):
                    spdma(LNF[:, l, j, :], ln_gb[key][l].rearrange("(c p) -> p c", p=128), [], ["LNF"], "sm3", nbytes=8192)
            for l in range(2):
                i = wb_load([(0, [[128, 4], [1, 128]], mlp_ws[l].rearrange("g p q -> p g q"))])
                b = ps_get()
                pst = PSB[b][:, :].bitcast(BF16)
                for g in range(4):
                    pe(lambda e, g=g, i=i, pst=pst: e.transpose(out=pst[:, g * 128:(g + 1) * 128],
                                                                in_=WB[i][:, g * 128:(g + 1) * 128], identity=IDB[:, :]),
                       [f"WB{i}", "IDB"], pk(b), n=128)
                dve(lambda e, l=l, pst=pst: e.tensor_copy(out=WST[:, l, :, :].rearrange("p g q -> p (g q)"),
                                                           in_=pst[:, 0:512]), pk(b), ["WST"])
                ps_put(b)

        def load_msc(l, cond, half):
            lp = l % 2
            key = [f"MSC{lp}{half}"]
            for j, idx in (((0, 0), (1, 1)) if half == 0 else ((2, 3), (3, 4))):
                spdma(MSC[:, lp, j, :], modD[l, cond, idx * D:(idx + 1) * D].rearrange("(c p) -> p c", p=128),
                      [f"modD{l}_{idx}"], key, f"msc{lp}{half}", nbytes=4096)
            j = 1 if half == 0 else 3
            dve(lambda e: e.tensor_scalar(out=MSC[:, lp, j, :], in0=MSC[:, lp, j, :], scalar1=1.0,
                                          scalar2=None, op0=ALU.add), key, key, n=8)

        def bcast_row(dst, src_t, off_ap, key_r, key_w, sem):
            spdma(dst[:, :], AP(src_t.tensor, off_ap.offset, [[0, 128], [1, D]]), key_r, key_w, sem)

        def nmt_a1(t):
            xk = [f"X{t}"]
            xt = X[:, t, :]
            st, sk = stt_next()
            dve(lambda e: e.bn_stats(out=st[:, 0:6], in_=xt[:, 0:512]), xk, sk, n=450)
            dve(lambda e: e.bn_stats(out=st[:, 6:12], in_=xt[:, 512:1024]), xk, sk, n=450)
            dve(lambda e: e.bn_aggr(out=st[:, 12:14], in_=st[:, 0:12]), sk, sk, n=100)
            return st, sk

        def nmt_a2(t, stsk, final_store=None):
            st, sk = stsk
            xk = [f"X{t}"]
            xt = X[:, t, :]
            act(lambda e: e.activation(out=st[:, 14:15], in_=st[:, 13:14], func=AF.Sqrt, bias=EPSB[:, 0:1]),
                sk + ["EPSB"], sk, n=60, tbl="sqrt")
            dve(lambda e: e.reciprocal(out=st[:, 14:15], in_=st[:, 14:15]), sk, sk, n=80)
            if final_store is None:
                dve(lambda e: e.tensor_scalar(out=st[:, 15:16], in0=st[:, 12:13], scalar1=st[:, 14:15], scalar2=-1.0,
                                              op0=ALU.mult, op1=ALU.mult), sk, sk, n=8)
                act(lambda e: e.activation(out=XHB[:, :], in_=xt, func=AF.Identity, scale=st[:, 14:15],
                                           bias=st[:, 15:16]), xk + sk, ["XHB"], n=1024)
            dve(lambda e: e.scalar_tensor_tensor(out=xt, in0=xt, scalar=st[:, 12:13], in1=LNA[:, :],
                                                 op0=ALU.subtract, op1=ALU.mult), xk + sk + ["LNA"], xk, n=900)
            dve(lambda e: e.scalar_tensor_tensor(out=xt, in0=xt, scalar=st[:, 14:15], in1=LNB[:, :],
                                                 op0=ALU.mult, op1=ALU.add), xk + sk + ["LNB"], xk, n=900)
            if final_store is not None:
                spdma(final_store, xt, xk, [], "yst", nbytes=524288)

        def nmt_b(t, which):
            b = ps_get()
            pst = PSB[b][:, :].bitcast(BF16)
            for k in range(8):
                pe(lambda e, k=k, pst=pst: e.transpose(out=pst[:, k * 128:(k + 1) * 128],
                                                       in_=XHB[:, k * 128:(k + 1) * 128], identity=IDB[:, :]),
                   ["XHB", "IDB"], pk(b), n=128)
            for k in range(8):
                act(lambda e, k=k, pst=pst: e.activation(out=HT[:, k, t * 128:(t + 1) * 128],
                                                         in_=pst[:, k * 128:(k + 1) * 128], func=AF.Identity,
                                                         scale=GCB[:, which, 0, k:k + 1],
                                                         bias=GCB[:, which, 1, k:k + 1]),
                    pk(b) + [f"GCB{which}"], htk([k], [t]), n=128)
            ps_put(b)

        def nmt(t, l, which, norm, la_lb, final_store=None):
            xk = [f"X{t}"]
            xt = X[:, t, :]
            if norm:
                nmt_a2(t, nmt_a1(t), final_store)
            else:
                act(lambda e: e.copy(out=XHB[:, :], in_=xt), xk, ["XHB"], n=1024)
                dve(lambda e: e.tensor_scalar(out=xt, in0=xt, scalar1=ALPHA, scalar2=None, op0=ALU.mult), xk, xk, n=600)
            if final_store is None:
                nmt_b(t, which)

        def qk_norm1(src_ps, nh, bkeys, sq, sqk):
            n = nh * 64
            st, sk = stt_next()
            act(lambda e: e.activation(out=sq[:, 0:n], in_=src_ps, func=AF.Square), bkeys, sqk, n=n)
            dve(lambda e: e.tensor_reduce(out=st[:, 0:nh], in_=sq[:, 0:n].rearrange("p (h d) -> p h d", h=nh),
                                          axis=AX.X, op=ALU.add), sqk, sk, n=n)
            return st, sk

        def qk_norm2(stsk, src_ps, nh, dst_f, gtile, bkeys, dkeys, gkey, out_ap=None, okeys=None):
            st, sk = stsk
            n = nh * 64
            act(lambda e: e.activation(out=st[:, 0:nh], in_=st[:, 0:nh], func=AF.Sqrt, bias=EPSB[:, 1:2]),
                sk + ["EPSB"], sk, n=60, tbl="sqrt")
            dve(lambda e: e.reciprocal(out=st[:, 0:nh], in_=st[:, 0:nh]), sk, sk, n=80)
            dve(lambda e: e.tensor_tensor(out=dst_f.rearrange("p (h d) -> p h d", h=nh),
                                          in0=src_ps.rearrange("p (h d) -> p h d", h=nh),
                                          in1=ap_of(st[:, 0:1], 0, [[1, nh], [0, 64]]), op=ALU.mult),
                bkeys + sk, dkeys, n=n)
            if out_ap is None:
                d3 = dst_f.rearrange("p (h d) -> p h d", h=nh)
                g3 = ap_of(gtile[:, 0:1], 0, [[0, nh], [1, 64]])
                dve(lambda e: e.tensor_tensor(out=d3, in0=d3, in1=g3, op=ALU.mult), dkeys + [gkey], dkeys, n=n)
            else:
                o_, i0_, i1_ = out_ap
                dve(lambda e: e.tensor_tensor(out=o_, in0=i0_, in1=i1_, op=ALU.mult), dkeys + [gkey], okeys, n=n)

        def rope(src_f, nh, t, dst_b, dst_dims_even, skeys, dkeys):
            T = [TG[:, i * 256:i * 256 + nh * 32].rearrange("p (h i) -> p h i", h=nh) for i in range(4)]
            x1 = ap_of(src_f[:, 0:1], 0, [[64, nh], [2, 32]])
            x2 = ap_of(src_f[:, 0:1], 1, [[64, nh], [2, 32]])
            cs = ap_of(ROPE[:, t, 0:1], 0, [[0, nh], [1, 32]])
            sn = ap_of(ROPE[:, t, 0:1], 32, [[0, nh], [1, 32]])
            dve(lambda e: e.tensor_tensor(out=T[0], in0=x1, in1=cs, op=ALU.mult), skeys + ["ROPE"], ["TG"])
            dve(lambda e: e.tensor_tensor(out=T[1], in0=x2, in1=sn, op=ALU.mult), skeys + ["ROPE"], ["TG"])
            dve(lambda e: e.tensor_tensor(out=T[2], in0=x1, in1=sn, op=ALU.mult), skeys + ["ROPE"], ["TG"])
            dve(lambda e: e.tensor_tensor(out=T[3], in0=x2, in1=cs, op=ALU.mult), skeys + ["ROPE"], ["TG"])
            if nh == 8:
                Tv = [ap_of(TG[:, 0:1], i * 256, [[128, 2], [32, 4], [1, 32]]) for i in range(4)]
            else:
                Tv = T
            de = ap_of(dst_b, 0, dst_dims_even)
            do = ap_of(dst_b, 1, dst_dims_even)
            dve(lambda e: e.tensor_tensor(out=de, in0=Tv[0], in1=Tv[1], op=ALU.subtract), ["TG"], dkeys)
            dve(lambda e: e.tensor_tensor(out=do, in0=Tv[2], in1=Tv[3], op=ALU.add), ["TG"], dkeys)

        class _Stop(Exception):
            pass

        import os as _os2
        _stop = _os2.environ.get("K_STOP")

        def stop_at(name):
            if _stop == name:
                raise _Stop()

        def group_layer(grp, l, first, last):
            is_s = grp == "S"
            cond = 1 if is_s else 0
            lp = l % 2
            nseq = 1 if is_s else 4
            L = 1024 // nseq
            ktoff = 256 if is_s else 0
            vtoff = 2 if is_s else 0

            dve(lambda e: e.memset(ap_of(V2[:, 0, 0:1], 64, [[128, 20], [1, 64]]), 1.0), [],
                [f"V2_{tt}" for tt in range(10)], n=1280)
            if first:
                load_msc(l, cond, 0)
                if is_s:
                    for t in range(8):
                        spdma(X[:, t, :], xin[grp][t * 128:(t + 1) * 128, :], [], [f"X{t}"], f"xs{t}", nbytes=524288)
                dve(lambda e: e.tensor_copy(out=GCB[:, 0, 0, :], in_=MSC[:, lp, 1, :]), [f"MSC{lp}0"], ["GCB0"], n=8)
                dve(lambda e: e.tensor_copy(out=GCB[:, 0, 1, :], in_=MSC[:, lp, 0, :]), [f"MSC{lp}0"], ["GCB0"], n=8)
            if is_s:
                for tt in range(2):
                    spdma(KF[tt][:, :], ck_d[l, tt * 128:(tt + 1) * 128, :], [], [f"KF{tt}"], f"kst{tt}")
                    spdma(VF[tt][:, :], cv_d[l, tt * 128:(tt + 1) * 128, :], [], [f"VF{tt}"], f"vst{tt}")

            def late_setup():
                mk = [f"MSC{lp}1"]
                load_msc(l, cond, 1)
                bcast_row(G1, modD, modD[l, cond, 2 * D:2 * D + 1], [f"modD{l}_2"], ["G1"], "g1")
                dve(lambda e: e.tensor_tensor(out=GCB[:, 1, 0, :], in0=LNF[:, l, 0, :], in1=MSC[:, lp, 3, :],
                                              op=ALU.mult), ["LNF"] + mk, ["GCB1"], n=8)
                dve(lambda e: e.tensor_tensor(out=GCB[:, 1, 1, :], in0=LNF[:, l, 1, :], in1=MSC[:, lp, 3, :],
                                              op=ALU.mult), ["LNF"] + mk, ["GCB1"], n=8)
                dve(lambda e: e.tensor_tensor(out=GCB[:, 1, 1, :], in0=GCB[:, 1, 1, :], in1=MSC[:, lp, 2, :],
                                              op=ALU.add), ["GCB1"] + mk, ["GCB1"], n=8)
                bcast_row(G2, modD, modD[l, cond, 5 * D:5 * D + 1], [f"modD{l}_5"], ["G2"], "g2")
                bcast_row(LNA, ln_gb[(1, 0)], ln_gb[(1, 0)][l, 0:1], [], ["LNA"], "lna")
                bcast_row(LNB, ln_gb[(1, 1)], ln_gb[(1, 1)][l, 0:1], [], ["LNB"], "lnb")
                bcast_row(TG, b_ff2, b_ff2[l, 0:1], [], ["TG", "TGv0", "TGv1"], "lnc")
                dve(lambda e: e.tensor_scalar(out=LNA[:, :], in0=LNA[:, :], scalar1=ALPHA, scalar2=None, op0=ALU.mult),
                    ["LNA"], ["LNA"], n=1024)
                dve(lambda e: e.tensor_tensor(out=TG[:, :], in0=TG[:, :], in1=G2[:, :], op=ALU.mult), ["TG", "G2"],
                    ["TG"], n=1024)
                dve(lambda e: e.scalar_tensor_tensor(out=LNB[:, :], in0=LNB[:, :], scalar=ALPHA, in1=TG[:, :],
                                                     op0=ALU.mult, op1=ALU.add), ["LNB", "TG"], ["LNB"], n=1024)

            stop_at(f"{grp}{l}:start")
            S.phase = f"{grp}{l}:win"
            if first:
                for t in range(2):
                    nmt(t, l, 0, norm=False, la_lb=False)

            stop_at(f"{grp}{l}:h1")
            iA = wb_load([(0, [[512, 8], [1, 512]], wsrc(w_in, l, 0, 0, 8, 512))])
            iB = wb_load([(0, [[512, 8], [1, 256]], wsrc(w_in, l, 0, 512, 8, 256)),
                          (256, [[512, 8], [1, 256]], wsrc(w_in, l, 0, 1536, 8, 256))])
            iC = wb_load([(0, [[512, 8], [1, 512]], wsrc(w_in, l, 0, 768, 8, 512))])
            iD = wb_load([(0, [[512, 8], [1, 256]], wsrc(w_in, l, 0, 1280, 8, 256))])

            if is_s:
                for tt in range(2):
                    act(lambda e, tt=tt: e.copy(out=KB[:, :], in_=KF[tt][:, :]), [f"KF{tt}"], ["KB"])
                    b = ps_get()
                    pst = PSB[b][:, :].bitcast(BF16)
                    pe(lambda e, pst=pst: e.transpose(out=pst[:, 0:128], in_=KB[:, :], identity=IDB[:, :]),
                       ["KB", "IDB"], pk(b), n=128)
                    dve(lambda e, tt=tt, pst=pst: e.tensor_copy(out=KT[:, tt * 128:(tt + 1) * 128], in_=pst[:, 0:128]),
                        pk(b), [f"KT{tt}"])
                    ps_put(b)
                    dve(lambda e, tt=tt: e.tensor_copy(
                        out=ap_of(V2[:, tt, 0:1], 0, [[128, 2], [1, 64]]),
                        in_=VF[tt][:, :].rearrange("p (k d) -> p k d", k=2)), [f"VF{tt}"], [f"V2_{tt}"], n=128)

            pm_dims = [[64, 2], [128, 4], [1, 64]]
            nat_dims = [[256, 2], [64, 4], [1, 64]]
            qst = {}

            def q_a1(t):
                if first and t + 2 < 8:
                    nmt(t + 2, l, 0, norm=False, la_lb=False)
                b = ps_get()
                for k in range(8):
                    pe(lambda e, k=k, b=b, t=t: e.matmul(PSB[b][:, :], lhsT=HT[:, k, t * 128:(t + 1) * 128],
                                                         rhs=WB[iA][:, k * 512:(k + 1) * 512],
                                                         start=(k == 0), stop=(k == 7)),
                       htk([k], [t]) + [f"WB{iA}"], pk(b))
                qst[t] = (b, qk_norm1(PSB[b][:, :], 8, pk(b), FS[2], ["FS2"]))

            def q_a2(t):
                b, stsk = qst[t]
                qf = FS[t % 2]
                qfk = [f"FS{t % 2}"]
                QB = QBS[t % 2]
                qbk = [f"QB{t % 2}"]
                if is_s:
                    qk_norm2(stsk, PSB[b][:, :], 8, qf[:, :], GQ[:, l, :], pk(b), qfk, "GQ")
                    ps_put(b)
                    rope(qf[:, :], 8, t, QB[:, 0:1], [[64, 2], [128, 4], [2, 32]], qfk, qbk)
                else:
                    qk_norm2(stsk, PSB[b][:, :], 8, qf[:, :], GQ[:, l, :], pk(b), qfk, "GQ",
                             out_ap=(ap_of(QB[:, 0:1], 0, pm_dims), ap_of(qf[:, 0:1], 0, nat_dims),
                                     ap_of(GQ[:, l, 0:1], 0, [[0, 2], [0, 4], [1, 64]])), okeys=qbk)
                    ps_put(b)

            def q_b(t):
                QB = QBS[t % 2]
                qbk = [f"QB{t % 2}"]
                b2 = ps_get()
                pst = PSB[b2][:, :].bitcast(BF16)
                for j in range(4):
                    pe(lambda e, j=j, pst=pst, QB=QB: e.transpose(out=pst[:, j * 128:(j + 1) * 128],
                                                                  in_=QB[:, j * 128:(j + 1) * 128], identity=IDB[:, :]),
                       qbk + ["IDB"], pk(b2), n=128)
                act(lambda e, t=t, pst=pst: e.copy(out=QT[:, :, t * 128:(t + 1) * 128],
                                                   in_=pst[:, 0:512].rearrange("p (j q) -> p j q", j=4)),
                    pk(b2), qtk(range(4), [t]))
                ps_put(b2)

            kst = {}

            def kv_a1(t):
                b = ps_get()
                for k in range(8):
                    pe(lambda e, k=k, b=b, t=t: e.matmul(PSB[b][:, :], lhsT=HT[:, k, t * 128:(t + 1) * 128],
                                                         rhs=WB[iB][:, k * 512:(k + 1) * 512],
                                                         start=(k == 0), stop=(k == 7)),
                       htk([k], [t]) + [f"WB{iB}"], pk(b))
                sq = FS[2][:, (t % 2) * 128:(t % 2 + 1) * 128]
                stsk = qk_norm1(PSB[b][:, 0:128], 2, pk(b), sq, [f"FS2k{t % 2}", "FS2"])
                tv = vtoff + t
                if not is_s:
                    vf = VF[t % 2]
                    act(lambda e, vf=vf, b=b: e.copy(out=vf[:, :], in_=PSB[b][:, 128:256]), pk(b), [f"VF{t % 2}"], n=128)
                    s_, tt = t // 2, t % 2
                    spdma(nv_d[s_, l, tt * 128:(tt + 1) * 128, :], vf[:, :], [f"VF{t % 2}"], [], f"vst{t % 2}")
                dve(lambda e, tv=tv, b=b: e.tensor_copy(
                    out=ap_of(V2[:, tv, 0:1], 0, [[128, 2], [1, 64]]),
                    in_=PSB[b][:, 128:256].rearrange("p (k d) -> p k d", k=2)), pk(b), [f"V2_{tv}"], n=128)
                vg = TG[:, (t % 2) * 256:(t % 2 + 1) * 256] if not is_s else FS[t % 2][:, 0:256]
                vgk = [f"TGv{t % 2}"] if not is_s else [f"FS{t % 2}"]
                act(lambda e, b=b, vg=vg: e.activation(out=vg, in_=PSB[b][:, 256:512], func=AF.Gelu_apprx_tanh),
                    pk(b), vgk, n=256, tbl="gelu")
                st2, sk2 = stt_next()
                dve(lambda e, st2=st2, vg=vg: e.bn_stats(out=st2[:, 0:6], in_=vg), vgk, sk2, n=256)
                dve(lambda e, st2=st2: e.bn_aggr(out=st2[:, 12:14], in_=st2[:, 0:6]), sk2, sk2, n=60)
                kst[t] = (b, stsk, vg, vgk, st2, sk2)

            def kv_a2(t):
                b, stsk, vg, vgk, st2, sk2 = kst[t]
                kf = KF[t % 2]
                kfk = [f"KF{t % 2}"]
                qk_norm2(stsk, PSB[b][:, 0:128], 2, kf[:, :], GK8[:, l, :], pk(b), kfk, "GK8")
                ps_put(b)
                act(lambda e, st2=st2: e.activation(out=st2[:, 14:15], in_=st2[:, 13:14], func=AF.Sqrt,
                                                    bias=EPSB[:, 0:1]), sk2 + ["EPSB"], sk2, n=60, tbl="sqrt")
                dve(lambda e, st2=st2: e.reciprocal(out=st2[:, 14:15], in_=st2[:, 14:15]), sk2, sk2, n=80)
                dve(lambda e, st2=st2, vg=vg: e.tensor_scalar(out=vg, in0=vg, scalar1=st2[:, 12:13],
                                                              scalar2=st2[:, 14:15], op0=ALU.subtract, op1=ALU.mult),
                    vgk + sk2, vgk, n=256)
                dve(lambda e, vg=vg: e.tensor_tensor(out=vg, in0=vg, in1=MG[:, l, :], op=ALU.mult), vgk + ["MG"], vgk,
                    n=256)
                dve(lambda e, t=t, vg=vg: e.tensor_tensor(out=VN[:, t, :], in0=vg, in1=MBt[:, l, :], op=ALU.add),
                    vgk + ["MBt"], [f"VN{t}"], n=256)
                if is_s:
                    rope(kf[:, :], 2, t, KB[:, 0:1], [[64, 2], [2, 32]], kfk, ["KB"])

            def kv_b(t):
                kf = KF[t % 2]
                kfk = [f"KF{t % 2}"]
                if not is_s:
                    s_, tt = t // 2, t % 2
                    spdma(nk_d[s_, l, tt * 128:(tt + 1) * 128, :], kf[:, :], kfk, [], f"kst{t % 2}")
                    act(lambda e, kf=kf: e.copy(out=KB[:, :], in_=kf[:, :]), kfk, ["KB"], n=128)
                b2 = ps_get()
                pst = PSB[b2][:, :].bitcast(BF16)
                pe(lambda e, pst=pst: e.transpose(out=pst[:, 0:128], in_=KB[:, :], identity=IDB[:, :]),
                   ["KB", "IDB"], pk(b2), n=128)
                kc = ktoff + t * 128
                dve(lambda e, kc=kc, pst=pst: e.tensor_copy(out=KT[:, kc:kc + 128], in_=pst[:, 0:128]),
                    pk(b2), [f"KT{kc // 128}"], n=128)
                ps_put(b2)

            def skewed(a1, a2, bb):
                for step in range(8 + 2):
                    if step < 8:
                        a1(step)
                    if 0 <= step - 2 < 8:
                        bb(step - 2)
                    if 0 <= step - 1 < 8:
                        a2(step - 1)

            skewed(q_a1, q_a2, q_b)
            stop_at(f"{grp}{l}:q")
            skewed(kv_a1, kv_a2, kv_b)
            stop_at(f"{grp}{l}:kv")

            for (slot, ncol, j) in [(iC, 512, 0), (iC, 512, 1), (iC, 512, 2), (iC, 512, 3), (iD, 256, 0), (iD, 256, 1)]:
                for tg in range(2):
                    b = ps_get()
                    for k in range(8):
                        pe(lambda e, k=k, b=b, tg=tg, slot=slot, ncol=ncol, j=j: e.matmul(
                            PSB[b][:, :], lhsT=WB[slot][:, k * 512 + j * 128:k * 512 + (j + 1) * 128],
                            rhs=HT[:, k, tg * 512:(tg + 1) * 512], start=(k == 0), stop=(k == 7)),
                           htk([k], range(tg * 4, tg * 4 + 4)) + [f"WB{slot}"], pk(b))
                    cols = slice(tg * 512, (tg + 1) * 512)
                    if slot == iC and j < 2:
                        act(lambda e, b=b, j=j, cols=cols: e.copy(out=XR[:, j, cols], in_=PSB[b][:, :]),
                            pk(b), [f"XR{j}"])
                    elif slot == iC:
                        act(lambda e, b=b, j=j, cols=cols: e.activation(out=GG[:, j - 2, cols], in_=PSB[b][:, :],
                                                                        func=AF.Gelu_apprx_tanh),
                            pk(b), [f"GG{j - 2}"], tbl="gelu")
                    else:
                        act(lambda e, b=b, j=j, cols=cols: e.activation(out=UT[:, j, cols], in_=PSB[b][:, :],
                                                                        func=AF.Gelu_apprx_tanh),
                            pk(b), [f"UT{j}"], tbl="gelu")
                    ps_put(b)


            if grp == "P" and l == 0:
                S.phase = "P0:defer"
                deferred_params()
            stop_at(f"{grp}{l}:att")
            S.phase = f"{grp}{l}:att"
            npt = 0
            nunit = 0
            npair = [0]
            if is_s:
                for b_ in (0, 1, 2, 3):
                    ps_free.remove(b_)
            for u in range(nseq if not is_s else 2):
                if is_s:
                    q0, N = u * 512, 512
                    tts = list(range(10))
                else:
                    q0, N = u * 256, 256
                    tts = [2 * u, 2 * u + 1]
                qts = list(range(q0 // 128, (q0 + N) // 128))
                for j in range(4):
                    for a in range(2):
                        h = j + 4 * a
                        rows = slice(a * 64, a * 64 + 64)
                        bo = ps_get()
                        if is_s:
                            for pi in range(5):
                                kp = npair[0] % 2
                                npair[0] += 1
                                b0_, b1_ = 2 * kp, 2 * kp + 1
                                for hb_, tt in ((b0_, 2 * pi), (b1_, 2 * pi + 1)):
                                    kc = tt * 128
                                    pe(lambda e, hb_=hb_, kc=kc, rows=rows, j=j, q0=q0, N=N: e.matmul(
                                        PSB[hb_][:, 0:N], lhsT=KT[rows, kc:kc + 128], rhs=QT[rows, j, q0:q0 + N],
                                        start=True, stop=True),
                                       [f"KT{kc // 128}"] + qtk([j], qts), pk(hb_), n=N)
                                pt = PT2[npt % 2]
                                ptk = [f"PT{npt % 2}_{k_}" for k_ in range(4)]
                                npt += 1
                                act(lambda e, kp=kp, pt=pt: e.activation(out=pt[:, :], in_=PS2[kp][:, :], func=AF.Exp),
                                    pk(b0_) + pk(b1_), ptk, n=850, tbl="explog")
                                for hi_, tt in ((0, 2 * pi), (1, 2 * pi + 1)):
                                    pe(lambda e, bo=bo, tt=tt, a=a, pt=pt, hi_=hi_: e.matmul(
                                        PSB[bo][:, 0:512], lhsT=V2[:, tt, a * 128:(a + 1) * 128],
                                        rhs=pt[:, hi_ * 512:(hi_ + 1) * 512],
                                        start=(tt == 0), stop=(tt == 9)), [f"V2_{tt}"] + ptk, pk(bo), n=512)
                        else:
                            for ti, tt in enumerate(tts):
                                bs_ = ps_get()
                                kc = tt * 128
                                pe(lambda e, bs_=bs_, kc=kc, rows=rows, j=j, q0=q0, N=N: e.matmul(
                                    PSB[bs_][:, 0:N], lhsT=KT[rows, kc:kc + 128], rhs=QT[rows, j, q0:q0 + N],
                                    start=True, stop=True),
                                   [f"KT{kc // 128}"] + qtk([j], qts), pk(bs_), n=N)
                                pt = PT2[(npt // 4) % 2][:, (npt % 4) * 256:(npt % 4 + 1) * 256]
                                ptk = [f"PT{(npt // 4) % 2}_{npt % 4}"]
                                npt += 1
                                act(lambda e, bs_=bs_, pt=pt, N=N: e.activation(out=pt[:, 0:N], in_=PSB[bs_][:, 0:N],
                                                                                func=AF.Exp), pk(bs_), ptk, n=N,
                                    tbl="explog")
                                ps_put(bs_)
                                pe(lambda e, bo=bo, tt=tt, a=a, pt=pt, N=N, ti=ti, nt=len(tts): e.matmul(
                                    PSB[bo][:, 0:N], lhsT=V2[:, tt, a * 128:(a + 1) * 128], rhs=pt[:, 0:N],
                                    start=(ti == 0), stop=(ti == nt - 1)), [f"V2_{tt}"] + ptk, pk(bo), n=N)
                        pr = slice((h % 2) * 64, (h % 2) * 64 + 64)
                        rz = FS[h % 2]
                        rzk = [f"FS{h % 2}"]
                        if is_s or nunit % 3 == 2:
                            dve(lambda e, bo=bo, rz=rz, N=N: e.reciprocal(out=rz[0:64, 0:N], in_=PSB[bo][64:128, 0:N]),
                                pk(bo), rzk, n=int(5.0 * N))
                        else:
                            act(lambda e, bo=bo, rz=rz, N=N: e.activation(out=rz[0:64, 0:N], in_=PSB[bo][64:128, 0:N],
                                                                          func=AF.Ln), pk(bo), rzk, n=N, tbl="explog")
                            act(lambda e, rz=rz, N=N: e.activation(out=rz[0:64, 0:N], in_=rz[0:64, 0:N], func=AF.Exp,
                                                                   scale=-1.0), rzk, rzk, n=N, tbl="explog")
                        dve(lambda e, bo=bo, rz=rz, pr=pr, N=N, h=h, q0=q0: e.tensor_tensor(
                            out=MH[pr, h // 2, q0:q0 + N], in0=PSB[bo][0:64, 0:N], in1=rz[0:64, 0:N], op=ALU.mult),
                            pk(bo) + rzk, mhk([h // 2], qts), n=N)
                        ps_put(bo)
                        nunit += 1

            if is_s:
                ps_free.extend([0, 1, 2, 3])
            stop_at(f"{grp}{l}:lru")
            S.phase = f"{grp}{l}:lru"
            if grp == "P" and l == 0:
                deferred_cl()
            def lru_chunk(c):
                xr = XR[:, c, :]
                xc = xc_ap(c)
                xck = xc_key(c)
                xrk = [f"XR{c}"]
                w_ = lambda j: CW[:, l, j, c:c + 1]
                xr3 = xr.rearrange("p (s t) -> p s t", s=nseq)
                xc3 = xc.rearrange("p (s t) -> p s t", s=nseq)
                dve(lambda e: e.tensor_scalar(out=xc, in0=xr, scalar1=w_(1), scalar2=CB[:, l, c:c + 1],
                                              op0=ALU.mult, op1=ALU.add), xrk + ["CW", "CB"], xck)
                dve(lambda e: e.scalar_tensor_tensor(out=xc3[:, :, 1:L], in0=xr3[:, :, 0:L - 1], scalar=w_(0),
                                                     in1=xc3[:, :, 1:L], op0=ALU.mult, op1=ALU.add),
                    xrk + xck + ["CW"], xck)
                dve(lambda e: e.scalar_tensor_tensor(out=xc3[:, :, 0:L - 1], in0=xr3[:, :, 1:L], scalar=w_(2),
                                                     in1=xc3[:, :, 0:L - 1], op0=ALU.mult, op1=ALU.add),
                    xrk + xck + ["CW"], xck)
                dve(lambda e: e.scalar_tensor_tensor(out=xc3[:, :, 0:L - 2], in0=xr3[:, :, 2:L], scalar=w_(3),
                                                     in1=xc3[:, :, 0:L - 2], op0=ALU.mult, op1=ALU.add),
                    xrk + xck + ["CW"], xck)
                xcb = xcb_ap(c)
                xcbk = xcb_key(c)
                act(lambda e: e.copy(out=xcb, in_=xc), xck, xcbk)
                def lru_dir(d):
                    for gi, (dst, dkey) in enumerate(((LA_, LKEY[0]), (LI_, LKEY[1]))):
                        for tg in range(2):
                            b = ps_get()
                            pe(lambda e, b=b, gi=gi, tg=tg: e.matmul(
                                PSB[b][:, :], lhsT=BD[:, l, d, gi, c, :], rhs=xcb[:, tg * 512:(tg + 1) * 512],
                                start=True, stop=True), [f"BDx{l}"] + xcbk, pk(b))
                            act(lambda e, b=b, gi=gi, tg=tg, dst=dst: e.activation(
                                out=dst[:, tg * 512:(tg + 1) * 512], in_=PSB[b][:, :], func=AF.Sigmoid,
                                bias=LBA[:, l, d, gi, c:c + 1]), pk(b) + ["LBA"], dkey, tbl="sig")
                            ps_put(b)
                    act(lambda e: e.activation(out=LA_, in_=LA_, func=AF.Exp, scale=CL[:, l, d, c:c + 1]),
                        LKEY[0] + ["CL"], LKEY[0], n=1024, tbl="explog")
                    dve(lambda e: e.tensor_tensor(out=LT_, in0=LA_, in1=LA_, op=ALU.mult), LKEY[0], LKEY[2], n=1024)
                    act(lambda e: e.activation(out=LT_, in_=LT_, func=AF.Ln, scale=-1.0, bias=1.0), LKEY[2], LKEY[2],
                        n=1024, tbl="explog")
                    act(lambda e: e.activation(out=LT_, in_=LT_, func=AF.Exp, scale=0.5), LKEY[2], LKEY[2],
                        n=1024, tbl="explog")
                    dve(lambda e: e.tensor_tensor(out=LI_, in0=LI_, in1=LT_, op=ALU.mult), LKEY[1] + LKEY[2], LKEY[1],
                        n=1024)
                    dve(lambda e: e.tensor_tensor(out=LI_, in0=LI_, in1=xc, op=ALU.mult), LKEY[1] + xck, LKEY[1], n=1024)
                    hdst = xr if d == 0 else LH_
                    hkey = xrk if d == 0 else LKEY[3]
                    for s_ in range(nseq):
                        if d == 0:
                            sl = lambda tns: tns[:, s_ * L:(s_ + 1) * L]
                        else:
                            def sl(tns, s_=s_):
                                base = tns[:, (s_ + 1) * L - 1:(s_ + 1) * L]
                                return AP(base.tensor, base.offset, [list(base.ap[0]), [-1, L]])
                        init = STI[:, l, d, c:c + 1] if is_s else 0.0
                        o_, a_, u_ = sl(hdst), sl(LA_), sl(LI_)
                        dve(lambda e, o_=o_, a_=a_, u_=u_, init=init: e.tensor_tensor_scan(
                            out=o_, data0=a_, data1=u_, initial=init, op0=ALU.mult, op1=ALU.add),
                            LKEY[0] + LKEY[1] + ["STI"], hkey)
                    if not is_s:
                        fin = (L - 1) if d == 0 else 0
                        dve(lambda e, hdst=hdst, fin=fin, d=d: e.tensor_copy(
                            out=NSS[:, d, c, :], in_=ap_of(hdst[:, 0:1], fin, [[L, 4]])), hkey, ["NSS"])
                for d in range(2):
                    bg_step(2)
                    lru_dir(d)
                dve(lambda e: e.tensor_tensor(out=xr, in0=xr, in1=LH_, op=ALU.add), xrk + LKEY[3], xrk)
                dve(lambda e: e.tensor_tensor(out=MH[:, 4 + c, :], in0=xr, in1=GG[:, c, :], op=ALU.mult),
                    xrk + [f"GG{c}"], mhk([4 + c], ALL8))
            for c in range(2):
                lru_chunk(c)
            while bg and bg[0][0] <= l:
                bg_step()
            late_setup()
            iE = wb_load([(0, [[512, 8], [1, 512]], wsrc(w_out, l, 0, 0, 8, 512))])
            iF = wb_load([(0, [[512, 8], [1, 512]], wsrc(w_out, l, 0, 512, 8, 512))])
            if not is_s:
                for s_ in range(4):
                    for d in range(2):
                        spdma(ns_d[s_, l, d, :].rearrange("(c p) -> p c", p=128), NSS[:, d, :, s_], ["NSS"], [], "nst")

            stop_at(f"{grp}{l}:mlp")
            S.phase = f"{grp}{l}:mlp"
            def gmlp_mix(c2, gi):
                if True:
                    g = 2 * c2 + gi
                    b0, b1 = ps_get(), ps_get()
                    for t in range(8):
                        bb = b0 if t < 4 else b1
                        pe(lambda e, bb=bb, t=t, g=g: e.matmul(
                            PSB[bb][:, (t % 4) * 128:(t % 4 + 1) * 128], lhsT=VN[:, t, c2 * 128:(c2 + 1) * 128],
                            rhs=WST[:, l, g, :], start=True, stop=True), [f"VN{t}", "WST"], pk(bb))
                    pr = slice(gi * 64, gi * 64 + 64)
                    for hh, bb in enumerate((b0, b1)):
                        cols = slice(hh * 512, (hh + 1) * 512)
                        dve(lambda e, bb=bb, pr=pr, cols=cols: e.tensor_tensor(
                            out=TG[pr, cols].rearrange("p (t q) -> p t q", t=4),
                            in0=PSB[bb][pr, :].rearrange("p (t q) -> p t q", t=4),
                            in1=ap_of(BSB[pr, l, c2, 0:1], 0, [[0, 4], [1, 128]]), op=ALU.add),
                            pk(bb) + ["BSB"], ["TG"])
                        dve(lambda e, pr=pr, cols=cols: e.tensor_tensor(
                            out=MH[pr, 6 + c2, cols], in0=TG[pr, cols], in1=UT[pr, c2, cols], op=ALU.mult),
                            ["TG", f"UT{c2}"], mhk([6 + c2], range(hh * 4, hh * 4 + 4)))
                    ps_put(b0)
                    ps_put(b1)

            for c2 in range(2):
                for gi in range(2):
                    bg_step(1)
                    gmlp_mix(c2, gi)

            stop_at(f"{grp}{l}:wout")
            S.phase = f"{grp}{l}:wout"
            wst_ = {}

            def wo_a1(t):
                for hf, slot in enumerate((iE, iF)):
                    b = ps_get()
                    for k in range(8):
                        pe(lambda e, k=k, b=b, t=t, slot=slot: e.matmul(
                            PSB[b][:, :], lhsT=MH[:, k, t * 128:(t + 1) * 128], rhs=WB[slot][:, k * 512:(k + 1) * 512],
                            start=(k == 0), stop=(k == 7)), mhk([k], [t]) + [f"WB{slot}"], pk(b))
                    cols = slice(hf * 512, (hf + 1) * 512)
                    tmp = FS[(2 * t + hf) % 3]
                    tk = [f"FS{(2 * t + hf) % 3}"]
                    dve(lambda e, b=b, cols=cols, tmp=tmp: e.tensor_tensor(out=tmp[:, :], in0=PSB[b][:, :],
                                                                           in1=G1[:, cols], op=ALU.mult),
                        pk(b) + ["G1"], tk)
                    ps_put(b)
                    dve(lambda e, t=t, cols=cols, tmp=tmp: e.tensor_tensor(out=X[:, t, cols], in0=X[:, t, cols],
                                                                           in1=tmp[:, :], op=ALU.add),
                        tk + [f"X{t}"], [f"X{t}"])
                wst_[t] = nmt_a1(t)

            skewed(wo_a1, lambda t: nmt_a2(t, wst_[t]), lambda t: nmt_b(t, 1))

            if not last:
                load_msc(l + 1, cond, 0)
                nlp = (l + 1) % 2
                dve(lambda e: e.tensor_tensor(out=GCB[:, 0, 0, :], in0=LNF[:, l, 2, :], in1=MSC[:, nlp, 1, :],
                                              op=ALU.mult), ["LNF", f"MSC{nlp}0"], ["GCB0"], n=8)
                dve(lambda e: e.tensor_tensor(out=GCB[:, 0, 1, :], in0=LNF[:, l, 3, :], in1=MSC[:, nlp, 1, :],
                                              op=ALU.mult), ["LNF", f"MSC{nlp}0"], ["GCB0"], n=8)
                dve(lambda e: e.tensor_tensor(out=GCB[:, 0, 1, :], in0=GCB[:, 0, 1, :], in1=MSC[:, nlp, 0, :],
                                              op=ALU.add), ["GCB0", f"MSC{nlp}0"], ["GCB0"], n=8)
            bcast_row(LNA, ln_gb[(2, 0)], ln_gb[(2, 0)][l, 0:1], [], ["LNA"], "lna")
            bcast_row(LNB, ln_gb[(2, 1)], ln_gb[(2, 1)][l, 0:1], [], ["LNB"], "lnb")
            if not last:
                dve(lambda e: e.tensor_scalar(out=LNA[:, :], in0=LNA[:, :], scalar1=ALPHA, scalar2=None,
                                              op0=ALU.mult), ["LNA"], ["LNA"])
                dve(lambda e: e.tensor_scalar(out=LNB[:, :], in0=LNB[:, :], scalar1=ALPHA, scalar2=None,
                                              op0=ALU.mult), ["LNB"], ["LNB"])

            stop_at(f"{grp}{l}:ffn")
            S.phase = f"{grp}{l}:ffn"
            if grp == "P" and l == 0:
                deferred_bd1()
            nrl = 0
            for qd in range(4):
                i1 = [wb_load([(0, [[512, 8], [1, 512]], wsrc(w_ff1, l, 0, qd * 1024 + hb * 512, 8, 512))])
                      for hb in range(2)]
                i2 = [wb_load([(0, [[1024, 4], [1, 1024]], wsrc(w_ff2, l, qd * 1024 + hb * 512, 0, 4, 1024))])
                      for hb in range(2)]
                for tg in range(2):
                    for hc in range(8):
                        slot = i1[hc // 4]
                        cc = (hc % 4) * 128
                        b = ps_get()
                        for k in range(8):
                            pe(lambda e, k=k, b=b, tg=tg, slot=slot, cc=cc: e.matmul(
                                PSB[b][:, :], lhsT=WB[slot][:, k * 512 + cc:k * 512 + cc + 128],
                                rhs=HT[:, k, tg * 512:(tg + 1) * 512], start=(k == 0), stop=(k == 7)),
                               htk([k], range(tg * 4, tg * 4 + 4)) + [f"WB{slot}"], pk(b))
                        rl = FS[nrl % 3]
                        rk = [f"FS{nrl % 3}"]
                        nrl += 1
                        chunk = qd * 8 + hc
                        act(lambda e, b=b, rl=rl, chunk=chunk: e.activation(out=rl[:, :], in_=PSB[b][:, :], func=AF.Relu,
                                                                            bias=B1[:, l, chunk:chunk + 1]),
                            pk(b) + ["B1"], rk)
                        ps_put(b)
                        act(lambda e, rl=rl, hc=hc, tg=tg: e.activation(out=MH[:, hc, tg * 512:(tg + 1) * 512],
                                                                        in_=rl[:, :], func=AF.Square),
                            rk, mhk([hc], range(tg * 4, tg * 4 + 4)))
                for tg in range(2):
                    for t in range(tg * 4, tg * 4 + 4):
                        for hf in range(2):
                            b = ps_get()
                            for hc in range(8):
                                slot = i2[hc // 4]
                                pe(lambda e, hc=hc, b=b, t=t, slot=slot, hf=hf: e.matmul(
                                    PSB[b][:, :], lhsT=MH[:, hc, t * 128:(t + 1) * 128],
                                    rhs=WB[slot][:, (hc % 4) * 1024 + hf * 512:(hc % 4) * 1024 + (hf + 1) * 512],
                                    start=(hc == 0), stop=(hc == 7)), mhk([hc], [t]) + [f"WB{slot}"], pk(b))
                            cols = slice(hf * 512, (hf + 1) * 512)
                            tmp = FS[nrl % 3]
                            tk = [f"FS{nrl % 3}"]
                            nrl += 1
                            dve(lambda e, b=b, cols=cols, tmp=tmp: e.tensor_tensor(out=tmp[:, :], in0=PSB[b][:, :],
                                                                                   in1=G2[:, cols], op=ALU.mult),
                                pk(b) + ["G2"], tk)
                            ps_put(b)
                            dve(lambda e, t=t, cols=cols, tmp=tmp: e.tensor_tensor(out=X[:, t, cols], in0=X[:, t, cols],
                                                                                   in1=tmp[:, :], op=ALU.add),
                                tk + [f"X{t}"], [f"X{t}"])
                        if qd == 3:
                            S.phase = f"{grp}{l}:ln2"
                            if last:
                                nmt(t, l, 0, norm=True, la_lb=True, final_store=yout[grp][t * 128:(t + 1) * 128, :])
                            else:
                                nmt(t, l, 0, norm=True, la_lb=True)
                            S.phase = f"{grp}{l}:ffn"

        try:
            for grp in ("P", "S"):
                for l in range(2):
                    group_layer(grp, l, first=(l == 0), last=(l == 1))
        except _Stop:
            pass

        final_sems = ["yst", "nst", "kst0", "kst1", "vst0", "vst1"]
        import os as _os
        S.schedule()
        if _os.environ.get("KDEBUG"):
            print("ops", {e: len(v) for e, v in S.ops.items()}, "sim_us", S.sim_time / 1e3,
                  "sbuf_left", nc.sbuf_bytes_remaining)
        S.finalize()
        S.check()
        if _os.environ.get("KDUMP"):
            for i, op in enumerate(S.ops[_os.environ["KDUMP"]]):
                print(i, "lidx", op.lidx, "tbl", op.tbl, "cost", int(op.cost), "sig", op.sigidx if op.sig else None,
                      "deps", [(d.eng, d.sigidx, d.lidx) for d in op.deps], "dw", op.dwaits)

        dma_names = sorted(S.dma_counts.keys())
        sems = {}
        for n in list(dma_names) + ["e_pe", "e_act", "e_dve", "e_pool", "e_sp"]:
            sems[n] = es.enter_context(nc.semaphore(n))
        esem = {e: sems["e_" + e] for e in Sched.ENGS}
        dsem = {n: sems[n] for n in dma_names}

        with nc.Block() as block:
            @block.tensor
            def _(e):
                S.emit("pe", e, esem, dsem)

            @block.scalar
            def _(e):
                S.emit("act", e, esem, dsem)

            @block.vector
            def _(e):
                S.emit("dve", e, esem, dsem)

            @block.gpsimd
            def _(e):
                S.emit("pool", e, esem, dsem)

            @block.sync
            def _(e):
                S.emit("sp", e, esem, dsem)
                for n in final_sems:
                    if n in dsem:
                        e.wait_ge(dsem[n], S.dma_counts[n] * 16)
    return nc


_NC_CACHE = {}


def _rope_table():
    pos = np.arange(1024)
    pr = (pos // 64).astype(np.float32)
    pc = (pos % 64).astype(np.float32)
    inv = (10000.0 ** (-np.arange(16, dtype=np.float32) / 16)).astype(np.float32)
    ang = np.concatenate([pr[:, None] * inv, pc[:, None] * inv], -1).astype(np.float32)
    return np.concatenate([np.cos(ang), np.sin(ang)], -1).astype(np.float32)


def kernel(x_prompt, x_sample, c, cache_k, cache_v, state_lru, c_ctx, w_ada, b_ada, w_in,
           q_norm_g, k_norm_g, conv_w, conv_b, lru_wa, lru_ba, lru_wx, lru_bx, lru_lam,
           mlp_norm_g, mlp_norm_b, mlp_ws, mlp_bs, w_out, ln1_g, ln1_b, w_ff1, b_ff1,
           w_ff2, b_ff2, ln2_g, ln2_b):
    f = lambda a: np.ascontiguousarray(np.asarray(a, dtype=np.float32))
    if "nc" not in _NC_CACHE:
        _NC_CACHE["nc"] = build_nc()
    nc = _NC_CACHE["nc"]
    shared = dict(w_ada=f(w_ada), b_ada=f(b_ada), w_in=f(w_in), q_norm_g=f(q_norm_g), k_norm_g=f(k_norm_g),
                  conv_w=f(conv_w), conv_b=f(conv_b), lru_wa=f(lru_wa), lru_ba=f(lru_ba), lru_wx=f(lru_wx),
                  lru_bx=f(lru_bx), lru_lam=f(lru_lam), mlp_norm_g=f(mlp_norm_g), mlp_norm_b=f(mlp_norm_b),
                  mlp_ws=f(mlp_ws), mlp_bs=f(mlp_bs), w_out=f(w_out), ln1_g=f(ln1_g), ln1_b=f(ln1_b),
                  w_ff1=f(w_ff1), b_ff1=f(b_ff1), w_ff2=f(w_ff2), b_ff2=f(b_ff2), ln2_g=f(ln2_g), ln2_b=f(ln2_b),
                  ident=np.eye(128, dtype=np.float32), rope=_rope_table())
    x_prompt, x_sample, c, c_ctx = f(x_prompt), f(x_sample), f(c), f(c_ctx)
    cache_k, cache_v, state_lru = f(cache_k), f(cache_v), f(state_lru)
    in_maps = []
    for i in range(8):
        m = dict(shared)
        m["xp"] = np.ascontiguousarray(x_prompt[4 * i:4 * i + 4].reshape(1024, D))
        m["xs"] = np.ascontiguousarray(x_sample[i])
        m["cvec"] = np.ascontiguousarray(np.stack([c_ctx, c[i]], 0))
        m["ck"] = np.ascontiguousarray(cache_k[i].reshape(2, 256, 128))
        m["cv"] = np.ascontiguousarray(cache_v[i].reshape(2, 256, 128))
        m["st"] = np.ascontiguousarray(state_lru[i])
        in_maps.append(m)
    res = run_bass_kernel_spmd(nc, in_maps, core_ids=list(range(8)))
    R = res.results
    y_prompt = np.concatenate([r["yp"].reshape(4, 256, D) for r in R], 0)
    y_sample = np.stack([r["ys"] for r in R], 0)
    nk = np.concatenate([r["nk"].reshape(4, 2, 256, 2, 64) for r in R], 0)
    nv = np.concatenate([r["nv"].reshape(4, 2, 256, 2, 64) for r in R], 0)
    ns = np.concatenate([r["ns"] for r in R], 0)
    return (y_prompt.astype(np.float32), y_sample.astype(np.float32), nk.astype(np.float32),
            nv.astype(np.float32), ns.astype(np.float32))
```

```python
import contextlib
import numpy as np
import concourse.bass as bass
import concourse.mybir as mybir
from concourse.bass_utils import run_bass_kernel_spmd
from concourse.ap import AP

F32 = mybir.dt.float32
BF16 = mybir.dt.bfloat16
AF = mybir.ActivationFunctionType
ALU = mybir.AluOpType
AX = mybir.AxisListType

D = 1024
ALPHA = 4.0 ** 0.25
EPS = 1e-6
NWB = 6


class Op:
    __slots__ = ("eng", "fn", "deps", "odeps", "ddeps", "dwaits", "sig", "sigidx", "dma", "dma_cnt",
                 "pos", "cost", "users", "nrem", "ready", "fin", "lidx", "tbl", "phase", "issue", "rdma")

    def __init__(self, eng, fn, dma, cost):
        self.eng, self.fn, self.dma, self.cost = eng, fn, dma, cost
        self.deps = []
        self.odeps = []
        self.ddeps = []
        self.dwaits = []
        self.sig = False
        self.sigidx = 0
        self.dma_cnt = 0
        self.pos = 0
        self.users = []
        self.nrem = 0
        self.ready = 0.0
        self.fin = 0.0
        self.lidx = 0
        self.tbl = None
        self.issue = None
        self.rdma = False


class Sched:
    ENGS = ("pe", "act", "dve", "pool", "sp")
    import os as _os4
    OOO = tuple(_os4.environ.get("K_OOO", "pe,dve").split(","))
    import os as _os3
    WINDOW = int(_os3.environ.get("K_WIN", "48"))
    MAXFILL = int(_os3.environ.get("K_MAXFILL", "12000"))
    FILLFRAC = float(_os3.environ.get("K_FILLFRAC", "0.95"))
    FILLCAP = int(_os3.environ.get("K_FILLCAP", "100"))
    FILLGAP = float(_os3.environ.get("K_FILLGAP", "250"))

    def __init__(self):
        self.ops = {e: [] for e in self.ENGS}
        self.lastw = {}
        self.readers = {}
        self.dma_counts = {}
        self.dma_ops = {}
        self.filler = None
        self.nfill = 0
        self.nops = 0
        self.phase = "pro"

    def add(self, eng, fn, r=(), w=(), dma=None, cost=300.0, tbl=None):
        op = Op(eng, fn, dma, cost)
        op.tbl = tbl
        op.phase = self.phase
        op.lidx = self.nops
        self.nops += 1
        r = list(r)
        w = list(w)
        deps = {}

        def consider(d):
            if d is None or d is op:
                return
            deps[id(d)] = d

        for k in r:
            consider(self.lastw.get(k))
        for k in w:
            consider(self.lastw.get(k))
            for d in self.readers.get(k, ()):
                consider(d)
        for d in deps.values():
            if d.dma is not None:
                op.dwaits.append((d.dma, self.dma_counts[d.dma] * 16))
                op.ddeps.append(d)
                lastd = self.dma_ops[d.dma][-1]
                if lastd is not d and lastd is not op:
                    op.ddeps.append(lastd)
            elif d.eng == eng and eng == "pe" and op.dma is None:
                op.odeps.append(d)
            else:
                d.sig = True
                op.deps.append(d)
        for k in r:
            self.readers.setdefault(k, []).append(op)
        for k in w:
            self.lastw[k] = op
            self.readers[k] = []
        if dma is not None:
            self.dma_counts[dma] = self.dma_counts.get(dma, 0) + 1
            op.dma_cnt = self.dma_counts[dma]
            self.dma_ops.setdefault(dma, []).append(op)
        self.ops[eng].append(op)
        return op

    def schedule(self):
        allops = [op for e in self.ENGS for op in self.ops[e]]
        for op in allops:
            op.users = []
        for op in allops:
            ds = op.deps + op.odeps + op.ddeps
            op.nrem = len(ds)
            op.ready = 0.0
            for d in ds:
                d.users.append(op)
        pend = {e: list(self.ops[e]) for e in self.ENGS}
        new = {e: [] for e in self.ENGS}
        free = {e: 0.0 for e in self.ENGS}
        dma_free = [0.0]
        cur_tbl = [None]
        total = len(allops)
        done = 0
        while done < total:
            best = None
            for e in self.ENGS:
                q = pend[e]
                if not q:
                    continue
                cand = None
                if e in self.OOO:
                    lim = min(len(q), self.WINDOW)
                    if e == "act" and q[0].phase == "pro":
                        lim = 1
                    ft = free[e]
                    for i in range(lim):
                        op = q[i]
                        if op.nrem:
                            continue
                        st = op.ready if op.ready > ft else ft
                        if e == "act" and op.tbl is not None and op.tbl != cur_tbl[0]:
                            st += 1300.0
                        if cand is None or st < cand[0]:
                            cand = (st, i, op)
                            if st <= ft:
                                break
                else:
                    op = q[0]
                    if op.nrem == 0:
                        cand = (max(op.ready, free[e]), 0, op)
                if cand is not None and (best is None or cand[0] < best[1][0]):
                    best = (e, cand)
            assert best is not None, "scheduler deadlock (dependency cycle)"
            e, (st, i, op) = best
            if e == "pe" and self.filler is not None and self.nfill < self.MAXFILL:
                gap = st - free["pe"]
                if gap > self.FILLGAP and free["pe"] > 0.0 and not op.rdma:
                    nf = min(int(gap * self.FILLFRAC / 215.0), self.FILLCAP)
                    for _ in range(nf):
                        fo = Op("pe", self.filler[0], None, 215.0)
                        fo.deps = [self.filler[1]]
                        fo.phase = "fill"
                        new["pe"].append(fo)
                        self.nfill += 1
            pend[e].pop(i)
            new[e].append(op)
            if op.dma is not None:
                free[e] = st + (op.issue if op.issue else (1100.0 if e == "pool" else 400.0))
                b = max(st + 1800.0, dma_free[0])
                op.fin = b + op.cost
                dma_free[0] = op.fin
            else:
                op.fin = st + op.cost
                free[e] = op.fin
                if e == "act" and op.tbl is not None:
                    cur_tbl[0] = op.tbl
            for u in op.users:
                u.nrem -= 1
                t = op.fin + (0.0 if u.eng == e and op.dma is None else 120.0)
                if t > u.ready:
                    u.ready = t
                    u.rdma = op.dma is not None
            done += 1
        self.ops = new
        self.sim_time = max(free.values())
        import os as _os
        if _os.environ.get("KDEBUG"):
            ph = {}
            for e in self.ENGS:
                for op in new[e]:
                    d = ph.setdefault(op.phase, {})
                    a = d.setdefault(e, [1e18, 0.0, 0.0])
                    a[0] = min(a[0], op.fin - op.cost)
                    a[1] = max(a[1], op.fin)
                    a[2] += op.cost if op.dma is None else 0.0
            for p, d in ph.items():
                print(f"{p:10s}", "  ".join(f"{e}:{a[0] / 1e3:7.0f}-{a[1] / 1e3:7.0f} busy{a[2] / 1e3:6.0f}" for e, a in d.items()))

    def finalize(self):
        for e in self.ENGS:
            c = 0
            for op in self.ops[e]:
                if op.dma is None and op.sig:
                    c += 1
                    op.sigidx = c

    def check(self):
        ptr = {e: 0 for e in self.ENGS}
        cnt = {("e", e): 0 for e in self.ENGS}
        seen = {e: {} for e in self.ENGS}
        total = sum(len(v) for v in self.ops.values())
        done = 0
        while done < total:
            prog = False
            for e in self.ENGS:
                while ptr[e] < len(self.ops[e]):
                    op = self.ops[e][ptr[e]]
                    ok = True
                    for d in op.deps:
                        if cnt[("e", d.eng)] < d.sigidx:
                            ok = False
                    for (sname, v) in op.dwaits:
                        if cnt.get(("d", sname), 0) < v:
                            ok = False
                    if not ok:
                        break
                    if op.dma is not None:
                        cnt[("d", op.dma)] = cnt.get(("d", op.dma), 0) + 16
                    elif op.sig:
                        cnt[("e", e)] += 1
                        assert cnt[("e", e)] == op.sigidx
                    ptr[e] += 1
                    done += 1
                    prog = True
            if not prog:
                for e in self.ENGS:
                    if ptr[e] < len(self.ops[e]):
                        op = self.ops[e][ptr[e]]
                        print("BLOCKED", e, ptr[e], op.phase, [(d.eng, d.sigidx, cnt[("e", d.eng)]) for d in op.deps],
                              [(s_, v, cnt.get(("d", s_), 0)) for s_, v in op.dwaits])
                raise AssertionError("abstract deadlock")
        return True

    def emit(self, eng, handle, esem, dsem):
        seen = {}
        for op in self.ops[eng]:
            waits = {}
            for d in op.deps:
                k = ("e", d.eng)
                waits[k] = max(waits.get(k, 0), d.sigidx)
            for (s, v) in op.dwaits:
                k = ("d", s)
                waits[k] = max(waits.get(k, 0), v)
            for k, v in waits.items():
                if seen.get(k, 0) >= v:
                    continue
                seen[k] = v
                sem = esem[k[1]] if k[0] == "e" else dsem[k[1]]
                handle.wait_ge(sem, v)
            ins = op.fn(handle)
            if op.dma is not None:
                ins.then_inc(dsem[op.dma], 16)
            elif op.sig:
                ins.then_inc(esem[eng], 1)


def ap_of(base, off, dims):
    return AP(base.tensor, base.offset + off, [list(base.ap[0])] + [list(d) for d in dims])


def build_nc():
    nc = bass.Bass("TRN2", target_bir_lowering=False)
    S = Sched()

    def din(name, shape):
        return nc.dram_tensor(name, list(shape), F32, kind="ExternalInput").ap()

    def dout(name, shape):
        return nc.dram_tensor(name, list(shape), F32, kind="ExternalOutput").ap()

    xin = {"P": din("xp", [1024, D]), "S": din("xs", [1024, D])}
    cvec = din("cvec", [2, D])
    ck_d = din("ck", [2, 256, 128])
    cv_d = din("cv", [2, 256, 128])
    st_d = din("st", [2, 2, 256])
    w_ada = din("w_ada", [2, D, 6 * D])
    b_ada = din("b_ada", [2, 6 * D])
    w_in = din("w_in", [2, D, 1792])
    q_g = din("q_norm_g", [2, 64])
    k_g = din("k_norm_g", [2, 64])
    conv_w = din("conv_w", [2, 4, 256])
    conv_b = din("conv_b", [2, 256])
    lru_wa = din("lru_wa", [2, 2, 4, 64, 64])
    lru_ba = din("lru_ba", [2, 2, 256])
    lru_wx = din("lru_wx", [2, 2, 4, 64, 64])
    lru_bx = din("lru_bx", [2, 2, 256])
    lru_lam = din("lru_lam", [2, 2, 256])
    mlp_g = din("mlp_norm_g", [2, 256])
    mlp_b = din("mlp_norm_b", [2, 256])
    mlp_ws = din("mlp_ws", [2, 4, 128, 128])
    mlp_bs = din("mlp_bs", [2, 4, 128])
    w_out = din("w_out", [2, D, D])
    ln_gb = {(1, 0): din("ln1_g", [2, D]), (1, 1): din("ln1_b", [2, D]),
             (2, 0): din("ln2_g", [2, D]), (2, 1): din("ln2_b", [2, D])}
    w_ff1 = din("w_ff1", [2, D, 4 * D])
    b_ff1 = din("b_ff1", [2, 4 * D])
    w_ff2 = din("w_ff2", [2, 4 * D, D])
    b_ff2 = din("b_ff2", [2, D])
    ident_d = din("ident", [128, 128])
    rope_d = din("rope", [1024, 64])

    yout = {"P": dout("yp", [1024, D]), "S": dout("ys", [1024, D])}
    nk_d = dout("nk", [4, 2, 256, 128])
    nv_d = dout("nv", [4, 2, 256, 128])
    ns_d = dout("ns", [4, 2, 2, 256])
    modD = nc.dram_tensor("modD", [2, 2, 6 * D], F32).ap()

    es = contextlib.ExitStack()
    with es:
        def sb(name, shape, dt):
            return es.enter_context(nc.sbuf_tensor(name, list(shape), dt))

        X = sb("X", [128, 8, D], F32)
        HT = sb("HT", [128, 8, 1024], BF16)
        MH = sb("MH", [128, 8, 1024], BF16)
        WB = [sb(f"WB{i}", [128, 4096], BF16) for i in range(NWB)]
        G1 = sb("G1", [128, D], F32)
        G2 = sb("G2", [128, D], F32)
        LNA = sb("LNA", [128, D], F32)
        LNB = sb("LNB", [128, D], F32)
        QT = sb("QT", [128, 4, 1024], BF16)
        KT = sb("KT", [128, 1280], BF16)
        V2 = sb("V2", [128, 10, 256], BF16)
        PT2 = [sb(f"PT{i}", [128, 1024], BF16) for i in range(2)]
        XR = sb("XR", [128, 2, 1024], F32)
        GG = sb("GG", [128, 2, 1024], BF16)
        UT = sb("UT", [128, 2, 1024], BF16)
        VN = sb("VN", [128, 8, 256], BF16)
        FS = [sb(f"FS{i}", [128, 512], F32) for i in range(3)]
        TG = sb("TG", [128, 1024], F32)
        QBS = [sb(f"QB{i}", [128, 512], BF16) for i in range(2)]
        KF = [sb(f"KF{i}", [128, 128], F32) for i in range(2)]
        VF = [sb(f"VF{i}", [128, 128], F32) for i in range(2)]
        KB = sb("KB", [128, 128], BF16)
        XHB = sb("XHB", [128, 1024], BF16)
        STTA = sb("STTA", [128, 8, 32], F32)
        MSB = sb("MSB", [2, 1024], F32)
        IDB = sb("IDB", [128, 128], BF16)
        ONES = sb("ONES", [128, 128], BF16)
        ROPE = sb("ROPE", [128, 8, 64], F32)
        CVF = sb("CVF", [128, 8, 2], F32)
        SCV = sb("SCV", [128, 8, 2], BF16)
        GQ = sb("GQ", [128, 2, 64], F32)
        GK8 = sb("GK8", [128, 2, 64], F32)
        CW = sb("CW", [128, 2, 4, 2], F32)
        CB = sb("CB", [128, 2, 2], F32)
        BD = sb("BD", [128, 2, 2, 2, 2, 128], BF16)
        LBA = sb("LBA", [128, 2, 2, 2, 2], F32)
        CL = sb("CL", [128, 2, 2, 2], F32)
        MG = sb("MG", [128, 2, 256], F32)
        MBt = sb("MBt", [128, 2, 256], F32)
        WST = sb("WST", [128, 2, 4, 128], BF16)
        BSB = sb("BSB", [128, 2, 2, 128], F32)
        B1 = sb("B1", [128, 2, 32], F32)
        STI = sb("STI", [128, 2, 2, 2], F32)
        LNF = sb("LNF", [128, 2, 4, 8], F32)
        MSC = sb("MSC", [128, 2, 4, 8], F32)
        GCB = sb("GCB", [128, 2, 2, 8], F32)
        MS = MSB[:, 0:512]
        BA = MSB[:, 512:1024]
        stt_n = [0]

        def stt_next():
            i = stt_n[0] % 8
            stt_n[0] += 1
            return STTA[:, i, 0:16], [f"STT{i}"]
        NSS = sb("NSS", [128, 2, 2, 4], F32)
        EPSB = sb("EPSB", [128, 2], F32)
        FILL = sb("FILL", [128, 512], BF16)

        PS2 = [es.enter_context(nc.psum_tensor(f"PS{i}", [128, 1024], F32)) for i in range(4)]
        PSB = [PS2[b // 2][:, (b % 2) * 512:(b % 2 + 1) * 512] for b in range(8)]

        ps_free = list(range(7))

        def ps_get():
            assert ps_free, "out of PSUM banks"
            return ps_free.pop(0)

        def ps_put(b):
            ps_free.append(b)

        def pk(b):
            return [f"ps{b}"]

        wb_state = {"n": 0}

        def wb_load(parts):
            i = wb_state["n"] % NWB
            wb_state["n"] += 1
            for (off, dims, src) in parts:
                dst = ap_of(WB[i][:, 0:1], off, dims)
                nel = 128
                for d_ in dims:
                    nel *= d_[1]
                S.add("pool", (lambda e, dst=dst, src=src: e.dma_start(out=dst, in_=src)),
                      w=[f"WB{i}"], dma=f"wb{i}", cost=nel * 4 / 180.0)
            return i

        def wsrc(wt, l, r0, c0, nk, ncol):
            return wt[l, r0:r0 + nk * 128, c0:c0 + ncol].rearrange("(k p) c -> p k c", p=128)

        def dve(fn, r, w, n=512):
            return S.add("dve", fn, r, w, cost=70.0 + 1.3 * n)

        def act(fn, r, w, n=512, tbl=None):
            return S.add("act", fn, r, w, cost=240.0 + 0.7 * n, tbl=tbl)

        def pe(fn, r, w, n=512):
            return S.add("pe", fn, r, w, cost=12.0 + 0.40 * n)

        def spdma(out, in_, r, w, sem, nbytes=65536):
            return S.add("sp", (lambda e: e.dma_start(out=out, in_=in_, allow_slow_non_contiguous=True)),
                         r, w, dma=sem, cost=nbytes / 180.0)

        def pooldma(out, in_, r, w, sem, nbytes=16384, issue=None):
            op = S.add("pool", (lambda e: e.dma_start(out=out, in_=in_, allow_slow_non_contiguous=True)),
                       r, w, dma=sem, cost=nbytes / 180.0)
            op.issue = issue
            return op

        def htk(ks, ts):
            return [f"HT{k}_{t}" for k in ks for t in ts]

        def mhk(ks, ts):
            return [f"MH{k}_{t}" for k in ks for t in ts]

        def qtk(js, ts):
            return [f"QT{j}_{t}" for j in js for t in ts]

        ALL8 = list(range(8))

        def lbuf(m):
            return HT[:, 2 * m:2 * m + 2, :].rearrange("p a b -> p (a b)").bitcast(F32)

        LA_, LI_, LT_, LH_ = lbuf(0), lbuf(1), lbuf(2), lbuf(3)
        LKEY = [htk([2 * m, 2 * m + 1], ALL8) for m in range(4)]

        def xc_ap(c):
            return QT[:, 2 * c:2 * c + 2, :].rearrange("p a b -> p (a b)").bitcast(F32)

        def xc_key(c):
            return qtk([2 * c, 2 * c + 1], ALL8)

        V2flat = V2[:, :, :].rearrange("p a b -> p (a b)")

        def xcb_ap(c):
            return V2flat[:, c * 1024:(c + 1) * 1024]

        def xcb_key(c):
            return [f"V2_{tt}" for tt in range(4 * c, 4 * c + 4)]

        for t in range(8):
            spdma(X[:, t, :], xin["P"][t * 128:(t + 1) * 128, :], [], [f"X{t}"], "xl", nbytes=524288)
        pooldma(IDB[:, :], ident_d[:, :], [], ["IDB"], "idb")
        for c_ in range(2):
            spdma(CVF[:, :, c_], cvec[c_].rearrange("(k p) -> p k", p=128), [], ["CVF"], "cvf", nbytes=4096)
        act(lambda e: e.activation(out=SCV[:, :, :], in_=CVF[:, :, :], func=AF.Silu), ["CVF"], ["SCV"], n=16)
        def mod_block(l, blk):
            i = wb_load([(0, [[512, 8], [1, 512]], wsrc(w_ada, l, 0, blk * 512, 8, 512))])
            spdma(BA, AP(b_ada.tensor, b_ada[l, blk * 512:blk * 512 + 1].offset, [[0, 2], [1, 512]]),
                  [], ["BA"], "ba", nbytes=4096)
            b = ps_get()
            for k in range(8):
                pe(lambda e, k=k, i=i, b=b: e.matmul(PSB[b][0:2, :], lhsT=SCV[:, k, :],
                                                     rhs=WB[i][:, k * 512:(k + 1) * 512],
                                                     start=(k == 0), stop=(k == 7)),
                   ["SCV", f"WB{i}"], pk(b))
            dve(lambda e, b=b: e.tensor_tensor(out=MS, in0=PSB[b][0:2, :], in1=BA, op=ALU.add),
                pk(b) + ["BA"], ["MS"])
            ps_put(b)
            spdma(modD[l, :, blk * 512:(blk + 1) * 512], MS, ["MS"], [f"modD{l}_{blk // 2}"], "ms", nbytes=4096)

        bg = [(0, blk) for blk in range(4, 12)] + [(1, blk) for blk in range(12)]

        def bg_step(n=1):
            for _ in range(n):
                if bg:
                    mod_block(*bg.pop(0))


        for blk in range(4):
            mod_block(0, blk)
        dve(lambda e: e.memset(ONES[:, :], 1.0), [], ["ONES"])
        fill_ms = dve(lambda e: e.memset(FILL[:, :], 0.5), [], ["FILL"])
        fill_ms.sig = True
        S.filler = (lambda e: e.matmul(PSB[7][:, :], lhsT=ONES[:, :], rhs=FILL[:, :], start=True, stop=True), fill_ms)
        dve(lambda e: e.memset(EPSB[:, 0:1], EPS), [], ["EPSB"])
        dve(lambda e: e.memset(EPSB[:, 1:2], 64 * EPS), [], ["EPSB"])
        for l in range(2):
            spdma(GQ[:, l, :], AP(q_g.tensor, q_g[l, 0:1].offset, [[0, 128], [1, 64]]), [], ["GQ"], "sm", nbytes=32768)
            spdma(GK8[:, l, :], AP(k_g.tensor, k_g[l, 0:1].offset, [[0, 128], [1, 64]]), [], ["GK8"], "sm", nbytes=32768)
            spdma(MG[:, l, :], AP(mlp_g.tensor, mlp_g[l, 0:1].offset, [[0, 128], [1, 256]]), [], ["MG"], "sm")
            spdma(MBt[:, l, :], AP(mlp_b.tensor, mlp_b[l, 0:1].offset, [[0, 128], [1, 256]]), [], ["MBt"], "sm")
        dve(lambda e: e.tensor_scalar(out=GK8[:, :, :], in0=GK8[:, :, :], scalar1=8.0, scalar2=None, op0=ALU.mult),
            ["GK8"], ["GK8"], n=128)

        def load_bd(l, d, gi, wt):
            for n in range(4):
                c, h = n // 2, n % 2
                pooldma(BD[h * 64:(h + 1) * 64, l, d, gi, c, h * 64:(h + 1) * 64],
                        wt[l, d, n, :, :], ["BD"], [f"BDx{l}"], f"bd{l}", issue=4000.0)

        def deferred_bd1():
            for d in range(2):
                for gi, wt in enumerate((lru_wa, lru_wx)):
                    load_bd(1, d, gi, wt)

        def deferred_cl():
            act(lambda e: e.activation(out=CL[:, :, :, :], in_=CL[:, :, :, :], func=AF.Exp, scale=-1.0), ["CL"], ["CL"],
                n=8, tbl="explog")
            act(lambda e: e.activation(out=CL[:, :, :, :], in_=CL[:, :, :, :], func=AF.Ln, bias=1.0), ["CL"], ["CL"],
                n=8, tbl="explog")
            dve(lambda e: e.tensor_scalar(out=CL[:, :, :, :], in0=CL[:, :, :, :], scalar1=-8.0, scalar2=None,
                                          op0=ALU.mult), ["CL"], ["CL"], n=8)

        def deferred_params():
            dve(lambda e: e.memset(BD[:, :, :, :, :, :].rearrange("p a b c d f -> p (a b c d f)"), 0.0), [], ["BD"],
                n=2048)
            spdma(ROPE[:, :, :], rope_d.rearrange("(t p) c -> p t c", p=128), [], ["ROPE"], "sm3", nbytes=8192)
            for l in range(2):
                for j_ in range(4):
                    spdma(CW[:, l, j_, :], conv_w[l, j_].rearrange("(c p) -> p c", p=128), [], ["CW"], "sm3", nbytes=8192)
                spdma(CB[:, l, :], conv_b[l].rearrange("(c p) -> p c", p=128), [], ["CB"], "sm3", nbytes=8192)
                for d in range(2):
                    for gi, (wt, bt) in enumerate(((lru_wa, lru_ba), (lru_wx, lru_bx))):
                        spdma(LBA[:, l, d, gi, :], bt[l, d].rearrange("(c p) -> p c", p=128), [], ["LBA"], "sm3", nbytes=8192)
                        if l == 0:
                            load_bd(l, d, gi, wt)
                    spdma(CL[:, l, d, :], lru_lam[l, d].rearrange("(c p) -> p c", p=128), [], ["CL"], "cl", nbytes=8192)
                    spdma(STI[:, l, d, :], st_d[l, d].rearrange("(c p) -> p c", p=128), [], ["STI"], "sm3", nbytes=8192)
                for g in range(4):
                    gi, c2 = g % 2, g // 2
                    spdma(BSB[gi * 64:(gi + 1) * 64, l, c2, :],
                          AP(mlp_bs.tensor, mlp_bs[l, g, 0:1].offset, [[0, 64], [1, 128]]), [], ["BSB"], "sm3", nbytes=8192)
                spdma(B1[:, l, :], b_ff1[l].rearrange("(c p) -> p c", p=128), [], ["B1"], "sm3", nbytes=8192)
                for j, key in enumerate(((1, 0), (1, 1), (2, 0), (2, 1))):
                    spdma(LNF[:, l, j, :], ln_gb[key][l].rearrange("(c p) -> p c", p=128), [], ["LNF"], "sm3", nbytes=8192)
            for l in range(2):
                i = wb_load([(0, [[128, 4], [1, 128]], mlp_ws[l].rearrange("g p q -> p g q"))])
                b = ps_get()
                pst = PSB[b][:, :].bitcast(BF16)
                for g in range(4):
                    pe(lambda e, g=g, i=i, pst=pst: e.transpose(out=pst[:, g * 128:(g + 1) * 128],
                                                                in_=WB[i][:, g * 128:(g + 1) * 128], identity=IDB[:, :]),
                       [f"WB{i}", "IDB"], pk(b), n=128)
                dve(lambda e, l=l, pst=pst: e.tensor_copy(out=WST[:, l, :, :].rearrange("p g q -> p (g q)"),
                                                           in_=pst[:, 0:512]), pk(b), ["WST"])
                ps_put(b)

        def load_msc(l, cond, half):
            lp = l % 2
            key = [f"MSC{lp}{half}"]
            for j, idx in (((0, 0), (1, 1)) if half == 0 else ((2, 3), (3, 4))):
                spdma(MSC[:, lp, j, :], modD[l, cond, idx * D:(idx + 1) * D].rearrange("(c p) -> p c", p=128),
                      [f"modD{l}_{idx}"], key, f"msc{lp}{half}", nbytes=4096)
            j = 1 if half == 0 else 3
            dve(lambda e: e.tensor_scalar(out=MSC[:, lp, j, :], in0=MSC[:, lp, j, :], scalar1=1.0,
                                          scalar2=None, op0=ALU.add), key, key, n=8)

        def bcast_row(dst, src_t, off_ap, key_r, key_w, sem):
            spdma(dst[:, :], AP(src_t.tensor, off_ap.offset, [[0, 128], [1, D]]), key_r, key_w, sem)

        def nmt_a1(t):
            xk = [f"X{t}"]
            xt = X[:, t, :]
            st, sk = stt_next()
            dve(lambda e: e.bn_stats(out=st[:, 0:6], in_=xt[:, 0:512]), xk, sk, n=450)
            dve(lambda e: e.bn_stats(out=st[:, 6:12], in_=xt[:, 512:1024]), xk, sk, n=450)
            dve(lambda e: e.bn_aggr(out=st[:, 12:14], in_=st[:, 0:12]), sk, sk, n=100)
            return st, sk

        def nmt_a2(t, stsk, final_store=None):
            st, sk = stsk
            xk = [f"X{t}"]
            xt = X[:, t, :]
            act(lambda e: e.activation(out=st[:, 14:15], in_=st[:, 13:14], func=AF.Sqrt, bias=EPSB[:, 0:1]),
                sk + ["EPSB"], sk, n=60, tbl="sqrt")
            dve(lambda e: e.reciprocal(out=st[:, 14:15], in_=st[:, 14:15]), sk, sk, n=80)
            if final_store is None:
                dve(lambda e: e.tensor_scalar(out=st[:, 15:16], in0=st[:, 12:13], scalar1=st[:, 14:15], scalar2=-1.0,
                                              op0=ALU.mult, op1=ALU.mult), sk, sk, n=8)
                act(lambda e: e.activation(out=XHB[:, :], in_=xt, func=AF.Identity, scale=st[:, 14:15],
                                           bias=st[:, 15:16]), xk + sk, ["XHB"], n=1024)
            dve(lambda e: e.scalar_tensor_tensor(out=xt, in0=xt, scalar=st[:, 12:13], in1=LNA[:, :],
                                                 op0=ALU.subtract, op1=ALU.mult), xk + sk + ["LNA"], xk, n=900)
            dve(lambda e: e.scalar_tensor_tensor(out=xt, in0=xt, scalar=st[:, 14:15], in1=LNB[:, :],
                                                 op0=ALU.mult, op1=ALU.add), xk + sk + ["LNB"], xk, n=900)
            if final_store is not None:
                spdma(final_store, xt, xk, [], "yst", nbytes=524288)

        def nmt_b(t, which):
            b = ps_get()
            pst = PSB[b][:, :].bitcast(BF16)
            for k in range(8):
                pe(lambda e, k=k, pst=pst: e.transpose(out=pst[:, k * 128:(k + 1) * 128],
                                                       in_=XHB[:, k * 128:(k + 1) * 128], identity=IDB[:, :]),
                   ["XHB", "IDB"], pk(b), n=128)
            for k in range(8):
                act(lambda e, k=k, pst=pst: e.activation(out=HT[:, k, t * 128:(t + 1) * 128],
                                                         in_=pst[:, k * 128:(k + 1) * 128], func=AF.Identity,
                                                         scale=GCB[:, which, 0, k:k + 1],
                                                         bias=GCB[:, which, 1, k:k + 1]),
                    pk(b) + [f"GCB{which}"], htk([k], [t]), n=128)
            ps_put(b)

        def nmt(t, l, which, norm, la_lb, final_store=None):
            xk = [f"X{t}"]
            xt = X[:, t, :]
            if norm:
                nmt_a2(t, nmt_a1(t), final_store)
            else:
                act(lambda e: e.copy(out=XHB[:, :], in_=xt), xk, ["XHB"], n=1024)
                dve(lambda e: e.tensor_scalar(out=xt, in0=xt, scalar1=ALPHA, scalar2=None, op0=ALU.mult), xk, xk, n=600)
            if final_store is None:
                nmt_b(t, which)

        def qk_norm1(src_ps, nh, bkeys, sq, sqk):
            n = nh * 64
            st, sk = stt_next()
            act(lambda e: e.activation(out=sq[:, 0:n], in_=src_ps, func=AF.Square), bkeys, sqk, n=n)
            dve(lambda e: e.tensor_reduce(out=st[:, 0:nh], in_=sq[:, 0:n].rearrange("p (h d) -> p h d", h=nh),
                                          axis=AX.X, op=ALU.add), sqk, sk, n=n)
            return st, sk

        def qk_norm2(stsk, src_ps, nh, dst_f, gtile, bkeys, dkeys, gkey, out_ap=None, okeys=None):
            st, sk = stsk
            n = nh * 64
            act(lambda e: e.activation(out=st[:, 0:nh], in_=st[:, 0:nh], func=AF.Sqrt, bias=EPSB[:, 1:2]),
                sk + ["EPSB"], sk, n=60, tbl="sqrt")
            dve(lambda e: e.reciprocal(out=st[:, 0:nh], in_=st[:, 0:nh]), sk, sk, n=80)
            dve(lambda e: e.tensor_tensor(out=dst_f.rearrange("p (h d) -> p h d", h=nh),
                                          in0=src_ps.rearrange("p (h d) -> p h d", h=nh),
                                          in1=ap_of(st[:, 0:1], 0, [[1, nh], [0, 64]]), op=ALU.mult),
                bkeys + sk, dkeys, n=n)
            if out_ap is None:
                d3 = dst_f.rearrange("p (h d) -> p h d", h=nh)
                g3 = ap_of(gtile[:, 0:1], 0, [[0, nh], [1, 64]])
                dve(lambda e: e.tensor_tensor(out=d3, in0=d3, in1=g3, op=ALU.mult), dkeys + [gkey], dkeys, n=n)
            else:
                o_, i0_, i1_ = out_ap
                dve(lambda e: e.tensor_tensor(out=o_, in0=i0_, in1=i1_, op=ALU.mult), dkeys + [gkey], okeys, n=n)

        def rope(src_f, nh, t, dst_b, dst_dims_even, skeys, dkeys):
            T = [TG[:, i * 256:i * 256 + nh * 32].rearrange("p (h i) -> p h i", h=nh) for i in range(4)]
            x1 = ap_of(src_f[:, 0:1], 0, [[64, nh], [2, 32]])
            x2 = ap_of(src_f[:, 0:1], 1, [[64, nh], [2, 32]])
            cs = ap_of(ROPE[:, t, 0:1], 0, [[0, nh], [1, 32]])
            sn = ap_of(ROPE[:, t, 0:1], 32, [[0, nh], [1, 32]])
            dve(lambda e: e.tensor_tensor(out=T[0], in0=x1, in1=cs, op=ALU.mult), skeys + ["ROPE"], ["TG"])
            dve(lambda e: e.tensor_tensor(out=T[1], in0=x2, in1=sn, op=ALU.mult), skeys + ["ROPE"], ["TG"])
            dve(lambda e: e.tensor_tensor(out=T[2], in0=x1, in1=sn, op=ALU.mult), skeys + ["ROPE"], ["TG"])
            dve(lambda e: e.tensor_tensor(out=T[3], in0=x2, in1=cs, op=ALU.mult), skeys + ["ROPE"], ["TG"])
            if nh == 8:
                Tv = [ap_of(TG[:, 0:1], i * 256, [[128, 2], [32, 4], [1, 32]]) for i in range(4)]
            else:
                Tv = T
            de = ap_of(dst_b, 0, dst_dims_even)
            do = ap_of(dst_b, 1, dst_dims_even)
            dve(lambda e: e.tensor_tensor(out=de, in0=Tv[0], in1=Tv[1], op=ALU.subtract), ["TG"], dkeys)
            dve(lambda e: e.tensor_tensor(out=do, in0=Tv[2], in1=Tv[3], op=ALU.add), ["TG"], dkeys)

        class _Stop(Exception):
            pass

        import os as _os2
        _stop = _os2.environ.get("K_STOP")

        def stop_at(name):
            if _stop == name:
                raise _Stop()

        def group_layer(grp, l, first, last):
            is_s = grp == "S"
            cond = 1 if is_s else 0
            lp = l % 2
            nseq = 1 if is_s else 4
            L = 1024 // nseq
            ktoff = 256 if is_s else 0
            vtoff = 2 if is_s else 0

            dve(lambda e: e.memset(ap_of(V2[:, 0, 0:1], 64, [[128, 20], [1, 64]]), 1.0), [],
                [f"V2_{tt}" for tt in range(10)], n=1280)
            if first:
                load_msc(l, cond, 0)
                if is_s:
                    for t in range(8):
                        spdma(X[:, t, :], xin[grp][t * 128:(t + 1) * 128, :], [], [f"X{t}"], f"xs{t}", nbytes=524288)
                dve(lambda e: e.tensor_copy(out=GCB[:, 0, 0, :], in_=MSC[:, lp, 1, :]), [f"MSC{lp}0"], ["GCB0"], n=8)
                dve(lambda e: e.tensor_copy(out=GCB[:, 0, 1, :], in_=MSC[:, lp, 0, :]), [f"MSC{lp}0"], ["GCB0"], n=8)
            if is_s:
                for tt in range(2):
                    spdma(KF[tt][:, :], ck_d[l, tt * 128:(tt + 1) * 128, :], [], [f"KF{tt}"], f"kst{tt}")
                    spdma(VF[tt][:, :], cv_d[l, tt * 128:(tt + 1) * 128, :], [], [f"VF{tt}"], f"vst{tt}")

            def late_setup():
                mk = [f"MSC{lp}1"]
                load_msc(l, cond, 1)
                bcast_row(G1, modD, modD[l, cond, 2 * D:2 * D + 1], [f"modD{l}_2"], ["G1"], "g1")
                dve(lambda e: e.tensor_tensor(out=GCB[:, 1, 0, :], in0=LNF[:, l, 0, :], in1=MSC[:, lp, 3, :],
                                              op=ALU.mult), ["LNF"] + mk, ["GCB1"], n=8)
                dve(lambda e: e.tensor_tensor(out=GCB[:, 1, 1, :], in0=LNF[:, l, 1, :], in1=MSC[:, lp, 3, :],
                                              op=ALU.mult), ["LNF"] + mk, ["GCB1"], n=8)
                dve(lambda e: e.tensor_tensor(out=GCB[:, 1, 1, :], in0=GCB[:, 1, 1, :], in1=MSC[:, lp, 2, :],
                                              op=ALU.add), ["GCB1"] + mk, ["GCB1"], n=8)
                bcast_row(G2, modD, modD[l, cond, 5 * D:5 * D + 1], [f"modD{l}_5"], ["G2"], "g2")
                bcast_row(LNA, ln_gb[(1, 0)], ln_gb[(1, 0)][l, 0:1], [], ["LNA"], "lna")
                bcast_row(LNB, ln_gb[(1, 1)], ln_gb[(1, 1)][l, 0:1], [], ["LNB"], "lnb")
                bcast_row(TG, b_ff2, b_ff2[l, 0:1], [], ["TG", "TGv0", "TGv1"], "lnc")
                dve(lambda e: e.tensor_scalar(out=LNA[:, :], in0=LNA[:, :], scalar1=ALPHA, scalar2=None, op0=ALU.mult),
                    ["LNA"], ["LNA"], n=1024)
                dve(lambda e: e.tensor_tensor(out=TG[:, :], in0=TG[:, :], in1=G2[:, :], op=ALU.mult), ["TG", "G2"],
                    ["TG"], n=1024)
                dve(lambda e: e.scalar_tensor_tensor(out=LNB[:, :], in0=LNB[:, :], scalar=ALPHA, in1=TG[:, :],
                                                     op0=ALU.mult, op1=ALU.add), ["LNB", "TG"], ["LNB"], n=1024)

            stop_at(f"{grp}{l}:start")
            S.phase = f"{grp}{l}:win"
            if first:
                for t in range(2):
                    nmt(t, l, 0, norm=False, la_lb=False)

            stop_at(f"{grp}{l}:h1")
            iA = wb_load([(0, [[512, 8], [1, 512]], wsrc(w_in, l, 0, 0, 8, 512))])
            iB = wb_load([(0, [[512, 8], [1, 256]], wsrc(w_in, l, 0, 512, 8, 256)),
                          (256, [[512, 8], [1, 256]], wsrc(w_in, l, 0, 1536, 8, 256))])
            iC = wb_load([(0, [[512, 8], [1, 512]], wsrc(w_in, l, 0, 768, 8, 512))])
            iD = wb_load([(0, [[512, 8], [1, 256]], wsrc(w_in, l, 0, 1280, 8, 256))])

            if is_s:
                for tt in range(2):
                    act(lambda e, tt=tt: e.copy(out=KB[:, :], in_=KF[tt][:, :]), [f"KF{tt}"], ["KB"])
                    b = ps_get()
                    pst = PSB[b][:, :].bitcast(BF16)
                    pe(lambda e, pst=pst: e.transpose(out=pst[:, 0:128], in_=KB[:, :], identity=IDB[:, :]),
                       ["KB", "IDB"], pk(b), n=128)
                    dve(lambda e, tt=tt, pst=pst: e.tensor_copy(out=KT[:, tt * 128:(tt + 1) * 128], in_=pst[:, 0:128]),
                        pk(b), [f"KT{tt}"])
                    ps_put(b)
                    dve(lambda e, tt=tt: e.tensor_copy(
                        out=ap_of(V2[:, tt, 0:1], 0, [[128, 2], [1, 64]]),
                        in_=VF[tt][:, :].rearrange("p (k d) -> p k d", k=2)), [f"VF{tt}"], [f"V2_{tt}"], n=128)

            pm_dims = [[64, 2], [128, 4], [1, 64]]
            nat_dims = [[256, 2], [64, 4], [1, 64]]
            qst = {}

            def q_a1(t):
                if first and t + 2 < 8:
                    nmt(t + 2, l, 0, norm=False, la_lb=False)
                b = ps_get()
                for k in range(8):
                    pe(lambda e, k=k, b=b, t=t: e.matmul(PSB[b][:, :], lhsT=HT[:, k, t * 128:(t + 1) * 128],
                                                         rhs=WB[iA][:, k * 512:(k + 1) * 512],
                                                         start=(k == 0), stop=(k == 7)),
                       htk([k], [t]) + [f"WB{iA}"], pk(b))
                qst[t] = (b, qk_norm1(PSB[b][:, :], 8, pk(b), FS[2], ["FS2"]))

            def q_a2(t):
                b, stsk = qst[t]
                qf = FS[t % 2]
                qfk = [f"FS{t % 2}"]
                QB = QBS[t % 2]
                qbk = [f"QB{t % 2}"]
                if is_s:
                    qk_norm2(stsk, PSB[b][:, :], 8, qf[:, :], GQ[:, l, :], pk(b), qfk, "GQ")
                    ps_put(b)
                    rope(qf[:, :], 8, t, QB[:, 0:1], [[64, 2], [128, 4], [2, 32]], qfk, qbk)
                else:
                    qk_norm2(stsk, PSB[b][:, :], 8, qf[:, :], GQ[:, l, :], pk(b), qfk, "GQ",
                             out_ap=(ap_of(QB[:, 0:1], 0, pm_dims), ap_of(qf[:, 0:1], 0, nat_dims),
                                     ap_of(GQ[:, l, 0:1], 0, [[0, 2], [0, 4], [1, 64]])), okeys=qbk)
                    ps_put(b)

            def q_b(t):
                QB = QBS[t % 2]
                qbk = [f"QB{t % 2}"]
                b2 = ps_get()
                pst = PSB[b2][:, :].bitcast(BF16)
                for j in range(4):
                    pe(lambda e, j=j, pst=pst, QB=QB: e.transpose(out=pst[:, j * 128:(j + 1) * 128],
                                                                  in_=QB[:, j * 128:(j + 1) * 128], identity=IDB[:, :]),
                       qbk + ["IDB"], pk(b2), n=128)
                act(lambda e, t=t, pst=pst: e.copy(out=QT[:, :, t * 128:(t + 1) * 128],
                                                   in_=pst[:, 0:512].rearrange("p (j q) -> p j q", j=4)),
                    pk(b2), qtk(range(4), [t]))
                ps_put(b2)

            kst = {}

            def kv_a1(t):
                b = ps_get()
                for k in range(8):
                    pe(lambda e, k=k, b=b, t=t: e.matmul(PSB[b][:, :], lhsT=HT[:, k, t * 128:(t + 1) * 128],
                                                         rhs=WB[iB][:, k * 512:(k + 1) * 512],
                                                         start=(k == 0), stop=(k == 7)),
                       htk([k], [t]) + [f"WB{iB}"], pk(b))
                sq = FS[2][:, (t % 2) * 128:(t % 2 + 1) * 128]
                stsk = qk_norm1(PSB[b][:, 0:128], 2, pk(b), sq, [f"FS2k{t % 2}", "FS2"])
                tv = vtoff + t
                if not is_s:
                    vf = VF[t % 2]
                    act(lambda e, vf=vf, b=b: e.copy(out=vf[:, :], in_=PSB[b][:, 128:256]), pk(b), [f"VF{t % 2}"], n=128)
                    s_, tt = t // 2, t % 2
                    spdma(nv_d[s_, l, tt * 128:(tt + 1) * 128, :], vf[:, :], [f"VF{t % 2}"], [], f"vst{t % 2}")
                dve(lambda e, tv=tv, b=b: e.tensor_copy(
                    out=ap_of(V2[:, tv, 0:1], 0, [[128, 2], [1, 64]]),
                    in_=PSB[b][:, 128:256].rearrange("p (k d) -> p k d", k=2)), pk(b), [f"V2_{tv}"], n=128)
                vg = TG[:, (t % 2) * 256:(t % 2 + 1) * 256] if not is_s else FS[t % 2][:, 0:256]
                vgk = [f"TGv{t % 2}"] if not is_s else [f"FS{t % 2}"]
                act(lambda e, b=b, vg=vg: e.activation(out=vg, in_=PSB[b][:, 256:512], func=AF.Gelu_apprx_tanh),
                    pk(b), vgk, n=256, tbl="gelu")
                st2, sk2 = stt_next()
                dve(lambda e, st2=st2, vg=vg: e.bn_stats(out=st2[:, 0:6], in_=vg), vgk, sk2, n=256)
                dve(lambda e, st2=st2: e.bn_aggr(out=st2[:, 12:14], in_=st2[:, 0:6]), sk2, sk2, n=60)
                kst[t] = (b, stsk, vg, vgk, st2, sk2)

            def kv_a2(t):
                b, stsk, vg, vgk, st2, sk2 = kst[t]
                kf = KF[t % 2]
                kfk = [f"KF{t % 2}"]
                qk_norm2(stsk, PSB[b][:, 0:128], 2, kf[:, :], GK8[:, l, :], pk(b), kfk, "GK8")
                ps_put(b)
                act(lambda e, st2=st2: e.activation(out=st2[:, 14:15], in_=st2[:, 13:14], func=AF.Sqrt,
                                                    bias=EPSB[:, 0:1]), sk2 + ["EPSB"], sk2, n=60, tbl="sqrt")
                dve(lambda e, st2=st2: e.reciprocal(out=st2[:, 14:15], in_=st2[:, 14:15]), sk2, sk2, n=80)
                dve(lambda e, st2=st2, vg=vg: e.tensor_scalar(out=vg, in0=vg, scalar1=st2[:, 12:13],
                                                              scalar2=st2[:, 14:15], op0=ALU.subtract, op1=ALU.mult),
                    vgk + sk2, vgk, n=256)
                dve(lambda e, vg=vg: e.tensor_tensor(out=vg, in0=vg, in1=MG[:, l, :], op=ALU.mult), vgk + ["MG"], vgk,
                    n=256)
                dve(lambda e, t=t, vg=vg: e.tensor_tensor(out=VN[:, t, :], in0=vg, in1=MBt[:, l, :], op=ALU.add),
                    vgk + ["MBt"], [f"VN{t}"], n=256)
                if is_s:
                    rope(kf[:, :], 2, t, KB[:, 0:1], [[64, 2], [2, 32]], kfk, ["KB"])

            def kv_b(t):
                kf = KF[t % 2]
                kfk = [f"KF{t % 2}"]
                if not is_s:
                    s_, tt = t // 2, t % 2
                    spdma(nk_d[s_, l, tt * 128:(tt + 1) * 128, :], kf[:, :], kfk, [], f"kst{t % 2}")
                    act(lambda e, kf=kf: e.copy(out=KB[:, :], in_=kf[:, :]), kfk, ["KB"], n=128)
                b2 = ps_get()
                pst = PSB[b2][:, :].bitcast(BF16)
                pe(lambda e, pst=pst: e.transpose(out=pst[:, 0:128], in_=KB[:, :], identity=IDB[:, :]),
                   ["KB", "IDB"], pk(b2), n=128)
                kc = ktoff + t * 128
                dve(lambda e, kc=kc, pst=pst: e.tensor_copy(out=KT[:, kc:kc + 128], in_=pst[:, 0:128]),
                    pk(b2), [f"KT{kc // 128}"], n=128)
                ps_put(b2)

            def skewed(a1, a2, bb):
                for step in range(8 + 2):
                    if step < 8:
                        a1(step)
                    if 0 <= step - 2 < 8:
                        bb(step - 2)
                    if 0 <= step - 1 < 8:
                        a2(step - 1)

            skewed(q_a1, q_a2, q_b)
            stop_at(f"{grp}{l}:q")
            skewed(kv_a1, kv_a2, kv_b)
            stop_at(f"{grp}{l}:kv")

            for (slot, ncol, j) in [(iC, 512, 0), (iC, 512, 1), (iC, 512, 2), (iC, 512, 3), (iD, 256, 0), (iD, 256, 1)]:
                for tg in range(2):
                    b = ps_get()
                    for k in range(8):
                        pe(lambda e, k=k, b=b, tg=tg, slot=slot, ncol=ncol, j=j: e.matmul(
                            PSB[b][:, :], lhsT=WB[slot][:, k * 512 + j * 128:k * 512 + (j + 1) * 128],
                            rhs=HT[:, k, tg * 512:(tg + 1) * 512], start=(k == 0), stop=(k == 7)),
                           htk([k], range(tg * 4, tg * 4 + 4)) + [f"WB{slot}"], pk(b))
                    cols = slice(tg * 512, (tg + 1) * 512)
                    if slot == iC and j < 2:
                        act(lambda e, b=b, j=j, cols=cols: e.copy(out=XR[:, j, cols], in_=PSB[b][:, :]),
                            pk(b), [f"XR{j}"])
                    elif slot == iC:
                        act(lambda e, b=b, j=j, cols=cols: e.activation(out=GG[:, j - 2, cols], in_=PSB[b][:, :],
                                                                        func=AF.Gelu_apprx_tanh),
                            pk(b), [f"GG{j - 2}"], tbl="gelu")
                    else:
                        act(lambda e, b=b, j=j, cols=cols: e.activation(out=UT[:, j, cols], in_=PSB[b][:, :],
                                                                        func=AF.Gelu_apprx_tanh),
                            pk(b), [f"UT{j}"], tbl="gelu")
                    ps_put(b)


            if grp == "P" and l == 0:
                S.phase = "P0:defer"
                deferred_params()
            stop_at(f"{grp}{l}:att")
            S.phase = f"{grp}{l}:att"
            npt = 0
            nunit = 0
            npair = [0]
            if is_s:
                for b_ in (0, 1, 2, 3):
                    ps_free.remove(b_)
            for u in range(nseq if not is_s else 2):
                if is_s:
                    q0, N = u * 512, 512
                    tts = list(range(10))
                else:
                    q0, N = u * 256, 256
                    tts = [2 * u, 2 * u + 1]
                qts = list(range(q0 // 128, (q0 + N) // 128))
                for j in range(4):
                    for a in range(2):
                        h = j + 4 * a
                        rows = slice(a * 64, a * 64 + 64)
                        bo = ps_get()
                        if is_s:
                            for pi in range(5):
                                kp = npair[0] % 2
                                npair[0] += 1
                                b0_, b1_ = 2 * kp, 2 * kp + 1
                                for hb_, tt in ((b0_, 2 * pi), (b1_, 2 * pi + 1)):
                                    kc = tt * 128
                                    pe(lambda e, hb_=hb_, kc=kc, rows=rows, j=j, q0=q0, N=N: e.matmul(
                                        PSB[hb_][:, 0:N], lhsT=KT[rows, kc:kc + 128], rhs=QT[rows, j, q0:q0 + N],
                                        start=True, stop=True),
                                       [f"KT{kc // 128}"] + qtk([j], qts), pk(hb_), n=N)
                                pt = PT2[npt % 2]
                                ptk = [f"PT{npt % 2}_{k_}" for k_ in range(4)]
                                npt += 1
                                act(lambda e, kp=kp, pt=pt: e.activation(out=pt[:, :], in_=PS2[kp][:, :], func=AF.Exp),
                                    pk(b0_) + pk(b1_), ptk, n=850, tbl="explog")
                                for hi_, tt in ((0, 2 * pi), (1, 2 * pi + 1)):
                                    pe(lambda e, bo=bo, tt=tt, a=a, pt=pt, hi_=hi_: e.matmul(
                                        PSB[bo][:, 0:512], lhsT=V2[:, tt, a * 128:(a + 1) * 128],
                                        rhs=pt[:, hi_ * 512:(hi_ + 1) * 512],
                                        start=(tt == 0), stop=(tt == 9)), [f"V2_{tt}"] + ptk, pk(bo), n=512)
                        else:
                            for ti, tt in enumerate(tts):
                                bs_ = ps_get()
                                kc = tt * 128
                                pe(lambda e, bs_=bs_, kc=kc, rows=rows, j=j, q0=q0, N=N: e.matmul(
                                    PSB[bs_][:, 0:N], lhsT=KT[rows, kc:kc + 128], rhs=QT[rows, j, q0:q0 + N],
                                    start=True, stop=True),
                                   [f"KT{kc // 128}"] + qtk([j], qts), pk(bs_), n=N)
                                pt = PT2[(npt // 4) % 2][:, (npt % 4) * 256:(npt % 4 + 1) * 256]
                                ptk = [f"PT{(npt // 4) % 2}_{npt % 4}"]
                                npt += 1
                                act(lambda e, bs_=bs_, pt=pt, N=N: e.activation(out=pt[:, 0:N], in_=PSB[bs_][:, 0:N],
                                                                                func=AF.Exp), pk(bs_), ptk, n=N,
                                    tbl="explog")
                                ps_put(bs_)
                                pe(lambda e, bo=bo, tt=tt, a=a, pt=pt, N=N, ti=ti, nt=len(tts): e.matmul(
                                    PSB[bo][:, 0:N], lhsT=V2[:, tt, a * 128:(a + 1) * 128], rhs=pt[:, 0:N],
                                    start=(ti == 0), stop=(ti == nt - 1)), [f"V2_{tt}"] + ptk, pk(bo), n=N)
                        pr = slice((h % 2) * 64, (h % 2) * 64 + 64)
                        rz = FS[h % 2]
                        rzk = [f"FS{h % 2}"]
                        if is_s or nunit % 3 == 2:
                            dve(lambda e, bo=bo, rz=rz, N=N: e.reciprocal(out=rz[0:64, 0:N], in_=PSB[bo][64:128, 0:N]),
                                pk(bo), rzk, n=int(5.0 * N))
                        else:
                            act(lambda e, bo=bo, rz=rz, N=N: e.activation(out=rz[0:64, 0:N], in_=PSB[bo][64:128, 0:N],
                                                                          func=AF.Ln), pk(bo), rzk, n=N, tbl="explog")
                            act(lambda e, rz=rz, N=N: e.activation(out=rz[0:64, 0:N], in_=rz[0:64, 0:N], func=AF.Exp,
                                                                   scale=-1.0), rzk, rzk, n=N, tbl="explog")
                        dve(lambda e, bo=bo, rz=rz, pr=pr, N=N, h=h, q0=q0: e.tensor_tensor(
                            out=MH[pr, h // 2, q0:q0 + N], in0=PSB[bo][0:64, 0:N], in1=rz[0:64, 0:N], op=ALU.mult),
                            pk(bo) + rzk, mhk([h // 2], qts), n=N)
                        ps_put(bo)
                        nunit += 1

            if is_s:
                ps_free.extend([0, 1, 2, 3])
            stop_at(f"{grp}{l}:lru")
            S.phase = f"{grp}{l}:lru"
            if grp == "P" and l == 0:
                deferred_cl()
            def lru_chunk(c):
                xr = XR[:, c, :]
                xc = xc_ap(c)
                xck = xc_key(c)
                xrk = [f"XR{c}"]
                w_ = lambda j: CW[:, l, j, c:c + 1]
                xr3 = xr.rearrange("p (s t) -> p s t", s=nseq)
                xc3 = xc.rearrange("p (s t) -> p s t", s=nseq)
                dve(lambda e: e.tensor_scalar(out=xc, in0=xr, scalar1=w_(1), scalar2=CB[:, l, c:c + 1],
                                              op0=ALU.mult, op1=ALU.add), xrk + ["CW", "CB"], xck)
                dve(lambda e: e.scalar_tensor_tensor(out=xc3[:, :, 1:L], in0=xr3[:, :, 0:L - 1], scalar=w_(0),
                                                     in1=xc3[:, :, 1:L], op0=ALU.mult, op1=ALU.add),
                    xrk + xck + ["CW"], xck)
                dve(lambda e: e.scalar_tensor_tensor(out=xc3[:, :, 0:L - 1], in0=xr3[:, :, 1:L], scalar=w_(2),
                                                     in1=xc3[:, :, 0:L - 1], op0=ALU.mult, op1=ALU.add),
                    xrk + xck + ["CW"], xck)
                dve(lambda e: e.scalar_tensor_tensor(out=xc3[:, :, 0:L - 2], in0=xr3[:, :, 2:L], scalar=w_(3),
                                                     in1=xc3[:, :, 0:L - 2], op0=ALU.mult, op1=ALU.add),
                    xrk + xck + ["CW"], xck)
                xcb = xcb_ap(c)
                xcbk = xcb_key(c)
                act(lambda e: e.copy(out=xcb, in_=xc), xck, xcbk)
                def lru_dir(d):
                    for gi, (dst, dkey) in enumerate(((LA_, LKEY[0]), (LI_, LKEY[1]))):
                        for tg in range(2):
                            b = ps_get()
                            pe(lambda e, b=b, gi=gi, tg=tg: e.matmul(
                                PSB[b][:, :], lhsT=BD[:, l, d, gi, c, :], rhs=xcb[:, tg * 512:(tg + 1) * 512],
                                start=True, stop=True), [f"BDx{l}"] + xcbk, pk(b))
                            act(lambda e, b=b, gi=gi, tg=tg, dst=dst: e.activation(
                                out=dst[:, tg * 512:(tg + 1) * 512], in_=PSB[b][:, :], func=AF.Sigmoid,
                                bias=LBA[:, l, d, gi, c:c + 1]), pk(b) + ["LBA"], dkey, tbl="sig")
                            ps_put(b)
                    act(lambda e: e.activation(out=LA_, in_=LA_, func=AF.Exp, scale=CL[:, l, d, c:c + 1]),
                        LKEY[0] + ["CL"], LKEY[0], n=1024, tbl="explog")
                    dve(lambda e: e.tensor_tensor(out=LT_, in0=LA_, in1=LA_, op=ALU.mult), LKEY[0], LKEY[2], n=1024)
                    act(lambda e: e.activation(out=LT_, in_=LT_, func=AF.Ln, scale=-1.0, bias=1.0), LKEY[2], LKEY[2],
                        n=1024, tbl="explog")
                    act(lambda e: e.activation(out=LT_, in_=LT_, func=AF.Exp, scale=0.5), LKEY[2], LKEY[2],
                        n=1024, tbl="explog")
                    dve(lambda e: e.tensor_tensor(out=LI_, in0=LI_, in1=LT_, op=ALU.mult), LKEY[1] + LKEY[2], LKEY[1],
                        n=1024)
                    dve(lambda e: e.tensor_tensor(out=LI_, in0=LI_, in1=xc, op=ALU.mult), LKEY[1] + xck, LKEY[1], n=1024)
                    hdst = xr if d == 0 else LH_
                    hkey = xrk if d == 0 else LKEY[3]
                    for s_ in range(nseq):
                        if d == 0:
                            sl = lambda tns: tns[:, s_ * L:(s_ + 1) * L]
                        else:
                            def sl(tns, s_=s_):
                                base = tns[:, (s_ + 1) * L - 1:(s_ + 1) * L]
                                return AP(base.tensor, base.offset, [list(base.ap[0]), [-1, L]])
                        init = STI[:, l, d, c:c + 1] if is_s else 0.0
                        o_, a_, u_ = sl(hdst), sl(LA_), sl(LI_)
                        dve(lambda e, o_=o_, a_=a_, u_=u_, init=init: e.tensor_tensor_scan(
                            out=o_, data0=a_, data1=u_, initial=init, op0=ALU.mult, op1=ALU.add),
                            LKEY[0] + LKEY[1] + ["STI"], hkey)
                    if not is_s:
                        fin = (L - 1) if d == 0 else 0
                        dve(lambda e, hdst=hdst, fin=fin, d=d: e.tensor_copy(
                            out=NSS[:, d, c, :], in_=ap_of(hdst[:, 0:1], fin, [[L, 4]])), hkey, ["NSS"])
                for d in range(2):
                    bg_step(2)
                    lru_dir(d)
                dve(lambda e: e.tensor_tensor(out=xr, in0=xr, in1=LH_, op=ALU.add), xrk + LKEY[3], xrk)
                dve(lambda e: e.tensor_tensor(out=MH[:, 4 + c, :], in0=xr, in1=GG[:, c, :], op=ALU.mult),
                    xrk + [f"GG{c}"], mhk([4 + c], ALL8))
            for c in range(2):
                lru_chunk(c)
            while bg and bg[0][0] <= l:
                bg_step()
            late_setup()
            iE = wb_load([(0, [[512, 8], [1, 512]], wsrc(w_out, l, 0, 0, 8, 512))])
            iF = wb_load([(0, [[512, 8], [1, 512]], wsrc(w_out, l, 0, 512, 8, 512))])
            if not is_s:
                for s_ in range(4):
                    for d in range(2):
                        spdma(ns_d[s_, l, d, :].rearrange("(c p) -> p c", p=128), NSS[:, d, :, s_], ["NSS"], [], "nst")

            stop_at(f"{grp}{l}:mlp")
            S.phase = f"{grp}{l}:mlp"
            def gmlp_mix(c2, gi):
                if True:
                    g = 2 * c2 + gi
                    b0, b1 = ps_get(), ps_get()
                    for t in range(8):
                        bb = b0 if t < 4 else b1
                        pe(lambda e, bb=bb, t=t, g=g: e.matmul(
                            PSB[bb][:, (t % 4) * 128:(t % 4 + 1) * 128], lhsT=VN[:, t, c2 * 128:(c2 + 1) * 128],
                            rhs=WST[:, l, g, :], start=True, stop=True), [f"VN{t}", "WST"], pk(bb))
                    pr = slice(gi * 64, gi * 64 + 64)
                    for hh, bb in enumerate((b0, b1)):
                        cols = slice(hh * 512, (hh + 1) * 512)
                        dve(lambda e, bb=bb, pr=pr, cols=cols: e.tensor_tensor(
                            out=TG[pr, cols].rearrange("p (t q) -> p t q", t=4),
                            in0=PSB[bb][pr, :].rearrange("p (t q) -> p t q", t=4),
                            in1=ap_of(BSB[pr, l, c2, 0:1], 0, [[0, 4], [1, 128]]), op=ALU.add),
                            pk(bb) + ["BSB"], ["TG"])
                        dve(lambda e, pr=pr, cols=cols: e.tensor_tensor(
                            out=MH[pr, 6 + c2, cols], in0=TG[pr, cols], in1=UT[pr, c2, cols], op=ALU.mult),
                            ["TG", f"UT{c2}"], mhk([6 + c2], range(hh * 4, hh * 4 + 4)))
                    ps_put(b0)
                    ps_put(b1)

            for c2 in range(2):
                for gi in range(2):
                    bg_step(1)
                    gmlp_mix(c2, gi)

            stop_at(f"{grp}{l}:wout")
            S.phase = f"{grp}{l}:wout"
            wst_ = {}

            def wo_a1(t):
                for hf, slot in enumerate((iE, iF)):
                    b = ps_get()
                    for k in range(8):
                        pe(lambda e, k=k, b=b, t=t, slot=slot: e.matmul(
                            PSB[b][:, :], lhsT=MH[:, k, t * 128:(t + 1) * 128], rhs=WB[slot][:, k * 512:(k + 1) * 512],
                            start=(k == 0), stop=(k == 7)), mhk([k], [t]) + [f"WB{slot}"], pk(b))
                    cols = slice(hf * 512, (hf + 1) * 512)
                    tmp = FS[(2 * t + hf) % 3]
                    tk = [f"FS{(2 * t + hf) % 3}"]
                    dve(lambda e, b=b, cols=cols, tmp=tmp: e.tensor_tensor(out=tmp[:, :], in0=PSB[b][:, :],
                                                                           in1=G1[:, cols], op=ALU.mult),
                        pk(b) + ["G1"], tk)
                    ps_put(b)
                    dve(lambda e, t=t, cols=cols, tmp=tmp: e.tensor_tensor(out=X[:, t, cols], in0=X[:, t, cols],
                                                                           in1=tmp[:, :], op=ALU.add),
                        tk + [f"X{t}"], [f"X{t}"])
                wst_[t] = nmt_a1(t)

            skewed(wo_a1, lambda t: nmt_a2(t, wst_[t]), lambda t: nmt_b(t, 1))

            if not last:
                load_msc(l + 1, cond, 0)
                nlp = (l + 1) % 2
                dve(lambda e: e.tensor_tensor(out=GCB[:, 0, 0, :], in0=LNF[:, l, 2, :], in1=MSC[:, nlp, 1, :],
                                              op=ALU.mult), ["LNF", f"MSC{nlp}0"], ["GCB0"], n=8)
                dve(lambda e: e.tensor_tensor(out=GCB[:, 0, 1, :], in0=LNF[:, l, 3, :], in1=MSC[:, nlp, 1, :],
                                              op=ALU.mult), ["LNF", f"MSC{nlp}0"], ["GCB0"], n=8)
                dve(lambda e: e.tensor_tensor(out=GCB[:, 0, 1, :], in0=GCB[:, 0, 1, :], in1=MSC[:, nlp, 0, :],
                                              op=ALU.add), ["GCB0", f"MSC{nlp}0"], ["GCB0"], n=8)
            bcast_row(LNA, ln_gb[(2, 0)], ln_gb[(2, 0)][l, 0:1], [], ["LNA"], "lna")
            bcast_row(LNB, ln_gb[(2, 1)], ln_gb[(2, 1)][l, 0:1], [], ["LNB"], "lnb")
            if not last:
                dve(lambda e: e.tensor_scalar(out=LNA[:, :], in0=LNA[:, :], scalar1=ALPHA, scalar2=None,
                                              op0=ALU.mult), ["LNA"], ["LNA"])
                dve(lambda e: e.tensor_scalar(out=LNB[:, :], in0=LNB[:, :], scalar1=ALPHA, scalar2=None,
                                              op0=ALU.mult), ["LNB"], ["LNB"])

            stop_at(f"{grp}{l}:ffn")
            S.phase = f"{grp}{l}:ffn"
            if grp == "P" and l == 0:
                deferred_bd1()
            nrl = 0
            for qd in range(4):
                i1 = [wb_load([(0, [[512, 8], [1, 512]], wsrc(w_ff1, l, 0, qd * 1024 + hb * 512, 8, 512))])
                      for hb in range(2)]
                i2 = [wb_load([(0, [[1024, 4], [1, 1024]], wsrc(w_ff2, l, qd * 1024 + hb * 512, 0, 4, 1024))])
                      for hb in range(2)]
                for tg in range(2):
                    for hc in range(8):
                        slot = i1[hc // 4]
                        cc = (hc % 4) * 128
                        b = ps_get()
                        for k in range(8):
                            pe(lambda e, k=k, b=b, tg=tg, slot=slot, cc=cc: e.matmul(
                                PSB[b][:, :], lhsT=WB[slot][:, k * 512 + cc:k * 512 + cc + 128],
                                rhs=HT[:, k, tg * 512:(tg + 1) * 512], start=(k == 0), stop=(k == 7)),
                               htk([k], range(tg * 4, tg * 4 + 4)) + [f"WB{slot}"], pk(b))
                        rl = FS[nrl % 3]
                        rk = [f"FS{nrl % 3}"]
                        nrl += 1
                        chunk = qd * 8 + hc
                        act(lambda e, b=b, rl=rl, chunk=chunk: e.activation(out=rl[:, :], in_=PSB[b][:, :], func=AF.Relu,
                                                                            bias=B1[:, l, chunk:chunk + 1]),
                            pk(b) + ["B1"], rk)
                        ps_put(b)
                        act(lambda e, rl=rl, hc=hc, tg=tg: e.activation(out=MH[:, hc, tg * 512:(tg + 1) * 512],
                                                                        in_=rl[:, :], func=AF.Square),
                            rk, mhk([hc], range(tg * 4, tg * 4 + 4)))
                for tg in range(2):
                    for t in range(tg * 4, tg * 4 + 4):
                        for hf in range(2):
                            b = ps_get()
                            for hc in range(8):
                                slot = i2[hc // 4]
                                pe(lambda e, hc=hc, b=b, t=t, slot=slot, hf=hf: e.matmul(
                                    PSB[b][:, :], lhsT=MH[:, hc, t * 128:(t + 1) * 128],
                                    rhs=WB[slot][:, (hc % 4) * 1024 + hf * 512:(hc % 4) * 1024 + (hf + 1) * 512],
                                    start=(hc == 0), stop=(hc == 7)), mhk([hc], [t]) + [f"WB{slot}"], pk(b))
                            cols = slice(hf * 512, (hf + 1) * 512)
                            tmp = FS[nrl % 3]
                            tk = [f"FS{nrl % 3}"]
                            nrl += 1
                            dve(lambda e, b=b, cols=cols, tmp=tmp: e.tensor_tensor(out=tmp[:, :], in0=PSB[b][:, :],
                                                                                   in1=G2[:, cols], op=ALU.mult),
                                pk(b) + ["G2"], tk)
                            ps_put(b)
                            dve(lambda e, t=t, cols=cols, tmp=tmp: e.tensor_tensor(out=X[:, t, cols], in0=X[:, t, cols],
                                                                                   in1=tmp[:, :], op=ALU.add),
                                tk + [f"X{t}"], [f"X{t}"])
                        if qd == 3:
                            S.phase = f"{grp}{l}:ln2"
                            if last:
                                nmt(t, l, 0, norm=True, la_lb=True, final_store=yout[grp][t * 128:(t + 1) * 128, :])
                            else:
                                nmt(t, l, 0, norm=True, la_lb=True)
                            S.phase = f"{grp}{l}:ffn"

        try:
            for grp in ("P", "S"):
                for l in range(2):
                    group_layer(grp, l, first=(l == 0), last=(l == 1))
        except _Stop:
            pass

        final_sems = ["yst", "nst", "kst0", "kst1", "vst0", "vst1"]
        import os as _os
        S.schedule()
        if _os.environ.get("KDEBUG"):
            print("ops", {e: len(v) for e, v in S.ops.items()}, "sim_us", S.sim_time / 1e3,
                  "sbuf_left", nc.sbuf_bytes_remaining)
        S.finalize()
        S.check()
        if _os.environ.get("KDUMP"):
            for i, op in enumerate(S.ops[_os.environ["KDUMP"]]):
                print(i, "lidx", op.lidx, "tbl", op.tbl, "cost", int(op.cost), "sig", op.sigidx if op.sig else None,
                      "deps", [(d.eng, d.sigidx, d.lidx) for d in op.deps], "dw", op.dwaits)

        dma_names = sorted(S.dma_counts.keys())
        sems = {}
        for n in list(dma_names) + ["e_pe", "e_act", "e_dve", "e_pool", "e_sp"]:
            sems[n] = es.enter_context(nc.semaphore(n))
        esem = {e: sems["e_" + e] for e in Sched.ENGS}
        dsem = {n: sems[n] for n in dma_names}

        with nc.Block() as block:
            @block.tensor
            def _(e):
                S.emit("pe", e, esem, dsem)

            @block.scalar
            def _(e):
                S.emit("act", e, esem, dsem)

            @block.vector
            def _(e):
                S.emit("dve", e, esem, dsem)

            @block.gpsimd
            def _(e):
                S.emit("pool", e, esem, dsem)

            @block.sync
            def _(e):
                S.emit("sp", e, esem, dsem)
                for n in final_sems:
                    if n in dsem:
                        e.wait_ge(dsem[n], S.dma_counts[n] * 16)
    return nc


_NC_CACHE = {}


def _rope_table():
    pos = np.arange(1024)
    pr = (pos // 64).astype(np.float32)
    pc = (pos % 64).astype(np.float32)
    inv = (10000.0 ** (-np.arange(16, dtype=np.float32) / 16)).astype(np.float32)
    ang = np.concatenate([pr[:, None] * inv, pc[:, None] * inv], -1).astype(np.float32)
    return np.concatenate([np.cos(ang), np.sin(ang)], -1).astype(np.float32)


def kernel(x_prompt, x_sample, c, cache_k, cache_v, state_lru, c_ctx, w_ada, b_ada, w_in,
           q_norm_g, k_norm_g, conv_w, conv_b, lru_wa, lru_ba, lru_wx, lru_bx, lru_lam,
           mlp_norm_g, mlp_norm_b, mlp_ws, mlp_bs, w_out, ln1_g, ln1_b, w_ff1, b_ff1,
           w_ff2, b_ff2, ln2_g, ln2_b):
    f = lambda a: np.ascontiguousarray(np.asarray(a, dtype=np.float32))
    if "nc" not in _NC_CACHE:
        _NC_CACHE["nc"] = build_nc()
    nc = _NC_CACHE["nc"]
    shared = dict(w_ada=f(w_ada), b_ada=f(b_ada), w_in=f(w_in), q_norm_g=f(q_norm_g), k_norm_g=f(k_norm_g),
                  conv_w=f(conv_w), conv_b=f(conv_b), lru_wa=f(lru_wa), lru_ba=f(lru_ba), lru_wx=f(lru_wx),
                  lru_bx=f(lru_bx), lru_lam=f(lru_lam), mlp_norm_g=f(mlp_norm_g), mlp_norm_b=f(mlp_norm_b),
                  mlp_ws=f(mlp_ws), mlp_bs=f(mlp_bs), w_out=f(w_out), ln1_g=f(ln1_g), ln1_b=f(ln1_b),
                  w_ff1=f(w_ff1), b_ff1=f(b_ff1), w_ff2=f(w_ff2), b_ff2=f(b_ff2), ln2_g=f(ln2_g), ln2_b=f(ln2_b),
                  ident=np.eye(128, dtype=np.float32), rope=_rope_table())
    x_prompt, x_sample, c, c_ctx = f(x_prompt), f(x_sample), f(c), f(c_ctx)
    cache_k, cache_v, state_lru = f(cache_k), f(cache_v), f(state_lru)
    in_maps = []
    for i in range(8):
        m = dict(shared)
        m["xp"] = np.ascontiguousarray(x_prompt[4 * i:4 * i + 4].reshape(1024, D))
        m["xs"] = np.ascontiguousarray(x_sample[i])
        m["cvec"] = np.ascontiguousarray(np.stack([c_ctx, c[i]], 0))
        m["ck"] = np.ascontiguousarray(cache_k[i].reshape(2, 256, 128))
        m["cv"] = np.ascontiguousarray(cache_v[i].reshape(2, 256, 128))
        m["st"] = np.ascontiguousarray(state_lru[i])
        in_maps.append(m)
    res = run_bass_kernel_spmd(nc, in_maps, core_ids=list(range(8)))
    R = res.results
    y_prompt = np.concatenate([r["yp"].reshape(4, 256, D) for r in R], 0)
    y_sample = np.stack([r["ys"] for r in R], 0)
    nk = np.concatenate([r["nk"].reshape(4, 2, 256, 2, 64) for r in R], 0)
    nv = np.concatenate([r["nv"].reshape(4, 2, 256, 2, 64) for r in R], 0)
    ns = np.concatenate([r["ns"] for r in R], 0)
    return (y_prompt.astype(np.float32), y_sample.astype(np.float32), nk.astype(np.float32),
            nv.astype(np.float32), ns.astype(np.float32))
```

```python
import contextlib
import numpy as np
import concourse.bass as bass
import concourse.mybir as mybir
from concourse.bass_utils import run_bass_kernel_spmd
from concourse.ap import AP

F32 = mybir.dt.float32
BF16 = mybir.dt.bfloat16
AF = mybir.ActivationFunctionType
ALU = mybir.AluOpType
AX = mybir.AxisListType

D = 1024
ALPHA = 4.0 ** 0.25
EPS = 1e-6
NWB = 6


class Op:
    __slots__ = ("eng", "fn", "deps", "odeps", "ddeps", "dwaits", "sig", "sigidx", "dma", "dma_cnt",
                 "pos", "cost", "users", "nrem", "ready", "fin", "lidx", "tbl", "phase", "issue", "rdma")

    def __init__(self, eng, fn, dma, cost):
        self.eng, self.fn, self.dma, self.cost = eng, fn, dma, cost
        self.deps = []
        self.odeps = []
        self.ddeps = []
        self.dwaits = []
        self.sig = False
        self.sigidx = 0
        self.dma_cnt = 0
        self.pos = 0
        self.users = []
        self.nrem = 0
        self.ready = 0.0
        self.fin = 0.0
        self.lidx = 0
        self.tbl = None
        self.issue = None
        self.rdma = False


class Sched:
    ENGS = ("pe", "act", "dve", "pool", "sp")
    import os as _os4
    OOO = tuple(_os4.environ.get("K_OOO", "pe,dve").split(","))
    import os as _os3
    WINDOW = int(_os3.environ.get("K_WIN", "48"))
    MAXFILL = int(_os3.environ.get("K_MAXFILL", "12000"))
    FILLFRAC = float(_os3.environ.get("K_FILLFRAC", "0.95"))
    FILLCAP = int(_os3.environ.get("K_FILLCAP", "100"))
    FILLGAP = float(_os3.environ.get("K_FILLGAP", "250"))

    def __init__(self):
        self.ops = {e: [] for e in self.ENGS}
        self.lastw = {}
        self.readers = {}
        self.dma_counts = {}
        self.dma_ops = {}
        self.filler = None
        self.nfill = 0
        self.nops = 0
        self.phase = "pro"

    def add(self, eng, fn, r=(), w=(), dma=None, cost=300.0, tbl=None):
        op = Op(eng, fn, dma, cost)
        op.tbl = tbl
        op.phase = self.phase
        op.lidx = self.nops
        self.nops += 1
        r = list(r)
        w = list(w)
        deps = {}

        def consider(d):
            if d is None or d is op:
                return
            deps[id(d)] = d

        for k in r:
            consider(self.lastw.get(k))
        for k in w:
            consider(self.lastw.get(k))
            for d in self.readers.get(k, ()):
                consider(d)
        for d in deps.values():
            if d.dma is not None:
                op.dwaits.append((d.dma, self.dma_counts[d.dma] * 16))
                op.ddeps.append(d)
                lastd = self.dma_ops[d.dma][-1]
                if lastd is not d and lastd is not op:
                    op.ddeps.append(lastd)
            elif d.eng == eng and eng == "pe" and op.dma is None:
                op.odeps.append(d)
            else:
                d.sig = True
                op.deps.append(d)
        for k in r:
            self.readers.setdefault(k, []).append(op)
        for k in w:
            self.lastw[k] = op
            self.readers[k] = []
        if dma is not None:
            self.dma_counts[dma] = self.dma_counts.get(dma, 0) + 1
            op.dma_cnt = self.dma_counts[dma]
            self.dma_ops.setdefault(dma, []).append(op)
        self.ops[eng].append(op)
        return op

    def schedule(self):
        allops = [op for e in self.ENGS for op in self.ops[e]]
        for op in allops:
            op.users = []
        for op in allops:
            ds = op.deps + op.odeps + op.ddeps
            op.nrem = len(ds)
            op.ready = 0.0
            for d in ds:
                d.users.append(op)
        pend = {e: list(self.ops[e]) for e in self.ENGS}
        new = {e: [] for e in self.ENGS}
        free = {e: 0.0 for e in self.ENGS}
        dma_free = [0.0]
        cur_tbl = [None]
        total = len(allops)
        done = 0
        while done < total:
            best = None
            for e in self.ENGS:
                q = pend[e]
                if not q:
                    continue
                cand = None
                if e in self.OOO:
                    lim = min(len(q), self.WINDOW)
                    if e == "act" and q[0].phase == "pro":
                        lim = 1
                    ft = free[e]
                    for i in range(lim):
                        op = q[i]
                        if op.nrem:
                            continue
                        st = op.ready if op.ready > ft else ft
                        if e == "act" and op.tbl is not None and op.tbl != cur_tbl[0]:
                            st += 1300.0
                        if cand is None or st < cand[0]:
                            cand = (st, i, op)
                            if st <= ft:
                                break
                else:
                    op = q[0]
                    if op.nrem == 0:
                        cand = (max(op.ready, free[e]), 0, op)
                if cand is not None and (best is None or cand[0] < best[1][0]):
                    best = (e, cand)
            assert best is not None, "scheduler deadlock (dependency cycle)"
            e, (st, i, op) = best
            if e == "pe" and self.filler is not None and self.nfill < self.MAXFILL:
                gap = st - free["pe"]
                if gap > self.FILLGAP and free["pe"] > 0.0 and not op.rdma:
                    nf = min(int(gap * self.FILLFRAC / 215.0), self.FILLCAP)
                    for _ in range(nf):
                        fo = Op("pe", self.filler[0], None, 215.0)
                        fo.deps = [self.filler[1]]
                        fo.phase = "fill"
                        new["pe"].append(fo)
                        self.nfill += 1
            pend[e].pop(i)
            new[e].append(op)
            if op.dma is not None:
                free[e] = st + (op.issue if op.issue else (1100.0 if e == "pool" else 400.0))
                b = max(st + 1800.0, dma_free[0])
                op.fin = b + op.cost
                dma_free[0] = op.fin
            else:
                op.fin = st + op.cost
                free[e] = op.fin
                if e == "act" and op.tbl is not None:
                    cur_tbl[0] = op.tbl
            for u in op.users:
                u.nrem -= 1
                t = op.fin + (0.0 if u.eng == e and op.dma is None else 120.0)
                if t > u.ready:
                    u.ready = t
                    u.rdma = op.dma is not None
            done += 1
        self.ops = new
        self.sim_time = max(free.values())
        import os as _os
        if _os.environ.get("KDEBUG"):
            ph = {}
            for e in self.ENGS:
                for op in new[e]:
                    d = ph.setdefault(op.phase, {})
                    a = d.setdefault(e, [1e18, 0.0, 0.0])
                    a[0] = min(a[0], op.fin - op.cost)
                    a[1] = max(a[1], op.fin)
                    a[2] += op.cost if op.dma is None else 0.0
            for p, d in ph.items():
                print(f"{p:10s}", "  ".join(f"{e}:{a[0] / 1e3:7.0f}-{a[1] / 1e3:7.0f} busy{a[2] / 1e3:6.0f}" for e, a in d.items()))

    def finalize(self):
        for e in self.ENGS:
            c = 0
            for op in self.ops[e]:
                if op.dma is None and op.sig:
                    c += 1
                    op.sigidx = c

    def check(self):
        ptr = {e: 0 for e in self.ENGS}
        cnt = {("e", e): 0 for e in self.ENGS}
        seen = {e: {} for e in self.ENGS}
        total = sum(len(v) for v in self.ops.values())
        done = 0
        while done < total:
            prog = False
            for e in self.ENGS:
                while ptr[e] < len(self.ops[e]):
                    op = self.ops[e][ptr[e]]
                    ok = True
                    for d in op.deps:
                        if cnt[("e", d.eng)] < d.sigidx:
                            ok = False
                    for (sname, v) in op.dwaits:
                        if cnt.get(("d", sname), 0) < v:
                            ok = False
                    if not ok:
                        break
                    if op.dma is not None:
                        cnt[("d", op.dma)] = cnt.get(("d", op.dma), 0) + 16
                    elif op.sig:
                        cnt[("e", e)] += 1
                        assert cnt[("e", e)] == op.sigidx
                    ptr[e] += 1
                    done += 1
                    prog = True
            if not prog:
                for e in self.ENGS:
                    if ptr[e] < len(self.ops[e]):
                        op = self.ops[e][ptr[e]]
                        print("BLOCKED", e, ptr[e], op.phase, [(d.eng, d.sigidx, cnt[("e", d.eng)]) for d in op.deps],
                              [(s_, v, cnt.get(("d", s_), 0)) for s_, v in op.dwaits])
                raise AssertionError("abstract deadlock")
        return True

    def emit(self, eng, handle, esem, dsem):
        seen = {}
        for op in self.ops[eng]:
            waits = {}
            for d in op.deps:
                k = ("e", d.eng)
                waits[k] = max(waits.get(k, 0), d.sigidx)
            for (s, v) in op.dwaits:
                k = ("d", s)
                waits[k] = max(waits.get(k, 0), v)
            for k, v in waits.items():
                if seen.get(k, 0) >= v:
                    continue
                seen[k] = v
                sem = esem[k[1]] if k[0] == "e" else dsem[k[1]]
                handle.wait_ge(sem, v)
            ins = op.fn(handle)
            if op.dma is not None:
                ins.then_inc(dsem[op.dma], 16)
            elif op.sig:
                ins.then_inc(esem[eng], 1)


def ap_of(base, off, dims):
    return AP(base.tensor, base.offset + off, [list(base.ap[0])] + [list(d) for d in dims])


def build_nc():
    nc = bass.Bass("TRN2", target_bir_lowering=False)
    S = Sched()

    def din(name, shape):
        return nc.dram_tensor(name, list(shape), F32, kind="ExternalInput").ap()

    def dout(name, shape):
        return nc.dram_tensor(name, list(shape), F32, kind="ExternalOutput").ap()

    xin = {"P": din("xp", [1024, D]), "S": din("xs", [1024, D])}
    cvec = din("cvec", [2, D])
    ck_d = din("ck", [2, 256, 128])
    cv_d = din("cv", [2, 256, 128])
    st_d = din("st", [2, 2, 256])
    w_ada = din("w_ada", [2, D, 6 * D])
    b_ada = din("b_ada", [2, 6 * D])
    w_in = din("w_in", [2, D, 1792])
    q_g = din("q_norm_g", [2, 64])
    k_g = din("k_norm_g", [2, 64])
    conv_w = din("conv_w", [2, 4, 256])
    conv_b = din("conv_b", [2, 256])
    lru_wa = din("lru_wa", [2, 2, 4, 64, 64])
    lru_ba = din("lru_ba", [2, 2, 256])
    lru_wx = din("lru_wx", [2, 2, 4, 64, 64])
    lru_bx = din("lru_bx", [2, 2, 256])
    lru_lam = din("lru_lam", [2, 2, 256])
    mlp_g = din("mlp_norm_g", [2, 256])
    mlp_b = din("mlp_norm_b", [2, 256])
    mlp_ws = din("mlp_ws", [2, 4, 128, 128])
    mlp_bs = din("mlp_bs", [2, 4, 128])
    w_out = din("w_out", [2, D, D])
    ln_gb = {(1, 0): din("ln1_g", [2, D]), (1, 1): din("ln1_b", [2, D]),
             (2, 0): din("ln2_g", [2, D]), (2, 1): din("ln2_b", [2, D])}
    w_ff1 = din("w_ff1", [2, D, 4 * D])
    b_ff1 = din("b_ff1", [2, 4 * D])
    w_ff2 = din("w_ff2", [2, 4 * D, D])
    b_ff2 = din("b_ff2", [2, D])
    ident_d = din("ident", [128, 128])
    rope_d = din("rope", [1024, 64])

    yout = {"P": dout("yp", [1024, D]), "S": dout("ys", [1024, D])}
    nk_d = dout("nk", [4, 2, 256, 128])
    nv_d = dout("nv", [4, 2, 256, 128])
    ns_d = dout("ns", [4, 2, 2, 256])
    modD = nc.dram_tensor("modD", [2, 2, 6 * D], F32).ap()

    es = contextlib.ExitStack()
    with es:
        def sb(name, shape, dt):
            return es.enter_context(nc.sbuf_tensor(name, list(shape), dt))

        X = sb("X", [128, 8, D], F32)
        HT = sb("HT", [128, 8, 1024], BF16)
        MH = sb("MH", [128, 8, 1024], BF16)
        WB = [sb(f"WB{i}", [128, 4096], BF16) for i in range(NWB)]
        G1 = sb("G1", [128, D], F32)
        G2 = sb("G2", [128, D], F32)
        LNA = sb("LNA", [128, D], F32)
        LNB = sb("LNB", [128, D], F32)
        QT = sb("QT", [128, 4, 1024], BF16)
        KT = sb("KT", [128, 1280], BF16)
        V2 = sb("V2", [128, 10, 256], BF16)
        PT2 = [sb(f"PT{i}", [128, 1024], BF16) for i in range(2)]
        XR = sb("XR", [128, 2, 1024], F32)
        GG = sb("GG", [128, 2, 1024], BF16)
        UT = sb("UT", [128, 2, 1024], BF16)
        VN = sb("VN", [128, 8, 256], BF16)
        FS = [sb(f"FS{i}", [128, 512], F32) for i in range(3)]
        TG = sb("TG", [128, 1024], F32)
        QBS = [sb(f"QB{i}", [128, 512], BF16) for i in range(2)]
        KF = [sb(f"KF{i}", [128, 128], F32) for i in range(2)]
        VF = [sb(f"VF{i}", [128, 128], F32) for i in range(2)]
        KB = sb("KB", [128, 128], BF16)
        XHB = sb("XHB", [128, 1024], BF16)
        STTA = sb("STTA", [128, 8, 32], F32)
        MSB = sb("MSB", [2, 1024], F32)
        IDB = sb("IDB", [128, 128], BF16)
        ONES = sb("ONES", [128, 128], BF16)
        ROPE = sb("ROPE", [128, 8, 64], F32)
        CVF = sb("CVF", [128, 8, 2], F32)
        SCV = sb("SCV", [128, 8, 2], BF16)
        GQ = sb("GQ", [128, 2, 64], F32)
        GK8 = sb("GK8", [128, 2, 64], F32)
        CW = sb("CW", [128, 2, 4, 2], F32)
        CB = sb("CB", [128, 2, 2], F32)
        BD = sb("BD", [128, 2, 2, 2, 2, 128], BF16)
        LBA = sb("LBA", [128, 2, 2, 2, 2], F32)
        CL = sb("CL", [128, 2, 2, 2], F32)
        MG = sb("MG", [128, 2, 256], F32)
        MBt = sb("MBt", [128, 2, 256], F32)
        WST = sb("WST", [128, 2, 4, 128], BF16)
        BSB = sb("BSB", [128, 2, 2, 128], F32)
        B1 = sb("B1", [128, 2, 32], F32)
        STI = sb("STI", [128, 2, 2, 2], F32)
        LNF = sb("LNF", [128, 2, 4, 8], F32)
        MSC = sb("MSC", [128, 2, 8, 8], F32)
        GCB = sb("GCB", [128, 2, 4, 8], F32)
        MS = MSB[:, 0:512]
        BA = MSB[:, 512:1024]
        stt_n = [0]

        def stt_next():
            i = stt_n[0] % 8
            stt_n[0] += 1
            return STTA[:, i, 0:16], [f"STT{i}"]
        NSS = sb("NSS", [128, 2, 2, 4], F32)
        EPSB = sb("EPSB", [128, 32], F32)
        FILL = sb("FILL", [128, 512], BF16)

        PS2 = [es.enter_context(nc.psum_tensor(f"PS{i}", [128, 1024], F32)) for i in range(4)]
        PSB = [PS2[b // 2][:, (b % 2) * 512:(b % 2 + 1) * 512] for b in range(8)]

        ps_free = list(range(7))

        def ps_get():
            assert ps_free, "out of PSUM banks"
            return ps_free.pop(0)

        def ps_put(b):
            ps_free.append(b)

        def pk(b):
            return [f"ps{b}"]

        wb_state = {"n": 0}

        def wb_load(parts):
            i = wb_state["n"] % NWB
            wb_state["n"] += 1
            for (off, dims, src) in parts:
                dst = ap_of(WB[i][:, 0:1], off, dims)
                nel = 128
                for d_ in dims:
                    nel *= d_[1]
                S.add("pool", (lambda e, dst=dst, src=src: e.dma_start(out=dst, in_=src)),
                      w=[f"WB{i}"], dma=f"wb{i}", cost=nel * 4 / 180.0)
            return i

        def wsrc(wt, l, r0, c0, nk, ncol):
            return wt[l, r0:r0 + nk * 128, c0:c0 + ncol].rearrange("(k p) c -> p k c", p=128)

        def dve(fn, r, w, n=512):
            return S.add("dve", fn, r, w, cost=70.0 + 1.3 * n)

        def act(fn, r, w, n=512, tbl=None):
            return S.add("act", fn, r, w, cost=240.0 + 0.7 * n, tbl=tbl)

        def pe(fn, r, w, n=512):
            return S.add("pe", fn, r, w, cost=12.0 + 0.40 * n)

        def spdma(out, in_, r, w, sem, nbytes=65536):
            return S.add("sp", (lambda e: e.dma_start(out=out, in_=in_, allow_slow_non_contiguous=True)),
                         r, w, dma=sem, cost=nbytes / 180.0)

        def pooldma(out, in_, r, w, sem, nbytes=16384, issue=None):
            op = S.add("pool", (lambda e: e.dma_start(out=out, in_=in_, allow_slow_non_contiguous=True)),
                       r, w, dma=sem, cost=nbytes / 180.0)
            op.issue = issue
            return op

        def htk(ks, ts):
            return [f"HT{k}_{t}" for k in ks for t in ts]

        def mhk(ks, ts):
            return [f"MH{k}_{t}" for k in ks for t in ts]

        def qtk(js, ts):
            return [f"QT{j}_{t}" for j in js for t in ts]

        ALL8 = list(range(8))

        def lbuf(m):
            return HT[:, 2 * m:2 * m + 2, :].rearrange("p a b -> p (a b)").bitcast(F32)

        LA_, LI_, LT_, LH_ = lbuf(0), lbuf(1), lbuf(2), lbuf(3)
        LKEY = [htk([2 * m, 2 * m + 1], ALL8) for m in range(4)]

        def xc_ap(c):
            return QT[:, 2 * c:2 * c + 2, :].rearrange("p a b -> p (a b)").bitcast(F32)

        def xc_key(c):
            return qtk([2 * c, 2 * c + 1], ALL8)

        V2flat = V2[:, :, :].rearrange("p a b -> p (a b)")

        def xcb_ap(c):
            return V2flat[:, c * 1024:(c + 1) * 1024]

        def xcb_key(c):
            return [f"V2_{tt}" for tt in range(4 * c, 4 * c + 4)]

        for t in range(8):
            spdma(X[:, t, :], xin["P"][t * 128:(t + 1) * 128, :], [], [f"X{t}"], "xl", nbytes=524288)
        pooldma(IDB[:, :], ident_d[:, :], [], ["IDB"], "idb")
        for c_ in range(2):
            spdma(CVF[:, :, c_], cvec[c_].rearrange("(k p) -> p k", p=128), [], ["CVF"], "cvf", nbytes=4096)
        act(lambda e: e.activation(out=SCV[:, :, :], in_=CVF[:, :, :], func=AF.Silu), ["CVF"], ["SCV"], n=16)
        def mod_block(l, blk):
            i = wb_load([(0, [[512, 8], [1, 512]], wsrc(w_ada, l, 0, blk * 512, 8, 512))])
            spdma(BA, AP(b_ada.tensor, b_ada[l, blk * 512:blk * 512 + 1].offset, [[0, 2], [1, 512]]),
                  [], ["BA"], "ba", nbytes=4096)
            b = ps_get()
            for k in range(8):
                pe(lambda e, k=k, i=i, b=b: e.matmul(PSB[b][0:2, :], lhsT=SCV[:, k, :],
                                                     rhs=WB[i][:, k * 512:(k + 1) * 512],
                                                     start=(k == 0), stop=(k == 7)),
                   ["SCV", f"WB{i}"], pk(b))
            dve(lambda e, b=b: e.tensor_tensor(out=MS, in0=PSB[b][0:2, :], in1=BA, op=ALU.add),
                pk(b) + ["BA"], ["MS"])
            ps_put(b)
            spdma(modD[l, :, blk * 512:(blk + 1) * 512], MS, ["MS"], [f"modD{l}_{blk // 2}"], "ms", nbytes=4096)

        bg = [(0, blk) for blk in range(4, 12)] + [(1, blk) for blk in range(12)]

        def bg_step(n=1):
            for _ in range(n):
                if bg:
                    mod_block(*bg.pop(0))


        for blk in range(4):
            mod_block(0, blk)
        dve(lambda e: e.memset(ONES[:, :], 1.0), [], ["ONES"])
        fill_ms = dve(lambda e: e.memset(FILL[:, :], 0.5), [], ["FILL"])
        fill_ms.sig = True
        S.filler = (lambda e: e.matmul(PSB[7][:, :], lhsT=ONES[:, :], rhs=FILL[:, :], start=True, stop=True), fill_ms)
        dve(lambda e: e.memset(EPSB[:, 16:17], EPS), [], ["EPSB"])
        dve(lambda e: e.memset(EPSB[:, 17:18], 64 * EPS), [], ["EPSB"])
        for l in range(2):
            spdma(GQ[:, l, :], AP(q_g.tensor, q_g[l, 0:1].offset, [[0, 128], [1, 64]]), [], ["GQ"], "sm", nbytes=32768)
            spdma(GK8[:, l, :], AP(k_g.tensor, k_g[l, 0:1].offset, [[0, 128], [1, 64]]), [], ["GK8"], "sm", nbytes=32768)
            spdma(MG[:, l, :], AP(mlp_g.tensor, mlp_g[l, 0:1].offset, [[0, 128], [1, 256]]), [], ["MG"], "sm")
            spdma(MBt[:, l, :], AP(mlp_b.tensor, mlp_b[l, 0:1].offset, [[0, 128], [1, 256]]), [], ["MBt"], "sm")
        dve(lambda e: e.tensor_scalar(out=GK8[:, :, :], in0=GK8[:, :, :], scalar1=8.0, scalar2=None, op0=ALU.mult),
            ["GK8"], ["GK8"], n=128)

        def load_bd(l, d, gi, wt):
            for n in range(4):
                c, h = n // 2, n % 2
                pooldma(BD[h * 64:(h + 1) * 64, l, d, gi, c, h * 64:(h + 1) * 64],
                        wt[l, d, n, :, :], ["BD"], [f"BDx{l}"], f"bd{l}", issue=4000.0)

        def deferred_bd1():
            for d in range(2):
                for gi, wt in enumerate((lru_wa, lru_wx)):
                    load_bd(1, d, gi, wt)

        def deferred_cl():
            act(lambda e: e.activation(out=CL[:, :, :, :], in_=CL[:, :, :, :], func=AF.Exp, scale=-1.0), ["CL"], ["CL"],
                n=8, tbl="explog")
            act(lambda e: e.activation(out=CL[:, :, :, :], in_=CL[:, :, :, :], func=AF.Ln, bias=1.0), ["CL"], ["CL"],
                n=8, tbl="explog")
            dve(lambda e: e.tensor_scalar(out=CL[:, :, :, :], in0=CL[:, :, :, :], scalar1=-8.0, scalar2=None,
                                          op0=ALU.mult), ["CL"], ["CL"], n=8)

        def deferred_params():
            dve(lambda e: e.memset(BD[:, :, :, :, :, :].rearrange("p a b c d f -> p (a b c d f)"), 0.0), [], ["BD"],
                n=2048)
            spdma(ROPE[:, :, :], rope_d.rearrange("(t p) c -> p t c", p=128), [], ["ROPE"], "sm3", nbytes=8192)
            for l in range(2):
                for j_ in range(4):
                    spdma(CW[:, l, j_, :], conv_w[l, j_].rearrange("(c p) -> p c", p=128), [], ["CW"], "sm3", nbytes=8192)
                spdma(CB[:, l, :], conv_b[l].rearrange("(c p) -> p c", p=128), [], ["CB"], "sm3", nbytes=8192)
                for d in range(2):
                    for gi, (wt, bt) in enumerate(((lru_wa, lru_ba), (lru_wx, lru_bx))):
                        spdma(LBA[:, l, d, gi, :], bt[l, d].rearrange("(c p) -> p c", p=128), [], ["LBA"], "sm3", nbytes=8192)
                        if l == 0:
                            load_bd(l, d, gi, wt)
                    spdma(CL[:, l, d, :], lru_lam[l, d].rearrange("(c p) -> p c", p=128), [], ["CL"], "cl", nbytes=8192)
                    spdma(STI[:, l, d, :], st_d[l, d].rearrange("(c p) -> p c", p=128), [], ["STI"], "sm3", nbytes=8192)
                for g in range(4):
                    gi, c2 = g % 2, g // 2
                    spdma(BSB[gi * 64:(gi + 1) * 64, l, c2, :],
                          AP(mlp_bs.tensor, mlp_bs[l, g, 0:1].offset, [[0, 64], [1, 128]]), [], ["BSB"], "sm3", nbytes=8192)
                spdma(B1[:, l, :], b_ff1[l].rearrange("(c p) -> p c", p=128), [], ["B1"], "sm3", nbytes=8192)
                for j, key in enumerate(((1, 0), (1, 1), (2, 0), (2, 1))):
                    spdma(LNF[:, l, j, :], ln_gb[key][l].rearrange("(c p) -> p c", p=128), [], ["LNF"], "sm3", nbytes=8192)
            for l in range(2):
                i = wb_load([(0, [[128, 4], [1, 128]], mlp_ws[l].rearrange("g p q -> p g q"))])
                b = ps_get()
                pst = PSB[b][:, :].bitcast(BF16)
                for g in range(4):
                    pe(lambda e, g=g, i=i, pst=pst: e.transpose(out=pst[:, g * 128:(g + 1) * 128],
                                                                in_=WB[i][:, g * 128:(g + 1) * 128], identity=IDB[:, :]),
                       [f"WB{i}", "IDB"], pk(b), n=128)
                dve(lambda e, l=l, pst=pst: e.tensor_copy(out=WST[:, l, :, :].rearrange("p g q -> p (g q)"),
                                                           in_=pst[:, 0:512]), pk(b), ["WST"])
                ps_put(b)

        def load_msc(l, cond, half):
            lp = l % 2
            key = [f"MSC{lp}{half}"]
            for j, idx in (((0, 0), (1, 1)) if half == 0 else ((4, 3), (5, 4))):
                spdma(MSC[:, lp, j, :], modD[l, cond, idx * D:(idx + 1) * D].rearrange("(c p) -> p c", p=128),
                      [f"modD{l}_{idx}"], key, f"msc{lp}{half}", nbytes=4096)
            j = 1 if half == 0 else 5
            dve(lambda e: e.tensor_scalar(out=MSC[:, lp, j, :], in0=MSC[:, lp, j, :], scalar1=1.0,
                                          scalar2=None, op0=ALU.add), key, key, n=8)

        def bcast_row(dst, src_t, off_ap, key_r, key_w, sem):
            spdma(dst[:, :], AP(src_t.tensor, off_ap.offset, [[0, 128], [1, D]]), key_r, key_w, sem)

        def nmt_a1(t):
            xk = [f"X{t}"]
            xt = X[:, t, :]
            st, sk = stt_next()
            dve(lambda e: e.bn_stats(out=st[:, 0:6], in_=xt[:, 0:512]), xk, sk, n=450)
            dve(lambda e: e.bn_stats(out=st[:, 6:12], in_=xt[:, 512:1024]), xk, sk, n=450)
            dve(lambda e: e.bn_aggr(out=st[:, 12:14], in_=st[:, 0:12]), sk, sk, n=100)
            return st, sk

        def nmt_a2(t, stsk, final_store=None):
            st, sk = stsk
            xk = [f"X{t}"]
            xt = X[:, t, :]
            act(lambda e: e.activation(out=st[:, 14:15], in_=st[:, 13:14], func=AF.Sqrt, bias=EPSB[:, 16:17]),
                sk + ["EPSB"], sk, n=60, tbl="sqrt")
            dve(lambda e: e.reciprocal(out=st[:, 14:15], in_=st[:, 14:15]), sk, sk, n=80)
            if final_store is None:
                dve(lambda e: e.tensor_scalar(out=st[:, 15:16], in0=st[:, 12:13], scalar1=st[:, 14:15], scalar2=-1.0,
                                              op0=ALU.mult, op1=ALU.mult), sk, sk, n=8)
                act(lambda e: e.activation(out=XHB[:, :], in_=xt, func=AF.Identity, scale=st[:, 14:15],
                                           bias=st[:, 15:16]), xk + sk, ["XHB"], n=1024)
            dve(lambda e: e.scalar_tensor_tensor(out=xt, in0=xt, scalar=st[:, 12:13], in1=LNA[:, :],
                                                 op0=ALU.subtract, op1=ALU.mult), xk + sk + ["LNA"], xk, n=900)
            dve(lambda e: e.scalar_tensor_tensor(out=xt, in0=xt, scalar=st[:, 14:15], in1=LNB[:, :],
                                                 op0=ALU.mult, op1=ALU.add), xk + sk + ["LNB"], xk, n=900)
            if final_store is not None:
                spdma(final_store, xt, xk, [], "yst", nbytes=524288)

        def nmt_b(t, which):
            b = ps_get()
            pst = PSB[b][:, :].bitcast(BF16)
            for k in range(8):
                pe(lambda e, k=k, pst=pst: e.transpose(out=pst[:, k * 128:(k + 1) * 128],
                                                       in_=XHB[:, k * 128:(k + 1) * 128], identity=IDB[:, :]),
                   ["XHB", "IDB"], pk(b), n=128)
            for k in range(8):
                act(lambda e, k=k, pst=pst: e.activation(out=HT[:, k, t * 128:(t + 1) * 128],
                                                         in_=pst[:, k * 128:(k + 1) * 128], func=AF.Identity,
                                                         scale=GCB[:, which, 0, k:k + 1],
                                                         bias=GCB[:, which, 1, k:k + 1]),
                    pk(b) + [f"GCB{which}"], htk([k], [t]), n=128)
            ps_put(b)

        def nmt(t, l, which, norm, la_lb, final_store=None):
            xk = [f"X{t}"]
            xt = X[:, t, :]
            if norm:
                nmt_a2(t, nmt_a1(t), final_store)
            else:
                act(lambda e: e.copy(out=XHB[:, :], in_=xt), xk, ["XHB"], n=1024)
                dve(lambda e: e.tensor_scalar(out=xt, in0=xt, scalar1=ALPHA, scalar2=None, op0=ALU.mult), xk, xk, n=600)
            if final_store is None:
                nmt_b(t, which)

        def qk_norm1(src_ps, nh, bkeys, sq, sqk):
            n = nh * 64
            st, sk = stt_next()
            act(lambda e: e.activation(out=sq[:, 0:n], in_=src_ps, func=AF.Square), bkeys, sqk, n=n)
            dve(lambda e: e.tensor_reduce(out=st[:, 0:nh], in_=sq[:, 0:n].rearrange("p (h d) -> p h d", h=nh),
                                          axis=AX.X, op=ALU.add), sqk, sk, n=n)
            return st, sk

        def qk_norm2(stsk, src_ps, nh, dst_f, gtile, bkeys, dkeys, gkey, out_ap=None, okeys=None):
            st, sk = stsk
            n = nh * 64
            act(lambda e: e.activation(out=st[:, 0:nh], in_=st[:, 0:nh], func=AF.Sqrt, bias=EPSB[:, 17:18]),
                sk + ["EPSB"], sk, n=60, tbl="sqrt")
            dve(lambda e: e.reciprocal(out=st[:, 0:nh], in_=st[:, 0:nh]), sk, sk, n=80)
            dve(lambda e: e.tensor_tensor(out=dst_f.rearrange("p (h d) -> p h d", h=nh),
                                          in0=src_ps.rearrange("p (h d) -> p h d", h=nh),
                                          in1=ap_of(st[:, 0:1], 0, [[1, nh], [0, 64]]), op=ALU.mult),
                bkeys + sk, dkeys, n=n)
            if out_ap is None:
                d3 = dst_f.rearrange("p (h d) -> p h d", h=nh)
                g3 = ap_of(gtile[:, 0:1], 0, [[0, nh], [1, 64]])
                dve(lambda e: e.tensor_tensor(out=d3, in0=d3, in1=g3, op=ALU.mult), dkeys + [gkey], dkeys, n=n)
            else:
                o_, i0_, i1_ = out_ap
                dve(lambda e: e.tensor_tensor(out=o_, in0=i0_, in1=i1_, op=ALU.mult), dkeys + [gkey], okeys, n=n)

        def rope(src_f, nh, t, dst_b, dst_dims_even, skeys, dkeys):
            T = [TG[:, i * 256:i * 256 + nh * 32].rearrange("p (h i) -> p h i", h=nh) for i in range(4)]
            x1 = ap_of(src_f[:, 0:1], 0, [[64, nh], [2, 32]])
            x2 = ap_of(src_f[:, 0:1], 1, [[64, nh], [2, 32]])
            cs = ap_of(ROPE[:, t, 0:1], 0, [[0, nh], [1, 32]])
            sn = ap_of(ROPE[:, t, 0:1], 32, [[0, nh], [1, 32]])
            dve(lambda e: e.tensor_tensor(out=T[0], in0=x1, in1=cs, op=ALU.mult), skeys + ["ROPE"], ["TG"])
            dve(lambda e: e.tensor_tensor(out=T[1], in0=x2, in1=sn, op=ALU.mult), skeys + ["ROPE"], ["TG"])
            dve(lambda e: e.tensor_tensor(out=T[2], in0=x1, in1=sn, op=ALU.mult), skeys + ["ROPE"], ["TG"])
            dve(lambda e: e.tensor_tensor(out=T[3], in0=x2, in1=cs, op=ALU.mult), skeys + ["ROPE"], ["TG"])
            if nh == 8:
                Tv = [ap_of(TG[:, 0:1], i * 256, [[128, 2], [32, 4], [1, 32]]) for i in range(4)]
            else:
                Tv = T
            de = ap_of(dst_b, 0, dst_dims_even)
            do = ap_of(dst_b, 1, dst_dims_even)
            dve(lambda e: e.tensor_tensor(out=de, in0=Tv[0], in1=Tv[1], op=ALU.subtract), ["TG"], dkeys)
            dve(lambda e: e.tensor_tensor(out=do, in0=Tv[2], in1=Tv[3], op=ALU.add), ["TG"], dkeys)

        class _Stop(Exception):
            pass

        import os as _os2
        _stop = _os2.environ.get("K_STOP")

        def stop_at(name):
            if _stop == name:
                raise _Stop()

        def group_layer(grp, l, first, last):
            is_s = grp == "S"
            cond = 1 if is_s else 0
            lp = l % 2
            nseq = 1 if is_s else 4
            L = 1024 // nseq
            ktoff = 256 if is_s else 0
            vtoff = 2 if is_s else 0

            dve(lambda e: e.memset(ap_of(V2[:, 0, 0:1], 64, [[128, 20], [1, 64]]), 1.0), [],
                [f"V2_{tt}" for tt in range(10)], n=1280)
            if first:
                load_msc(l, cond, 0)
                if is_s:
                    for t in range(8):
                        spdma(X[:, t, :], xin[grp][t * 128:(t + 1) * 128, :], [], [f"X{t}"], f"xs{t}", nbytes=524288)
                dve(lambda e: e.tensor_copy(out=GCB[:, 0, 0, :], in_=MSC[:, lp, 1, :]), [f"MSC{lp}0"], ["GCB0"], n=8)
                dve(lambda e: e.tensor_copy(out=GCB[:, 0, 1, :], in_=MSC[:, lp, 0, :]), [f"MSC{lp}0"], ["GCB0"], n=8)
            if is_s:
                for tt in range(2):
                    spdma(KF[tt][:, :], ck_d[l, tt * 128:(tt + 1) * 128, :], [], [f"KF{tt}"], f"kst{tt}")
                    spdma(VF[tt][:, :], cv_d[l, tt * 128:(tt + 1) * 128, :], [], [f"VF{tt}"], f"vst{tt}")

            def late_setup():
                mk = [f"MSC{lp}1"]
                load_msc(l, cond, 1)
                bcast_row(G1, modD, modD[l, cond, 2 * D:2 * D + 1], [f"modD{l}_2"], ["G1"], "g1")
                dve(lambda e: e.tensor_tensor(out=GCB[:, 1, 0, :], in0=LNF[:, l, 0, :], in1=MSC[:, lp, 5, :],
                                              op=ALU.mult), ["LNF"] + mk, ["GCB1"], n=8)
                dve(lambda e: e.tensor_tensor(out=GCB[:, 1, 1, :], in0=LNF[:, l, 1, :], in1=MSC[:, lp, 5, :],
                                              op=ALU.mult), ["LNF"] + mk, ["GCB1"], n=8)
                dve(lambda e: e.tensor_tensor(out=GCB[:, 1, 1, :], in0=GCB[:, 1, 1, :], in1=MSC[:, lp, 4, :],
                                              op=ALU.add), ["GCB1"] + mk, ["GCB1"], n=8)
                bcast_row(G2, modD, modD[l, cond, 5 * D:5 * D + 1], [f"modD{l}_5"], ["G2"], "g2")
                bcast_row(LNA, ln_gb[(1, 0)], ln_gb[(1, 0)][l, 0:1], [], ["LNA"], "lna")
                bcast_row(LNB, ln_gb[(1, 1)], ln_gb[(1, 1)][l, 0:1], [], ["LNB"], "lnb")
                bcast_row(TG, b_ff2, b_ff2[l, 0:1], [], ["TG", "TGv0", "TGv1"], "lnc")
                dve(lambda e: e.tensor_scalar(out=LNA[:, :], in0=LNA[:, :], scalar1=ALPHA, scalar2=None, op0=ALU.mult),
                    ["LNA"], ["LNA"], n=1024)
                dve(lambda e: e.tensor_tensor(out=TG[:, :], in0=TG[:, :], in1=G2[:, :], op=ALU.mult), ["TG", "G2"],
                    ["TG"], n=1024)
                dve(lambda e: e.scalar_tensor_tensor(out=LNB[:, :], in0=LNB[:, :], scalar=ALPHA, in1=TG[:, :],
                                                     op0=ALU.mult, op1=ALU.add), ["LNB", "TG"], ["LNB"], n=1024)

            stop_at(f"{grp}{l}:start")
            S.phase = f"{grp}{l}:win"
            if first:
                for t in range(2):
                    nmt(t, l, 0, norm=False, la_lb=False)

            stop_at(f"{grp}{l}:h1")
            iA = wb_load([(0, [[512, 8], [1, 512]], wsrc(w_in, l, 0, 0, 8, 512))])
            iB = wb_load([(0, [[512, 8], [1, 256]], wsrc(w_in, l, 0, 512, 8, 256)),
                          (256, [[512, 8], [1, 256]], wsrc(w_in, l, 0, 1536, 8, 256))])
            iC = wb_load([(0, [[512, 8], [1, 512]], wsrc(w_in, l, 0, 768, 8, 512))])
            iD = wb_load([(0, [[512, 8], [1, 256]], wsrc(w_in, l, 0, 1280, 8, 256))])

            if is_s:
                for tt in range(2):
                    act(lambda e, tt=tt: e.copy(out=KB[:, :], in_=KF[tt][:, :]), [f"KF{tt}"], ["KB"])
                    b = ps_get()
                    pst = PSB[b][:, :].bitcast(BF16)
                    pe(lambda e, pst=pst: e.transpose(out=pst[:, 0:128], in_=KB[:, :], identity=IDB[:, :]),
                       ["KB", "IDB"], pk(b), n=128)
                    dve(lambda e, tt=tt, pst=pst: e.tensor_copy(out=KT[:, tt * 128:(tt + 1) * 128], in_=pst[:, 0:128]),
                        pk(b), [f"KT{tt}"])
                    ps_put(b)
                    dve(lambda e, tt=tt: e.tensor_copy(
                        out=ap_of(V2[:, tt, 0:1], 0, [[128, 2], [1, 64]]),
                        in_=VF[tt][:, :].rearrange("p (k d) -> p k d", k=2)), [f"VF{tt}"], [f"V2_{tt}"], n=128)

            pm_dims = [[64, 2], [128, 4], [1, 64]]
            nat_dims = [[256, 2], [64, 4], [1, 64]]
            qst = {}

            def q_a1(t):
                if first and t + 2 < 8:
                    nmt(t + 2, l, 0, norm=False, la_lb=False)
                b = ps_get()
                for k in range(8):
                    pe(lambda e, k=k, b=b, t=t: e.matmul(PSB[b][:, :], lhsT=HT[:, k, t * 128:(t + 1) * 128],
                                                         rhs=WB[iA][:, k * 512:(k + 1) * 512],
                                                         start=(k == 0), stop=(k == 7)),
                       htk([k], [t]) + [f"WB{iA}"], pk(b))
                qst[t] = (b, qk_norm1(PSB[b][:, :], 8, pk(b), FS[2], ["FS2"]))

            def q_a2(t):
                b, stsk = qst[t]
                qf = FS[t % 2]
                qfk = [f"FS{t % 2}"]
                QB = QBS[t % 2]
                qbk = [f"QB{t % 2}"]
                if is_s:
                    qk_norm2(stsk, PSB[b][:, :], 8, qf[:, :], GQ[:, l, :], pk(b), qfk, "GQ")
                    ps_put(b)
                    rope(qf[:, :], 8, t, QB[:, 0:1], [[64, 2], [128, 4], [2, 32]], qfk, qbk)
                else:
                    qk_norm2(stsk, PSB[b][:, :], 8, qf[:, :], GQ[:, l, :], pk(b), qfk, "GQ",
                             out_ap=(ap_of(QB[:, 0:1], 0, pm_dims), ap_of(qf[:, 0:1], 0, nat_dims),
                                     ap_of(GQ[:, l, 0:1], 0, [[0, 2], [0, 4], [1, 64]])), okeys=qbk)
                    ps_put(b)

            def q_b(t):
                QB = QBS[t % 2]
                qbk = [f"QB{t % 2}"]
                b2 = ps_get()
                pst = PSB[b2][:, :].bitcast(BF16)
                for j in range(4):
                    pe(lambda e, j=j, pst=pst, QB=QB: e.transpose(out=pst[:, j * 128:(j + 1) * 128],
                                                                  in_=QB[:, j * 128:(j + 1) * 128], identity=IDB[:, :]),
                       qbk + ["IDB"], pk(b2), n=128)
                act(lambda e, t=t, pst=pst: e.copy(out=QT[:, :, t * 128:(t + 1) * 128],
                                                   in_=pst[:, 0:512].rearrange("p (j q) -> p j q", j=4)),
                    pk(b2), qtk(range(4), [t]))
                ps_put(b2)

            kst = {}

            def kv_a1(t):
                b = ps_get()
                for k in range(8):
                    pe(lambda e, k=k, b=b, t=t: e.matmul(PSB[b][:, :], lhsT=HT[:, k, t * 128:(t + 1) * 128],
                                                         rhs=WB[iB][:, k * 512:(k + 1) * 512],
                                                         start=(k == 0), stop=(k == 7)),
                       htk([k], [t]) + [f"WB{iB}"], pk(b))
                sq = FS[2][:, (t % 2) * 128:(t % 2 + 1) * 128]
                stsk = qk_norm1(PSB[b][:, 0:128], 2, pk(b), sq, [f"FS2k{t % 2}", "FS2"])
                tv = vtoff + t
                if not is_s:
                    vf = VF[t % 2]
                    act(lambda e, vf=vf, b=b: e.copy(out=vf[:, :], in_=PSB[b][:, 128:256]), pk(b), [f"VF{t % 2}"], n=128)
                    s_, tt = t // 2, t % 2
                    spdma(nv_d[s_, l, tt * 128:(tt + 1) * 128, :], vf[:, :], [f"VF{t % 2}"], [], f"vst{t % 2}")
                dve(lambda e, tv=tv, b=b: e.tensor_copy(
                    out=ap_of(V2[:, tv, 0:1], 0, [[128, 2], [1, 64]]),
                    in_=PSB[b][:, 128:256].rearrange("p (k d) -> p k d", k=2)), pk(b), [f"V2_{tv}"], n=128)
                vg = TG[:, (t % 2) * 256:(t % 2 + 1) * 256] if not is_s else FS[t % 2][:, 0:256]
                vgk = [f"TGv{t % 2}"] if not is_s else [f"FS{t % 2}"]
                act(lambda e, b=b, vg=vg: e.activation(out=vg, in_=PSB[b][:, 256:512], func=AF.Gelu_apprx_tanh),
                    pk(b), vgk, n=256, tbl="gelu")
                st2, sk2 = stt_next()
                dve(lambda e, st2=st2, vg=vg: e.bn_stats(out=st2[:, 0:6], in_=vg), vgk, sk2, n=256)
                dve(lambda e, st2=st2: e.bn_aggr(out=st2[:, 12:14], in_=st2[:, 0:6]), sk2, sk2, n=60)
                kst[t] = (b, stsk, vg, vgk, st2, sk2)

            def kv_a2(t):
                b, stsk, vg, vgk, st2, sk2 = kst[t]
                kf = KF[t % 2]
                kfk = [f"KF{t % 2}"]
                qk_norm2(stsk, PSB[b][:, 0:128], 2, kf[:, :], GK8[:, l, :], pk(b), kfk, "GK8")
                ps_put(b)
                act(lambda e, st2=st2: e.activation(out=st2[:, 14:15], in_=st2[:, 13:14], func=AF.Sqrt,
                                                    bias=EPSB[:, 16:17]), sk2 + ["EPSB"], sk2, n=60, tbl="sqrt")
                dve(lambda e, st2=st2: e.reciprocal(out=st2[:, 14:15], in_=st2[:, 14:15]), sk2, sk2, n=80)
                dve(lambda e, st2=st2, vg=vg: e.tensor_scalar(out=vg, in0=vg, scalar1=st2[:, 12:13],
                                                              scalar2=st2[:, 14:15], op0=ALU.subtract, op1=ALU.mult),
                    vgk + sk2, vgk, n=256)
                dve(lambda e, vg=vg: e.tensor_tensor(out=vg, in0=vg, in1=MG[:, l, :], op=ALU.mult), vgk + ["MG"], vgk,
                    n=256)
                dve(lambda e, t=t, vg=vg: e.tensor_tensor(out=VN[:, t, :], in0=vg, in1=MBt[:, l, :], op=ALU.add),
                    vgk + ["MBt"], [f"VN{t}"], n=256)
                if is_s:
                    rope(kf[:, :], 2, t, KB[:, 0:1], [[64, 2], [2, 32]], kfk, ["KB"])

            def kv_b(t):
                kf = KF[t % 2]
                kfk = [f"KF{t % 2}"]
                if not is_s:
                    s_, tt = t // 2, t % 2
                    spdma(nk_d[s_, l, tt * 128:(tt + 1) * 128, :], kf[:, :], kfk, [], f"kst{t % 2}")
                    act(lambda e, kf=kf: e.copy(out=KB[:, :], in_=kf[:, :]), kfk, ["KB"], n=128)
                b2 = ps_get()
                pst = PSB[b2][:, :].bitcast(BF16)
                pe(lambda e, pst=pst: e.transpose(out=pst[:, 0:128], in_=KB[:, :], identity=IDB[:, :]),
                   ["KB", "IDB"], pk(b2), n=128)
                kc = ktoff + t * 128
                dve(lambda e, kc=kc, pst=pst: e.tensor_copy(out=KT[:, kc:kc + 128], in_=pst[:, 0:128]),
                    pk(b2), [f"KT{kc // 128}"], n=128)
                ps_put(b2)

            def skewed(a1, a2, bb):
                for step in range(8 + 2):
                    if step < 8:
                        a1(step)
                    if 0 <= step - 2 < 8:
                        bb(step - 2)
                    if 0 <= step - 1 < 8:
                        a2(step - 1)

            skewed(q_a1, q_a2, q_b)
            stop_at(f"{grp}{l}:q")
            skewed(kv_a1, kv_a2, kv_b)
            stop_at(f"{grp}{l}:kv")

            for (slot, ncol, j) in [(iC, 512, 0), (iC, 512, 1), (iC, 512, 2), (iC, 512, 3), (iD, 256, 0), (iD, 256, 1)]:
                for tg in range(2):
                    b = ps_get()
                    for k in range(8):
                        pe(lambda e, k=k, b=b, tg=tg, slot=slot, ncol=ncol, j=j: e.matmul(
                            PSB[b][:, :], lhsT=WB[slot][:, k * 512 + j * 128:k * 512 + (j + 1) * 128],
                            rhs=HT[:, k, tg * 512:(tg + 1) * 512], start=(k == 0), stop=(k == 7)),
                           htk([k], range(tg * 4, tg * 4 + 4)) + [f"WB{slot}"], pk(b))
                    cols = slice(tg * 512, (tg + 1) * 512)
                    if slot == iC and j < 2:
                        act(lambda e, b=b, j=j, cols=cols: e.copy(out=XR[:, j, cols], in_=PSB[b][:, :]),
                            pk(b), [f"XR{j}"])
                    elif slot == iC:
                        act(lambda e, b=b, j=j, cols=cols: e.activation(out=GG[:, j - 2, cols], in_=PSB[b][:, :],
                                                                        func=AF.Gelu_apprx_tanh),
                            pk(b), [f"GG{j - 2}"], tbl="gelu")
                    else:
                        act(lambda e, b=b, j=j, cols=cols: e.activation(out=UT[:, j, cols], in_=PSB[b][:, :],
                                                                        func=AF.Gelu_apprx_tanh),
                            pk(b), [f"UT{j}"], tbl="gelu")
                    ps_put(b)


            if grp == "P" and l == 0:
                S.phase = "P0:defer"
                deferred_params()
            stop_at(f"{grp}{l}:att")
            S.phase = f"{grp}{l}:att"
            npt = 0
            nunit = 0
            npair = [0]
            if is_s:
                for b_ in (0, 1, 2, 3):
                    ps_free.remove(b_)
            for u in range(nseq if not is_s else 2):
                if is_s:
                    q0, N = u * 512, 512
                    tts = list(range(10))
                else:
                    q0, N = u * 256, 256
                    tts = [2 * u, 2 * u + 1]
                qts = list(range(q0 // 128, (q0 + N) // 128))
                for j in range(4):
                    for a in range(2):
                        h = j + 4 * a
                        rows = slice(a * 64, a * 64 + 64)
                        bo = ps_get()
                        if is_s:
                            for pi in range(5):
                                kp = npair[0] % 2
                                npair[0] += 1
                                b0_, b1_ = 2 * kp, 2 * kp + 1
                                for hb_, tt in ((b0_, 2 * pi), (b1_, 2 * pi + 1)):
                                    kc = tt * 128
                                    pe(lambda e, hb_=hb_, kc=kc, rows=rows, j=j, q0=q0, N=N: e.matmul(
                                        PSB[hb_][:, 0:N], lhsT=KT[rows, kc:kc + 128], rhs=QT[rows, j, q0:q0 + N],
                                        start=True, stop=True),
                                       [f"KT{kc // 128}"] + qtk([j], qts), pk(hb_), n=N)
                                pt = PT2[npt % 2]
                                ptk = [f"PT{npt % 2}_{k_}" for k_ in range(4)]
                                npt += 1
                                act(lambda e, kp=kp, pt=pt: e.activation(out=pt[:, :], in_=PS2[kp][:, :], func=AF.Exp),
                                    pk(b0_) + pk(b1_), ptk, n=850, tbl="explog")
                                for hi_, tt in ((0, 2 * pi), (1, 2 * pi + 1)):
                                    pe(lambda e, bo=bo, tt=tt, a=a, pt=pt, hi_=hi_: e.matmul(
                                        PSB[bo][:, 0:512], lhsT=V2[:, tt, a * 128:(a + 1) * 128],
                                        rhs=pt[:, hi_ * 512:(hi_ + 1) * 512],
                                        start=(tt == 0), stop=(tt == 9)), [f"V2_{tt}"] + ptk, pk(bo), n=512)
                        else:
                            for ti, tt in enumerate(tts):
                                bs_ = ps_get()
                                kc = tt * 128
                                pe(lambda e, bs_=bs_, kc=kc, rows=rows, j=j, q0=q0, N=N: e.matmul(
                                    PSB[bs_][:, 0:N], lhsT=KT[rows, kc:kc + 128], rhs=QT[rows, j, q0:q0 + N],
                                    start=True, stop=True),
                                   [f"KT{kc // 128}"] + qtk([j], qts), pk(bs_), n=N)
                                pt = PT2[(npt // 4) % 2][:, (npt % 4) * 256:(npt % 4 + 1) * 256]
                                ptk = [f"PT{(npt // 4) % 2}_{npt % 4}"]
                                npt += 1
                                act(lambda e, bs_=bs_, pt=pt, N=N: e.activation(out=pt[:, 0:N], in_=PSB[bs_][:, 0:N],
                                                                                func=AF.Exp), pk(bs_), ptk, n=N,
                                    tbl="explog")
                                ps_put(bs_)
                                pe(lambda e, bo=bo, tt=tt, a=a, pt=pt, N=N, ti=ti, nt=len(tts): e.matmul(
                                    PSB[bo][:, 0:N], lhsT=V2[:, tt, a * 128:(a + 1) * 128], rhs=pt[:, 0:N],
                                    start=(ti == 0), stop=(ti == nt - 1)), [f"V2_{tt}"] + ptk, pk(bo), n=N)
                        pr = slice((h % 2) * 64, (h % 2) * 64 + 64)
                        rz = FS[h % 2]
                        rzk = [f"FS{h % 2}"]
                        if is_s or nunit % 3 == 2:
                            dve(lambda e, bo=bo, rz=rz, N=N: e.reciprocal(out=rz[0:64, 0:N], in_=PSB[bo][64:128, 0:N]),
                                pk(bo), rzk, n=int(5.0 * N))
                        else:
                            act(lambda e, bo=bo, rz=rz, N=N: e.activation(out=rz[0:64, 0:N], in_=PSB[bo][64:128, 0:N],
                                                                          func=AF.Ln), pk(bo), rzk, n=N, tbl="explog")
                            act(lambda e, rz=rz, N=N: e.activation(out=rz[0:64, 0:N], in_=rz[0:64, 0:N], func=AF.Exp,
                                                                   scale=-1.0), rzk, rzk, n=N, tbl="explog")
                        dve(lambda e, bo=bo, rz=rz, pr=pr, N=N, h=h, q0=q0: e.tensor_tensor(
                            out=MH[pr, h // 2, q0:q0 + N], in0=PSB[bo][0:64, 0:N], in1=rz[0:64, 0:N], op=ALU.mult),
                            pk(bo) + rzk, mhk([h // 2], qts), n=N)
                        ps_put(bo)
                        nunit += 1

            if is_s:
                ps_free.extend([0, 1, 2, 3])
            stop_at(f"{grp}{l}:lru")
            S.phase = f"{grp}{l}:lru"
            if grp == "P" and l == 0:
                deferred_cl()
            def lru_chunk(c):
                xr = XR[:, c, :]
                xc = xc_ap(c)
                xck = xc_key(c)
                xrk = [f"XR{c}"]
                w_ = lambda j: CW[:, l, j, c:c + 1]
                xr3 = xr.rearrange("p (s t) -> p s t", s=nseq)
                xc3 = xc.rearrange("p (s t) -> p s t", s=nseq)
                dve(lambda e: e.tensor_scalar(out=xc, in0=xr, scalar1=w_(1), scalar2=CB[:, l, c:c + 1],
                                              op0=ALU.mult, op1=ALU.add), xrk + ["CW", "CB"], xck)
                dve(lambda e: e.scalar_tensor_tensor(out=xc3[:, :, 1:L], in0=xr3[:, :, 0:L - 1], scalar=w_(0),
                                                     in1=xc3[:, :, 1:L], op0=ALU.mult, op1=ALU.add),
                    xrk + xck + ["CW"], xck)
                dve(lambda e: e.scalar_tensor_tensor(out=xc3[:, :, 0:L - 1], in0=xr3[:, :, 1:L], scalar=w_(2),
                                                     in1=xc3[:, :, 0:L - 1], op0=ALU.mult, op1=ALU.add),
                    xrk + xck + ["CW"], xck)
                dve(lambda e: e.scalar_tensor_tensor(out=xc3[:, :, 0:L - 2], in0=xr3[:, :, 2:L], scalar=w_(3),
                                                     in1=xc3[:, :, 0:L - 2], op0=ALU.mult, op1=ALU.add),
                    xrk + xck + ["CW"], xck)
                xcb = xcb_ap(c)
                xcbk = xcb_key(c)
                act(lambda e: e.copy(out=xcb, in_=xc), xck, xcbk)
                def lru_dir(d):
                    for gi, (dst, dkey) in enumerate(((LA_, LKEY[0]), (LI_, LKEY[1]))):
                        for tg in range(2):
                            b = ps_get()
                            pe(lambda e, b=b, gi=gi, tg=tg: e.matmul(
                                PSB[b][:, :], lhsT=BD[:, l, d, gi, c, :], rhs=xcb[:, tg * 512:(tg + 1) * 512],
                                start=True, stop=True), [f"BDx{l}"] + xcbk, pk(b))
                            act(lambda e, b=b, gi=gi, tg=tg, dst=dst: e.activation(
                                out=dst[:, tg * 512:(tg + 1) * 512], in_=PSB[b][:, :], func=AF.Sigmoid,
                                bias=LBA[:, l, d, gi, c:c + 1]), pk(b) + ["LBA"], dkey, tbl="sig")
                            ps_put(b)
                    act(lambda e: e.activation(out=LA_, in_=LA_, func=AF.Exp, scale=CL[:, l, d, c:c + 1]),
                        LKEY[0] + ["CL"], LKEY[0], n=1024, tbl="explog")
                    dve(lambda e: e.tensor_tensor(out=LT_, in0=LA_, in1=LA_, op=ALU.mult), LKEY[0], LKEY[2], n=1024)
                    act(lambda e: e.activation(out=LT_, in_=LT_, func=AF.Ln, scale=-1.0, bias=1.0), LKEY[2], LKEY[2],
                        n=1024, tbl="explog")
                    act(lambda e: e.activation(out=LT_, in_=LT_, func=AF.Exp, scale=0.5), LKEY[2], LKEY[2],
                        n=1024, tbl="explog")
                    dve(lambda e: e.tensor_tensor(out=LI_, in0=LI_, in1=LT_, op=ALU.mult), LKEY[1] + LKEY[2], LKEY[1],
                        n=1024)
                    dve(lambda e: e.tensor_tensor(out=LI_, in0=LI_, in1=xc, op=ALU.mult), LKEY[1] + xck, LKEY[1], n=1024)
                    hdst = xr if d == 0 else LH_
                    hkey = xrk if d == 0 else LKEY[3]
                    for s_ in range(nseq):
                        if d == 0:
                            sl = lambda tns: tns[:, s_ * L:(s_ + 1) * L]
                        else:
                            def sl(tns, s_=s_):
                                base = tns[:, (s_ + 1) * L - 1:(s_ + 1) * L]
                                return AP(base.tensor, base.offset, [list(base.ap[0]), [-1, L]])
                        init = STI[:, l, d, c:c + 1] if is_s else 0.0
                        o_, a_, u_ = sl(hdst), sl(LA_), sl(LI_)
                        dve(lambda e, o_=o_, a_=a_, u_=u_, init=init: e.tensor_tensor_scan(
                            out=o_, data0=a_, data1=u_, initial=init, op0=ALU.mult, op1=ALU.add),
                            LKEY[0] + LKEY[1] + ["STI"], hkey)
                    if not is_s:
                        fin = (L - 1) if d == 0 else 0
                        dve(lambda e, hdst=hdst, fin=fin, d=d: e.tensor_copy(
                            out=NSS[:, d, c, :], in_=ap_of(hdst[:, 0:1], fin, [[L, 4]])), hkey, ["NSS"])
                for d in range(2):
                    bg_step(2)
                    lru_dir(d)
                dve(lambda e: e.tensor_tensor(out=xr, in0=xr, in1=LH_, op=ALU.add), xrk + LKEY[3], xrk)
                dve(lambda e: e.tensor_tensor(out=MH[:, 4 + c, :], in0=xr, in1=GG[:, c, :], op=ALU.mult),
                    xrk + [f"GG{c}"], mhk([4 + c], ALL8))
            for c in range(2):
                lru_chunk(c)
            while bg and bg[0][0] <= l:
                bg_step()
            late_setup()
            iE = wb_load([(0, [[512, 8], [1, 512]], wsrc(w_out, l, 0, 0, 8, 512))])
            iF = wb_load([(0, [[512, 8], [1, 512]], wsrc(w_out, l, 0, 512, 8, 512))])
            if not is_s:
                for s_ in range(4):
                    for d in range(2):
                        spdma(ns_d[s_, l, d, :].rearrange("(c p) -> p c", p=128), NSS[:, d, :, s_], ["NSS"], [], "nst")

            stop_at(f"{grp}{l}:mlp")
            S.phase = f"{grp}{l}:mlp"
            def gmlp_mix(c2, gi):
                if True:
                    g = 2 * c2 + gi
                    b0, b1 = ps_get(), ps_get()
                    for t in range(8):
                        bb = b0 if t < 4 else b1
                        pe(lambda e, bb=bb, t=t, g=g: e.matmul(
                            PSB[bb][:, (t % 4) * 128:(t % 4 + 1) * 128], lhsT=VN[:, t, c2 * 128:(c2 + 1) * 128],
                            rhs=WST[:, l, g, :], start=True, stop=True), [f"VN{t}", "WST"], pk(bb))
                    pr = slice(gi * 64, gi * 64 + 64)
                    for hh, bb in enumerate((b0, b1)):
                        cols = slice(hh * 512, (hh + 1) * 512)
                        dve(lambda e, bb=bb, pr=pr, cols=cols: e.tensor_tensor(
                            out=TG[pr, cols].rearrange("p (t q) -> p t q", t=4),
                            in0=PSB[bb][pr, :].rearrange("p (t q) -> p t q", t=4),
                            in1=ap_of(BSB[pr, l, c2, 0:1], 0, [[0, 4], [1, 128]]), op=ALU.add),
                            pk(bb) + ["BSB"], ["TG"])
                        dve(lambda e, pr=pr, cols=cols: e.tensor_tensor(
                            out=MH[pr, 6 + c2, cols], in0=TG[pr, cols], in1=UT[pr, c2, cols], op=ALU.mult),
                            ["TG", f"UT{c2}"], mhk([6 + c2], range(hh * 4, hh * 4 + 4)))
                    ps_put(b0)
                    ps_put(b1)

            for c2 in range(2):
                for gi in range(2):
                    bg_step(1)
                    gmlp_mix(c2, gi)

            stop_at(f"{grp}{l}:wout")
            S.phase = f"{grp}{l}:wout"
            wst_ = {}

            def wo_a1(t):
                for hf, slot in enumerate((iE, iF)):
                    b = ps_get()
                    for k in range(8):
                        pe(lambda e, k=k, b=b, t=t, slot=slot: e.matmul(
                            PSB[b][:, :], lhsT=MH[:, k, t * 128:(t + 1) * 128], rhs=WB[slot][:, k * 512:(k + 1) * 512],
                            start=(k == 0), stop=(k == 7)), mhk([k], [t]) + [f"WB{slot}"], pk(b))
                    cols = slice(hf * 512, (hf + 1) * 512)
                    tmp = FS[(2 * t + hf) % 3]
                    tk = [f"FS{(2 * t + hf) % 3}"]
                    dve(lambda e, b=b, cols=cols, tmp=tmp: e.tensor_tensor(out=tmp[:, :], in0=PSB[b][:, :],
                                                                           in1=G1[:, cols], op=ALU.mult),
                        pk(b) + ["G1"], tk)
                    ps_put(b)
                    dve(lambda e, t=t, cols=cols, tmp=tmp: e.tensor_tensor(out=X[:, t, cols], in0=X[:, t, cols],
                                                                           in1=tmp[:, :], op=ALU.add),
                        tk + [f"X{t}"], [f"X{t}"])
                wst_[t] = nmt_a1(t)

            skewed(wo_a1, lambda t: nmt_a2(t, wst_[t]), lambda t: nmt_b(t, 1))

            if not last:
                load_msc(l + 1, cond, 0)
                nlp = (l + 1) % 2
                dve(lambda e: e.tensor_tensor(out=GCB[:, 0, 0, :], in0=LNF[:, l, 2, :], in1=MSC[:, nlp, 1, :],
                                              op=ALU.mult), ["LNF", f"MSC{nlp}0"], ["GCB0"], n=8)
                dve(lambda e: e.tensor_tensor(out=GCB[:, 0, 1, :], in0=LNF[:, l, 3, :], in1=MSC[:, nlp, 1, :],
                                              op=ALU.mult), ["LNF", f"MSC{nlp}0"], ["GCB0"], n=8)
                dve(lambda e: e.tensor_tensor(out=GCB[:, 0, 1, :], in0=GCB[:, 0, 1, :], in1=MSC[:, nlp, 0, :],
                                              op=ALU.add), ["GCB0", f"MSC{nlp}0"], ["GCB0"], n=8)
            bcast_row(LNA, ln_gb[(2, 0)], ln_gb[(2, 0)][l, 0:1], [], ["LNA"], "lna")
            bcast_row(LNB, ln_gb[(2, 1)], ln_gb[(2, 1)][l, 0:1], [], ["LNB"], "lnb")
            if not last:
                dve(lambda e: e.tensor_scalar(out=LNA[:, :], in0=LNA[:, :], scalar1=ALPHA, scalar2=None,
                                              op0=ALU.mult), ["LNA"], ["LNA"])
                dve(lambda e: e.tensor_scalar(out=LNB[:, :], in0=LNB[:, :], scalar1=ALPHA, scalar2=None,
                                              op0=ALU.mult), ["LNB"], ["LNB"])

            stop_at(f"{grp}{l}:ffn")
            S.phase = f"{grp}{l}:ffn"
            if grp == "P" and l == 0:
                deferred_bd1()
            nrl = 0
            for qd in range(4):
                i1 = [wb_load([(0, [[512, 8], [1, 512]], wsrc(w_ff1, l, 0, qd * 1024 + hb * 512, 8, 512))])
                      for hb in range(2)]
                i2 = [wb_load([(0, [[1024, 4], [1, 1024]], wsrc(w_ff2, l, qd * 1024 + hb * 512, 0, 4, 1024))])
                      for hb in range(2)]
                for tg in range(2):
                    for hc in range(8):
                        slot = i1[hc // 4]
                        cc = (hc % 4) * 128
                        b = ps_get()
                        for k in range(8):
                            pe(lambda e, k=k, b=b, tg=tg, slot=slot, cc=cc: e.matmul(
                                PSB[b][:, :], lhsT=WB[slot][:, k * 512 + cc:k * 512 + cc + 128],
                                rhs=HT[:, k, tg * 512:(tg + 1) * 512], start=(k == 0), stop=(k == 7)),
                               htk([k], range(tg * 4, tg * 4 + 4)) + [f"WB{slot}"], pk(b))
                        rl = FS[nrl % 3]
                        rk = [f"FS{nrl % 3}"]
                        nrl += 1
                        chunk = qd * 8 + hc
                        act(lambda e, b=b, rl=rl, chunk=chunk: e.activation(out=rl[:, :], in_=PSB[b][:, :], func=AF.Relu,
                                                                            bias=B1[:, l, chunk:chunk + 1]),
                            pk(b) + ["B1"], rk)
                        ps_put(b)
                        act(lambda e, rl=rl, hc=hc, tg=tg: e.activation(out=MH[:, hc, tg * 512:(tg + 1) * 512],
                                                                        in_=rl[:, :], func=AF.Square),
                            rk, mhk([hc], range(tg * 4, tg * 4 + 4)))
                for tg in range(2):
                    for t in range(tg * 4, tg * 4 + 4):
                        for hf in range(2):
                            b = ps_get()
                            for hc in range(8):
                                slot = i2[hc // 4]
                                pe(lambda e, hc=hc, b=b, t=t, slot=slot, hf=hf: e.matmul(
                                    PSB[b][:, :], lhsT=MH[:, hc, t * 128:(t + 1) * 128],
                                    rhs=WB[slot][:, (hc % 4) * 1024 + hf * 512:(hc % 4) * 1024 + (hf + 1) * 512],
                                    start=(hc == 0), stop=(hc == 7)), mhk([hc], [t]) + [f"WB{slot}"], pk(b))
                            cols = slice(hf * 512, (hf + 1) * 512)
                            tmp = FS[nrl % 3]
                            tk = [f"FS{nrl % 3}"]
                            nrl += 1
                            dve(lambda e, b=b, cols=cols, tmp=tmp: e.tensor_tensor(out=tmp[:, :], in0=PSB[b][:, :],
                                                                                   in1=G2[:, cols], op=ALU.mult),
                                pk(b) + ["G2"], tk)
                            ps_put(b)
                            dve(lambda e, t=t, cols=cols, tmp=tmp: e.tensor_tensor(out=X[:, t, cols], in0=X[:, t, cols],
                                                                                   in1=tmp[:, :], op=ALU.add),
                                tk + [f"X{t}"], [f"X{t}"])
                        if qd == 3:
                            S.phase = f"{grp}{l}:ln2"
                            if last:
                                nmt(t, l, 0, norm=True, la_lb=True, final_store=yout[grp][t * 128:(t + 1) * 128, :])
                            else:
                                nmt(t, l, 0, norm=True, la_lb=True)
                            S.phase = f"{grp}{l}:ffn"

        try:
            for grp in ("P", "S"):
                for l in range(2):
                    group_layer(grp, l, first=(l == 0), last=(l == 1))
        except _Stop:
            pass

        final_sems = ["yst", "nst", "kst0", "kst1", "vst0", "vst1"]
        import os as _os
        S.schedule()
        if _os.environ.get("KDEBUG"):
            print("ops", {e: len(v) for e, v in S.ops.items()}, "sim_us", S.sim_time / 1e3,
                  "sbuf_left", nc.sbuf_bytes_remaining)
        S.finalize()
        S.check()
        if _os.environ.get("KDUMP"):
            for i, op in enumerate(S.ops[_os.environ["KDUMP"]]):
                print(i, "lidx", op.lidx, "tbl", op.tbl, "cost", int(op.cost), "sig", op.sigidx if op.sig else None,
                      "deps", [(d.eng, d.sigidx, d.lidx) for d in op.deps], "dw", op.dwaits)

        dma_names = sorted(S.dma_counts.keys())
        sems = {}
        for n in list(dma_names) + ["e_pe", "e_act", "e_dve", "e_pool", "e_sp"]:
            sems[n] = es.enter_context(nc.semaphore(n))
        esem = {e: sems["e_" + e] for e in Sched.ENGS}
        dsem = {n: sems[n] for n in dma_names}

        with nc.Block() as block:
            @block.tensor
            def _(e):
                S.emit("pe", e, esem, dsem)

            @block.scalar
            def _(e):
                S.emit("act", e, esem, dsem)

            @block.vector
            def _(e):
                S.emit("dve", e, esem, dsem)

            @block.gpsimd
            def _(e):
                S.emit("pool", e, esem, dsem)

            @block.sync
            def _(e):
                S.emit("sp", e, esem, dsem)
                for n in final_sems:
                    if n in dsem:
                        e.wait_ge(dsem[n], S.dma_counts[n] * 16)
    return nc


_NC_CACHE = {}


def _rope_table():
    pos = np.arange(1024)
    pr = (pos // 64).astype(np.float32)
    pc = (pos % 64).astype(np.float32)
    inv = (10000.0 ** (-np.arange(16, dtype=np.float32) / 16)).astype(np.float32)
    ang = np.concatenate([pr[:, None] * inv, pc[:, None] * inv], -1).astype(np.float32)
    return np.concatenate([np.cos(ang), np.sin(ang)], -1).astype(np.float32)


def kernel(x_prompt, x_sample, c, cache_k, cache_v, state_lru, c_ctx, w_ada, b_ada, w_in,
           q_norm_g, k_norm_g, conv_w, conv_b, lru_wa, lru_ba, lru_wx, lru_bx, lru_lam,
           mlp_norm_g, mlp_norm_b, mlp_ws, mlp_bs, w_out, ln1_g, ln1_b, w_ff1, b_ff1,
           w_ff2, b_ff2, ln2_g, ln2_b):
    f = lambda a: np.ascontiguousarray(np.asarray(a, dtype=np.float32))
    if "nc" not in _NC_CACHE:
        _NC_CACHE["nc"] = build_nc()
    nc = _NC_CACHE["nc"]
    shared = dict(w_ada=f(w_ada), b_ada=f(b_ada), w_in=f(w_in), q_norm_g=f(q_norm_g), k_norm_g=f(k_norm_g),
                  conv_w=f(conv_w), conv_b=f(conv_b), lru_wa=f(lru_wa), lru_ba=f(lru_ba), lru_wx=f(lru_wx),
                  lru_bx=f(lru_bx), lru_lam=f(lru_lam), mlp_norm_g=f(mlp_norm_g), mlp_norm_b=f(mlp_norm_b),
                  mlp_ws=f(mlp_ws), mlp_bs=f(mlp_bs), w_out=f(w_out), ln1_g=f(ln1_g), ln1_b=f(ln1_b),
                  w_ff1=f(w_ff1), b_ff1=f(b_ff1), w_ff2=f(w_ff2), b_ff2=f(b_ff2), ln2_g=f(ln2_g), ln2_b=f(ln2_b),
                  ident=np.eye(128, dtype=np.float32), rope=_rope_table())
    x_prompt, x_sample, c, c_ctx = f(x_prompt), f(x_sample), f(c), f(c_ctx)
    cache_k, cache_v, state_lru = f(cache_k), f(cache_v), f(state_lru)
    in_maps = []
    for i in range(8):
        m = dict(shared)
        m["xp"] = np.ascontiguousarray(x_prompt[4 * i:4 * i + 4].reshape(1024, D))
        m["xs"] = np.ascontiguousarray(x_sample[i])
        m["cvec"] = np.ascontiguousarray(np.stack([c_ctx, c[i]], 0))
        m["ck"] = np.ascontiguousarray(cache_k[i].reshape(2, 256, 128))
        m["cv"] = np.ascontiguousarray(cache_v[i].reshape(2, 256, 128))
        m["st"] = np.ascontiguousarray(state_lru[i])
        in_maps.append(m)
    res = run_bass_kernel_spmd(nc, in_maps, core_ids=list(range(8)))
    R = res.results
    y_prompt = np.concatenate([r["yp"].reshape(4, 256, D) for r in R], 0)
    y_sample = np.stack([r["ys"] for r in R], 0)
    nk = np.concatenate([r["nk"].reshape(4, 2, 256, 2, 64) for r in R], 0)
    nv = np.concatenate([r["nv"].reshape(4, 2, 256, 2, 64) for r in R], 0)
    ns = np.concatenate([r["ns"] for r in R], 0)
    return (y_prompt.astype(np.float32), y_sample.astype(np.float32), nk.astype(np.float32),
            nv.astype(np.float32), ns.astype(np.float32))
```
